# Optimizing a Trainium2 kernel written in Bass

```python
import jax, jax.numpy as jnp
from jax import lax
import numpy as np

D_MODEL = 1024
BATCH = 4
SEQ = 4096
DEPTH = 4

N_MIXERS = 2
N_MLA_LAYERS = (DEPTH + 1) // 2
N_SGU_LAYERS = DEPTH // 2
MLA_HEADS = 8
QK_NOPE_DIM = 128
QK_ROPE_DIM = 64
QK_HEAD_DIM = QK_NOPE_DIM + QK_ROPE_DIM
V_HEAD_DIM = 128
Q_LORA_RANK = 256
KV_LORA_RANK = 128
ROPE_THETA = 10000.0
Q_BLOCK = 128
SGU_CHUNK = 128
SGU_WIDTH = 2 * D_MODEL
SGU_GROUPS = 8
SGU_GROUP_DIM = SGU_WIDTH // SGU_GROUPS
FFN_HIDDEN = 4 * D_MODEL
NORM_EPS = 1e-6
LN_EPS = 1e-5

kernel_name = "hybrid_mla_chunked_sgu_trunk"


def rms_norm(x, g):
    xf = x.astype(jnp.float32)
    y = xf * lax.rsqrt(jnp.mean(xf * xf, axis=-1, keepdims=True) + NORM_EPS)
    return (y * g.astype(jnp.float32)).astype(x.dtype)


def layer_norm(x, g, b):
    xf = x.astype(jnp.float32)
    mu = jnp.mean(xf, axis=-1, keepdims=True)
    var = jnp.mean(jnp.square(xf - mu), axis=-1, keepdims=True)
    y = (xf - mu) * lax.rsqrt(var + LN_EPS)
    return (y * g.astype(jnp.float32) + b.astype(jnp.float32)).astype(x.dtype)


def apply_rope(x, cos, sin):
    x1, x2 = jnp.split(x.astype(jnp.float32), 2, axis=-1)
    out = jnp.concatenate([x1 * cos - x2 * sin, x2 * cos + x1 * sin], axis=-1)
    return out.astype(x.dtype)


def mla_mixer(h, positions, w_dkv, q_norm, kv_norm, w_uq, w_ukv, w_o):
    B, S, _ = h.shape
    lat = h @ w_dkv
    c_q, c_kv, k_rope = jnp.split(lat, [Q_LORA_RANK, Q_LORA_RANK + KV_LORA_RANK], axis=-1)
    c_q = rms_norm(c_q, q_norm)
    c_kv = rms_norm(c_kv, kv_norm)
    q = (c_q @ w_uq).reshape(B, S, MLA_HEADS, QK_HEAD_DIM)
    q_nope, q_rope = jnp.split(q, [QK_NOPE_DIM], axis=-1)
    kv = (c_kv @ w_ukv).reshape(B, S, MLA_HEADS, QK_NOPE_DIM + V_HEAD_DIM)
    k_nope, v = jnp.split(kv, [QK_NOPE_DIM], axis=-1)

    inv_freq = ROPE_THETA ** (-jnp.arange(0, QK_ROPE_DIM, 2, dtype=jnp.float32) / QK_ROPE_DIM)
    ang = positions.astype(jnp.float32)[..., None] * inv_freq
    cos, sin = jnp.cos(ang), jnp.sin(ang)
    q_rope = apply_rope(q_rope, cos[:, :, None, :], sin[:, :, None, :])
    k_rope = apply_rope(k_rope, cos, sin)

    nb = S // Q_BLOCK
    qn_b = q_nope.reshape(B, nb, Q_BLOCK, MLA_HEADS, QK_NOPE_DIM).transpose(1, 0, 2, 3, 4)
    qr_b = q_rope.reshape(B, nb, Q_BLOCK, MLA_HEADS, QK_ROPE_DIM).transpose(1, 0, 2, 3, 4)
    key_idx = jnp.arange(S)
    scale = QK_HEAD_DIM ** -0.5

    def attend(args):
        qn, qr, blk = args
        s = (jnp.einsum('bqhd,bkhd->bhqk', qn, k_nope)
             + jnp.einsum('bqhr,bkr->bhqk', qr, k_rope))
        s = s.astype(jnp.float32) * scale
        q_idx = blk * Q_BLOCK + jnp.arange(Q_BLOCK)
        causal = key_idx[None, :] <= q_idx[:, None]
        s = jnp.where(causal[None, None], s, -jnp.inf)
        p = jax.nn.softmax(s, axis=-1).astype(v.dtype)
        return jnp.einsum('bhqk,bkhd->bqhd', p, v)

    o = lax.map(attend, (qn_b, qr_b, jnp.arange(nb)))
    o = o.transpose(1, 0, 2, 3, 4).reshape(B, S, MLA_HEADS * V_HEAD_DIM)
    return o @ w_o


def chunked_sgu_mixer(h, w_in, ln_g, ln_b, w_spatial, b_spatial, w_out):
    B, S, _ = h.shape
    z = jax.nn.gelu(h @ w_in, approximate=False)
    u, v = jnp.split(z, 2, axis=-1)
    v = layer_norm(v, ln_g, ln_b)
    nc = S // SGU_CHUNK
    vg = v.reshape(B, nc, SGU_CHUNK, SGU_GROUPS, SGU_GROUP_DIM)
    w_causal = jnp.tril(w_spatial)
    mixed = (jnp.einsum('gts,bcsgd->bctgd', w_causal, vg)
             + b_spatial.T[None, None, :, :, None])
    v = mixed.reshape(B, S, SGU_WIDTH)
    return (u * v) @ w_out


def sq_relu_mlp(h, w_up, w_down):
    return jnp.square(jax.nn.relu(h @ w_up)) @ w_down


def setup_inputs(seed: int = 0) -> dict:
    key = jax.random.key(seed)
    ks = jax.random.split(key, 24)
    f32 = jnp.float32

    def nrm(k, shape, fan_in, gain=1.0):
        return jax.random.normal(k, shape, f32) * (gain * fan_in ** -0.5)

    def gain(k, shape):
        return 1.0 + 0.02 * jax.random.normal(k, shape, f32)

    x = jax.random.normal(ks[0], (BATCH, SEQ, D_MODEL), f32)
    offset = jax.random.randint(ks[1], (BATCH, 1), 0, 1024, dtype=jnp.int32)
    positions = offset + jnp.arange(SEQ, dtype=jnp.int32)[None, :]

    return {
        "x": x,
        "positions": positions,
        "norm_mix": gain(ks[2], (DEPTH, D_MODEL)),
        "norm_ffn": gain(ks[3], (DEPTH, D_MODEL)),
        "final_norm": gain(ks[4], (D_MODEL,)),
        "mla_w_dkv": nrm(ks[5], (N_MLA_LAYERS, D_MODEL, Q_LORA_RANK + KV_LORA_RANK + QK_ROPE_DIM), D_MODEL),
        "mla_q_norm": gain(ks[6], (N_MLA_LAYERS, Q_LORA_RANK)),
        "mla_kv_norm": gain(ks[7], (N_MLA_LAYERS, KV_LORA_RANK)),
        "mla_w_uq": nrm(ks[8], (N_MLA_LAYERS, Q_LORA_RANK, MLA_HEADS * QK_HEAD_DIM), Q_LORA_RANK),
        "mla_w_ukv": nrm(ks[9], (N_MLA_LAYERS, KV_LORA_RANK, MLA_HEADS * (QK_NOPE_DIM + V_HEAD_DIM)), KV_LORA_RANK),
        "mla_w_o": nrm(ks[10], (N_MLA_LAYERS, MLA_HEADS * V_HEAD_DIM, D_MODEL), MLA_HEADS * V_HEAD_DIM),
        "sgu_w_in": nrm(ks[11], (N_SGU_LAYERS, D_MODEL, 2 * SGU_WIDTH), D_MODEL),
        "sgu_ln_g": gain(ks[12], (N_SGU_LAYERS, SGU_WIDTH)),
        "sgu_ln_b": 0.02 * jax.random.normal(ks[13], (N_SGU_LAYERS, SGU_WIDTH), f32),
        "sgu_w_spatial": nrm(ks[14], (N_SGU_LAYERS, SGU_GROUPS, SGU_CHUNK, SGU_CHUNK), SGU_CHUNK, 0.5),
        "sgu_b_spatial": gain(ks[15], (N_SGU_LAYERS, SGU_GROUPS, SGU_CHUNK)),
        "sgu_w_out": nrm(ks[16], (N_SGU_LAYERS, SGU_WIDTH, D_MODEL), SGU_WIDTH),
        "ffn_w_up": nrm(ks[17], (DEPTH, D_MODEL, FFN_HIDDEN), D_MODEL),
        "ffn_w_down": nrm(ks[18], (DEPTH, FFN_HIDDEN, D_MODEL), FFN_HIDDEN),
    }


def reference(x, positions, norm_mix, norm_ffn, final_norm,
              mla_w_dkv, mla_q_norm, mla_kv_norm, mla_w_uq, mla_w_ukv, mla_w_o,
              sgu_w_in, sgu_ln_g, sgu_ln_b, sgu_w_spatial, sgu_b_spatial, sgu_w_out,
              ffn_w_up, ffn_w_down):
    for i in range(DEPTH):
        h = rms_norm(x, norm_mix[i])
        j = i // N_MIXERS
        if i % N_MIXERS == 0:
            x = x + mla_mixer(h, positions, mla_w_dkv[j], mla_q_norm[j], mla_kv_norm[j],
                              mla_w_uq[j], mla_w_ukv[j], mla_w_o[j])
        else:
            x = x + chunked_sgu_mixer(h, sgu_w_in[j], sgu_ln_g[j], sgu_ln_b[j],
                                      sgu_w_spatial[j], sgu_b_spatial[j], sgu_w_out[j])
        h = rms_norm(x, norm_ffn[i])
        x = x + sq_relu_mlp(h, ffn_w_up[i], ffn_w_down[i])
    return rms_norm(x, final_norm)
```

```python
import contextlib
import numpy as np
import ml_dtypes
import concourse.bass as bass
import concourse.mybir as mybir
from concourse.bass_utils import run_bass_kernel_spmd

F32 = mybir.dt.float32
BF16 = mybir.dt.bfloat16
I32 = mybir.dt.int32
AF = mybir.ActivationFunctionType
ALU = mybir.AluOpType
AX = mybir.AxisListType

SB_BASE = 16512
SB_END = 229376

D_MODEL = 1024
SEQ = 4096
NLOC = 2048
NTT = 4
NORM_EPS = 1e-6
LN_EPS = 1e-5
SCALE = 192.0 ** -0.5
PAIRS = [[0, 1], [2, 3], [4, 5], [6, 7]]
SUM_MODE = "dve"


class Prog:
    ENGS = ("pe", "act", "dve", "pool", "sp")

    def __init__(self, nc):
        self.nc = nc
        self.ops = {e: [] for e in self.ENGS}
        self.cnt = {e: 0 for e in self.ENGS}
        self.waited = {e: {} for e in self.ENGS}
        self.last_w = {}
        self.readers = {}
        self.dcnt = {}
        self.sems = {}
        self.sb_off = SB_BASE

    def sb(self, name, shape, dt, off=None):
        size = int(np.prod(shape[1:])) * (4 if dt in (F32, I32) else 2)
        size = (size + 31) // 32 * 32
        if off is None:
            off = self.sb_off
            self.sb_off += size
            assert self.sb_off <= SB_END, (name, self.sb_off)
        else:
            assert off + size <= SB_END, (name, off, size)
        return self.nc.alloc_sbuf_tensor_at(name, list(shape), dt, offset=off)

    def _deps(self, eng, reads, writes):
        deps = []
        for r in reads:
            if r in self.last_w:
                deps.append(self.last_w[r])
        for w in writes:
            if w in self.last_w:
                deps.append(self.last_w[w])
            deps.extend(self.readers.get(w, ()))
        best = {}
        for (sk, val, deng) in deps:
            if deng == "pe" and eng == "pe":
                continue
            if self.waited[eng].get(sk, 0) >= val:
                continue
            best[sk] = max(best.get(sk, 0), val)
        for sk, val in best.items():
            self.waited[eng][sk] = val
        return list(best.items())

    def _commit(self, token, reads, writes):
        for r in reads:
            self.readers.setdefault(r, []).append(token)
        for w in writes:
            self.last_w[w] = token
            self.readers[w] = []

    def op(self, eng, fn, reads=(), writes=()):
        reads = list(reads)
        writes = list(writes)
        waits = self._deps(eng, reads, writes)
        self.cnt[eng] += 1
        self.ops[eng].append((waits, fn, (eng, 1)))
        self._commit((eng, self.cnt[eng], eng), reads, writes)

    def dma(self, q, semname, out, in_, reads=(), writes=()):
        def fn(e, out=out, in_=in_):
            return e.dma_start(out=out, in_=in_)
        self.custom(q, semname, 16, fn, reads, writes)

    def custom(self, q, semname, inc, fn, reads=(), writes=()):
        reads = list(reads)
        writes = list(writes)
        waits = self._deps(q, reads, writes)
        sk = "d_" + semname
        self.dcnt[sk] = self.dcnt.get(sk, 0) + inc
        self.ops[q].append((waits, fn, (sk, inc)))
        self._commit((sk, self.dcnt[sk], "dma"), reads, writes)

    def fence(self):
        for e in self.ENGS:
            waits = []
            for f in ("pe", "act", "dve", "pool"):
                if f == e or self.cnt[f] == 0:
                    continue
                if self.waited[e].get(f, 0) >= self.cnt[f]:
                    continue
                self.waited[e][f] = self.cnt[f]
                waits.append((f, self.cnt[f]))
            for sk, val in self.dcnt.items():
                if self.waited[e].get(sk, 0) >= val:
                    continue
                self.waited[e][sk] = val
                waits.append((sk, val))
            if waits:
                self.ops[e].append((waits, None, None))

    def final_wait(self, q, keys):
        waits = self._deps(q, keys, [])
        self.ops[q].append((waits, None, None))

    def emit(self):
        nc = self.nc
        names = set()
        for e in self.ENGS:
            for waits, fn, inc in self.ops[e]:
                for sk, _ in waits:
                    names.add(sk)
                if inc is not None:
                    names.add(inc[0])
        with contextlib.ExitStack() as st:
            for n in sorted(names):
                self.sems[n] = st.enter_context(nc.semaphore("s_" + n))
            block = st.enter_context(nc.Block())

            def run(e, h):
                for waits, fn, inc in self.ops[e]:
                    for sk, val in waits:
                        h.wait_ge(self.sems[sk], val)
                    if fn is None:
                        continue
                    ins = fn(h)
                    if inc is not None:
                        ins.then_inc(self.sems[inc[0]], inc[1])

            @block.tensor
            def _(h):
                run("pe", h)

            @block.scalar
            def _(h):
                run("act", h)

            @block.vector
            def _(h):
                run("dve", h)

            @block.gpsimd
            def _(h):
                run("pool", h)

            @block.sync
            def _(h):
                run("sp", h)


def build_program(nlayers=4):
    nc = bass.Bass("TRN2", target_bir_lowering=False)

    def din(name, shape, dt):
        return nc.dram_tensor(name, list(shape), dt, kind="ExternalInput").ap()

    xs = din("xs", [NLOC, D_MODEL], F32)
    pos = din("pos", [64, NLOC], I32)
    masks_d = din("masks", [128, 2, 256], BF16)
    maskT_d = din("maskT", [128, 128], BF16)
    gains_d = din("gains", [128, 112], F32)
    rc_d = din("ropec", [64, 2], F32)
    ident_d = din("ident", [128, 128], F32)
    w_dkv = din("w_dkv", [2, 1024, 512], F32)
    w_uq = din("w_uq", [2, 256, 2048], F32)
    w_ukv = din("w_ukv", [2, 128, 2048], F32)
    w_o = din("w_o", [2, 1024, 1024], F32)
    s_win = din("s_win", [2, 1024, 4096], F32)
    s_wout = din("s_wout", [2, 2048, 1024], F32)
    s_l2 = din("s_l2", [2, 2, 2048], F32)
    s_b = din("s_b", [2, 8, 128], F32)
    s_wsT = din("s_wsT", [2, 128, 1024], F32)
    f_up = din("f_up", [4, 1024, 4096], F32)
    f_dn = din("f_dn", [4, 4096, 1024], F32)
    y = nc.dram_tensor("y", [NLOC, D_MODEL], F32, kind="ExternalOutput").ap()
    cc_in = [nc.dram_tensor("cc_in%d" % j, [192, NLOC], BF16) for j in range(2)]
    cc_out = [nc.dram_tensor("cc_out%d" % j, [384, NLOC], BF16) for j in range(2)]

    p = Prog(nc)
    xT = p.sb("xT", [128, 8, NLOC], F32)
    cos2 = p.sb("cos2", [64, NLOC], BF16)
    sin2s = p.sb("sin2s", [64, NLOC], BF16)
    ident = p.sb("ident", [128, 128], F32)
    ones = p.sb("ones", [128, 128], BF16)
    masks = p.sb("masksb", [128, 2, 256], BF16)
    maskT = p.sb("maskTb", [128, 128], BF16)
    gains = p.sb("gainsb", [128, 112], F32)
    ones_f = p.sb("ones_f", [128, 128], F32)
    ropec = p.sb("ropecb", [64, 2], F32)
    epsn = p.sb("epsn", [128, 1], F32)
    epsl = p.sb("epsl", [128, 1], F32)
    AR = p.sb_off
    hT = p.sb("hT", [128, 8, NLOC], BF16, off=AR)
    sq = p.sb("sq", [128, 8, 512], BF16, off=AR + 32768)
    rs = p.sb("rs", [128, 512], F32, off=AR + 40960)
    WB = AR + 43008
    PB = WB + 32768
    assert SB_END - PB >= 59000, SB_END - PB
    ps = [nc.alloc_psum_tensor("ps%d" % b, [128, 512], F32) for b in range(8)]

    state = {"mm": 0, "st": 0}

    def mmbank(pool=(0, 1, 2, 3, 4, 5)):
        b = pool[state["mm"] % len(pool)]
        state["mm"] += 1
        return b

    def stbank():
        b = (6, 7)[state["st"] % 2]
        state["st"] += 1
        return b

    def tok(tt):
        return slice(tt * 512, (tt + 1) * 512)

    def MM(mms, reads, writes):
        def fn(e, mms=mms):
            ins = None
            for (o, l, r, s0, s1) in mms:
                ins = e.matmul(o, l, r, start=s0, stop=s1, skip_group_check=True)
            return ins
        p.op("pe", fn, reads, writes)

    def ACT(out, in_, func, reads, writes, **kw):
        p.op("act", lambda e: e.activation(out, in_, func, **kw), reads, writes)

    def TT(out, a, b, op, reads, writes):
        p.op("dve", lambda e: e.tensor_tensor(out, a, b, op), reads, writes)

    def TS(out, a, s1, s2, op0, op1, reads, writes):
        if op1 is None:
            p.op("dve", lambda e: e.tensor_scalar(out, a, s1, None, op0), reads, writes)
        else:
            p.op("dve", lambda e: e.tensor_scalar(out, a, s1, s2, op0, op1), reads, writes)

    def STT(out, a, s, b, op0, op1, reads, writes):
        p.op("dve", lambda e: e.scalar_tensor_tensor(out=out, in0=a, scalar=s, in1=b, op0=op0, op1=op1), reads, writes)

    def DCOPY(out, in_, reads, writes):
        p.op("dve", lambda e: e.tensor_copy(out, in_), reads, writes)

    def RECIP(out, in_, reads, writes):
        p.op("dve", lambda e: e.reciprocal(out, in_), reads, writes)

    evs = {"n": 0}

    def EVAC(out, in_, reads, writes, scale=None):
        evs["n"] += 1
        if evs["n"] % 2 == 0 and scale is None:
            DCOPY(out, in_, reads, writes)
        else:
            if scale is None:
                ACT(out, in_, AF.Copy, reads, writes)
            else:
                ACT(out, in_, AF.Copy, reads, writes, scale=scale)

    def xk(c, tt):
        return ("x", c, tt)

    def xkeys(tt):
        return [xk(c, tt) for c in range(8)]

    def rmsnorm(src3, nch, D, gcol, dsts, eps_t, reads, writes, np_=128):
        P = slice(0, np_)
        reads = list(reads) + ["cos2", "sin2s"]
        ACT(sq[P, 0:nch, :], src3, AF.Square, reads, ["sq", ("tmpx", 0), ("tmpx", 1)])
        sb_ = stbank()
        MM([(ps[sb_][P, :], ones[P, 0:np_], sq[P, c, :], c == 0, c == nch - 1) for c in range(nch)],
           ["sq", "ones"], [("ps", sb_)])
        ACT(rs[P, :], ps[sb_][P, :], AF.Ln, ["eps"], [("ps", sb_), "rs"], bias=eps_t[P, 0:1], scale=1.0 / D)
        ACT(rs[P, :], rs[P, :], AF.Exp, [], ["rs"], scale=-0.5)
        for c in range(nch):
            STT(dsts[c], src3[:, c, :], gains[P, gcol + c:gcol + c + 1], rs[P, :], ALU.mult, ALU.mult,
                reads + ["rs", "c"], [writes[c]] if isinstance(writes, list) and len(writes) == nch else writes)

    for (dst, src) in ((ident[:, :], ident_d), (masks[:, :, :], masks_d), (maskT[:, :], maskT_d),
                       (gains[:, :], gains_d), (ropec[:, :], rc_d)):
        p.dma("sp", "c", dst, src, writes=["c"])
    p.op("dve", lambda e: e.memset(ones[:, :], 1.0), writes=["ones"])
    p.op("dve", lambda e: e.memset(ones_f[:, :], 1.0), writes=["ones"])
    p.op("dve", lambda e: e.memset(epsn[:, :], NORM_EPS), writes=["eps"])
    p.op("dve", lambda e: e.memset(epsl[:, :], LN_EPS), writes=["eps"])

    pos_i = p.sb("pos_i", [64, NLOC], I32, off=AR)
    pos_f = p.sb("pos_f", [64, NLOC], F32, off=AR + 8192)
    ang = p.sb("ang", [64, NLOC], F32, off=AR + 16384)
    ang2 = p.sb("ang2", [64, NLOC], F32, off=AR + 24576)
    TWO_PI = 2.0 * float(np.pi)
    p.dma("sp", "pos", pos_i[:, :], pos, writes=["pos_i"])
    DCOPY(pos_f[:, :], pos_i[:, :], ["pos_i"], ["pos_f"])
    TS(ang[:, :], pos_f[:, :], ropec[:, 0:1], None, ALU.mult, None, ["pos_f", "c"], ["ang"])

    def reduce_and_sin(dst, shift, key):
        TS(ang2[:, :], ang[:, :], shift, None, ALU.add, None, ["ang"], ["ang2"])
        TS(pos_f[:, :], ang2[:, :], 1.0 / TWO_PI, None, ALU.mult, None, ["ang2"], ["pos_f"])
        DCOPY(pos_i[:, :], pos_f[:, :], ["pos_f"], ["pos_i"])
        DCOPY(pos_f[:, :], pos_i[:, :], ["pos_i"], ["pos_f"])
        STT(ang2[:, :], pos_f[:, :], -TWO_PI, ang2[:, :], ALU.mult, ALU.add, ["pos_f", "ang2"], ["ang2"])
        TS(pos_f[:, :], ang2[:, :], float(np.pi), TWO_PI, ALU.is_gt, ALU.mult, ["ang2"], ["pos_f"])
        TT(ang2[:, :], ang2[:, :], pos_f[:, :], ALU.subtract, ["pos_f", "ang2"], ["ang2"])
        ACT(dst, ang2[:, :], AF.Sin, ["ang2"], [key])

    reduce_and_sin(cos2[:, :], 0.5 * float(np.pi), "cos2")
    sin_f = p.sb("sin_f", [64, NLOC], F32, off=AR + 32768 - 8192 + 8192)
    reduce_and_sin(sin_f[:, :], 0.0, "sin_f")
    TS(sin2s[:, :], sin_f[:, :], ropec[:, 1:2], None, ALU.mult, None, ["sin_f", "c"], ["sin2s"])

    xin = [p.sb("xin%d" % b, [128, 1024], F32, off=WB + 16384 + 4096 * b) for b in range(4)]

    def load_x():
        for tile in range(16):
            b = tile % 4
            tt = tile // 4
            p.dma("sp", "xin%d" % b, xin[b][:, :], xs[tile * 128:(tile + 1) * 128, :], writes=[("xin", b)])
            for half in range(2):
                bk = mmbank()
                p.op("pe", (lambda e, bk=bk, b=b, half=half: [e.transpose(ps[bk][:, k * 128:(k + 1) * 128],
                                                                          xin[b][:, (half * 4 + k) * 128:(half * 4 + k + 1) * 128],
                                                                          ident[:, :]) for k in range(4)][-1]),
                     [("xin", b), "c"], [("ps", bk)])
                EVAC(xT[:, half * 4:half * 4 + 4, tile * 128:(tile + 1) * 128],
                     ps[bk][:, :].rearrange("p (k t) -> p k t", k=4),
                     [], [("ps", bk)] + [xk(half * 4 + k, tt) for k in range(4)])

    phases = []
    for l_ in range(nlayers):
        phases.append(("mla" if l_ % 2 == 0 else "sgu", l_))
        phases.append(("ffn", l_))
    normed = set()
    okeys = []
    yT = p.sb("yT", [128, 8, 512], F32, off=PB + 16384)
    yo = [p.sb("yo%d" % b, [128, 1024], F32, off=WB + 4096 * b) for b in range(4)]

    def final_part(tt, extra_w):
        rmsnorm(xT[:, :, tok(tt)], 8, 1024, 64, [yT[:, c, :] for c in range(8)], epsn, xkeys(tt), ["yT"])
        for tl in range(4):
            tile = tt * 4 + tl
            b = tile % 4
            for half in range(2):
                bk = mmbank()
                p.op("pe", (lambda e, bk=bk, half=half, tl=tl: [e.transpose(ps[bk][:, k * 128:(k + 1) * 128],
                                                                            yT[:, half * 4 + k, tl * 128:(tl + 1) * 128],
                                                                            ident[:, :]) for k in range(4)][-1]),
                     ["yT", "c"], [("ps", bk)])
                EVAC(yo[b][:, half * 512:(half + 1) * 512], ps[bk][:, :], [], [("ps", bk), ("yo", b, half)] + list(extra_w))
            p.dma("sp", "yo%d" % b, y[tile * 128:(tile + 1) * 128, :], yo[b][:, :],
                  reads=[("yo", b, 0), ("yo", b, 1)], writes=[("yout", tile)])
            okeys.append(("yout", tile))

    def phase_norm(i, tt, extra_w=()):
        if (i, tt) in normed:
            return
        normed.add((i, tt))
        if i >= len(phases):
            final_part(tt, extra_w)
            return
        kind, l = phases[i]
        gcol = 32 + l * 8 if kind == "ffn" else l * 8
        rmsnorm(xT[:, :, tok(tt)], 8, 1024, gcol, [hT[:, c, tok(tt)] for c in range(8)], epsn,
                xkeys(tt), [("h", tt)] + list(extra_w))

    def wkey(r):
        return ("Wreg", r)

    def ffn_tensors(l):
        wu = [p.sb("wu%d_%d" % (l, s), [128, 8, 512], BF16, off=WB + 16384 * s) for s in range(2)]
        wd = [p.sb("wd%d_%d" % (l, s), [128, 4, 1024], BF16, off=WB + 8192 + 16384 * s) for s in range(2)]
        return wu, wd

    def ffn_load(l, hc, wu, wd, extra=()):
        s = hc % 2
        p.dma("pool", "wr%d" % (2 * s), wu[s][:, :, :],
              f_up[l, :, hc * 512:(hc + 1) * 512].rearrange("(c p) n -> p c n", p=128),
              writes=[wkey(2 * s)] + [k for k in extra if k[1] == 2 * s])
        p.dma("pool", "wr%d" % (2 * s + 1), wd[s][:, :, :],
              f_dn[l, hc * 512:(hc + 1) * 512, :].rearrange("(s p) n -> p s n", p=128),
              writes=[wkey(2 * s + 1)] + [k for k in extra if k[1] == 2 * s + 1])

    def ffn_prefetch(l):
        wu, wd = ffn_tensors(l)
        ffn_load(l, 0, wu, wd)

    def ffn_layer(l, next_prefetch=None, pi=0):
        hid = [p.sb("hid%d_%d" % (l, b), [128, 4, 512], BF16, off=PB + 4096 * b) for b in range(2)]
        rr = [p.sb("rr%d_%d" % (l, b), [128, 512], F32, off=PB + 8192 + 2048 * b) for b in range(2)]
        wu, wd = ffn_tensors(l)

        def norm(tt):
            phase_norm(pi, tt)
        norm(0)
        rcount = 0
        for hc in range(8):
            s = hc % 2
            if hc > 0:
                ffn_load(l, hc, wu, wd)
            if hc == 7 and next_prefetch is not None:
                next_prefetch()
            for tt in range(NTT):
                if hc == 0 and tt + 1 < NTT:
                    norm(tt + 1)
                hb = (hc * 4 + tt) % 2
                for sub in range(4):
                    bk = mmbank()
                    MM([(ps[bk][:, :], wu[s][:, c, sub * 128:(sub + 1) * 128], hT[:, c, tok(tt)], c == 0, c == 7)
                        for c in range(8)], [wkey(2 * s), ("h", tt)], [("ps", bk)])
                    rb = rcount % 2
                    rcount += 1
                    ACT(rr[rb][:, :], ps[bk][:, :], AF.Relu, [], [("ps", bk), ("rr", rb)])
                    TT(hid[hb][:, sub, :], rr[rb][:, :], rr[rb][:, :], ALU.mult, [("rr", rb)], [("hid", hb, sub)])
                for dc in range(8):
                    bk = mmbank()
                    MM([(ps[bk][:, :], wd[s][:, sub, dc * 128:(dc + 1) * 128], hid[hb][:, sub, :], sub == 0, sub == 3)
                        for sub in range(4)], [wkey(2 * s + 1)] + [("hid", hb, sub) for sub in range(4)], [("ps", bk)])
                    TT(xT[:, dc, tok(tt)], ps[bk][:, :], xT[:, dc, tok(tt)], ALU.add, [], [("ps", bk), xk(dc, tt)])
                if hc == 7:
                    phase_norm(pi + 1, tt, extra_w=[wkey(0), wkey(1)] if pi + 1 >= len(phases) else [])
        p.fence()

    def sgu_wload(j, wc, src_ap, a):
        s = wc["n"] % 4
        wc["n"] += 1
        wsl_s = p.sb("sw%d_%d_%d" % (j, s, wc["n"]), [128, 4096], BF16, off=WB + 8192 * s)
        dst = wsl_s[:, :].rearrange("p (a b) -> p a b", a=a)
        p.dma("pool", "wr%d" % s, dst, src_ap, writes=[wkey(s)])
        return s, dst

    def sgu_vsrc(j, vc):
        return s_win[j, :, 2048 + vc * 512:2048 + (vc + 1) * 512].rearrange("(c p) n -> p c n", p=128)

    def sgu_prefetch(j, st8):
        B2 = p.sb("B2_%d" % j, [128, 16, 128], F32, off=PB + 49152)
        WcT = p.sb("WcT%d" % j, [128, 8, 128], BF16, off=PB + 57344)
        L2 = p.sb("L2_%d" % j, [2, 2048], F32, off=PB + 16384)
        R2 = p.sb("R2_%d" % j, [2, 8, 128], F32, off=PB + 16384 + 8192)
        wtmp = p.sb("wtmp%d" % j, [128, 1024], F32, off=PB + 16384 + 12288)
        p.dma("sp", "sgc1", L2[:, :], s_l2[j, :, :], writes=["L2"])
        p.dma("sp", "sgc2", wtmp[:, :], s_wsT[j, :, :], writes=["wtmp"])
        p.dma("sp", "sgc3", R2[1:2, :, :], s_b[j:j + 1, :, :], writes=["R2b"])
        for g in range(8):
            TT(WcT[:, g, :], wtmp[:, g * 128:(g + 1) * 128], maskT[:, :], ALU.mult, ["wtmp", "c"], [("WcT", g)])
        for half in range(2):
            bk = mmbank()
            MM([(ps[bk][0:1, :], ones[:, 0:1], WcT[:, half * 4:half * 4 + 4, :].rearrange("p a b -> p (a b)"), True, True)],
               [("WcT", g) for g in range(8)] + ["ones"], [("ps", bk)])
            DCOPY(R2[0:1, half * 4:half * 4 + 4, :].rearrange("p a b -> p (a b)"), ps[bk][0:1, :], [], [("ps", bk), ("R2a", half)])
        for bq in range(4):
            bk = mmbank()
            MM([(ps[bk][:, k * 128:(k + 1) * 128], L2[0:2, (4 * bq + k) * 128:(4 * bq + k + 1) * 128],
                 R2[0:2, (4 * bq + k) // 2, :], True, True) for k in range(4)],
               ["L2", "R2b", ("R2a", 0), ("R2a", 1)], [("ps", bk)])
            DCOPY(B2[:, 4 * bq:4 * bq + 4, :], ps[bk][:, :].rearrange("p (k t) -> p k t", k=4), [], [("ps", bk), "B2"])
        st8["B2"], st8["WcT"] = B2, WcT
        st8["wc"] = {"n": 0}
        st8["pre"] = [sgu_wload(j, st8["wc"], sgu_vsrc(j, vc), 8) for vc in range(2)]

    def sgu_layer(j, l, st8, next_prefetch=None, pi=0):
        B2, WcT = st8["B2"], st8["WcT"]
        uT = p.sb("uT%d" % j, [128, 16, 512], BF16, off=PB)
        vraw = p.sb("vraw%d" % j, [128, 4, 2048], F32, off=PB + 16384)
        vlnv = p.sb("vlnv%d" % j, [128, 4, 4096], BF16, off=PB + 16384)
        st = p.sb("st%d" % j, [128, 64], F32, off=PB + 59392)
        tmpx = [p.sb("tmpx%d_%d" % (j, b), [128, 512], F32, off=AR + 32768 + 4096 + 2048 * b) for b in range(2)]
        GC = 80 + 16 * j

        def norm(tt):
            phase_norm(pi, tt)
        norm(0)
        wc = st8["wc"]
        pre = list(st8["pre"])

        def wload(src_ap, shape3):
            return sgu_wload(j, wc, src_ap, shape3[0])

        allv = [("vraw", tch, vc) for tch in range(4) for vc in range(4)]

        def v_path(tg):
            T0 = tg * 512
            for vc in range(4):
                if pre:
                    s, w3 = pre.pop(0)
                else:
                    s, w3 = wload(sgu_vsrc(j, vc), (8, 512))
                for tch in range(4):
                    bk = mmbank()
                    MM([(ps[bk][:, :], hT[:, c, T0 + tch * 128:T0 + (tch + 1) * 128], w3[:, c, :], c == 0, c == 7)
                        for c in range(8)], [wkey(s), ("h", tg)], [("ps", bk)])
                    ACT(vraw[:, tch, vc * 512:(vc + 1) * 512], ps[bk][:, :], AF.Gelu, [],
                        [("ps", bk), ("vraw", tch, vc), ("stp", tch, vc)],
                        accum_out=st[:, tch * 4 + vc:tch * 4 + vc + 1])

        def ln_stats(tg):
            allp = [("stp", tch, vc) for tch in range(4) for vc in range(4)]
            p.op("dve", lambda e: e.reduce_sum(st[:, 16:20], st[:, 0:16].rearrange("p (a b) -> p a b", a=4), axis=AX.X),
                 allp, ["ssum"])
            for tch in range(4):
                ACT(sq[:, 0:4, :].rearrange("p a b -> p (a b)"), vraw[:, tch, :], AF.Square,
                    [("vraw", tch, vc) for vc in range(4)], ["sq", ("ssq", tch)], accum_out=st[:, 20 + tch:21 + tch])
            TS(st[:, 24:28], st[:, 16:20], 1.0 / 2048, None, ALU.mult, None, ["ssum"], ["mean"])
            TT(st[:, 28:32], st[:, 24:28], st[:, 24:28], ALU.mult, ["mean"], ["msq"])
            STT(st[:, 32:36], st[:, 20:24], 1.0 / 2048, st[:, 28:32], ALU.mult, ALU.subtract,
                ["msq"] + [("ssq", t) for t in range(4)], ["var"])
            ACT(st[:, 36:40], st[:, 32:36], AF.Ln, ["var", "eps"], ["sd"], bias=epsl[:, 0:1], scale=1.0)
            ACT(st[:, 40:44], st[:, 36:40], AF.Exp, ["sd"], ["rstd"], scale=-0.5)
            STT(st[:, 44:48], st[:, 24:28], -1.0, st[:, 40:44], ALU.mult, ALU.mult, ["mean", "rstd"], ["nmr"])
            for tch in range(4):
                vk = [("vraw", tch, vc) for vc in range(4)]
                ACT(vlnv[:, tch, 0:2048], vraw[:, tch, :], AF.Identity, ["rstd", "nmr"], vk,
                    bias=st[:, 44 + tch:45 + tch], scale=st[:, 40 + tch:41 + tch])

        tcount = {"n": 0}

        def spatial_unit(dch):
            g = dch // 2
            bk = mmbank()
            MM([(ps[bk][:, tch * 128:(tch + 1) * 128], vlnv[:, tch, dch * 128:(dch + 1) * 128], WcT[:, g, :], True, True)
                for tch in range(4)], allv + [("WcT", g)], [("ps", bk)])
            tb = tcount["n"] % 2
            tcount["n"] += 1
            t3 = tmpx[tb][:, :].rearrange("p (k t) -> p k t", k=4)
            STT(t3, ps[bk][:, :].rearrange("p (k t) -> p k t", k=4), gains[:, GC + dch:GC + dch + 1],
                B2[:, dch, :].unsqueeze(1).broadcast_to([128, 4, 128]), ALU.mult, ALU.add,
                ["B2", "c"], [("ps", bk), ("tmpx", tb)])
            TT(uT[:, dch, :], tmpx[tb][:, :], uT[:, dch, :], ALU.mult, [("tmpx", tb)], [("u", dch)])

        def u_spatial(tg):
            for idx in range(16):
                uc, sub = idx // 4, idx % 4
                if sub == 0:
                    s, w3 = wload(s_win[j, :, uc * 512:(uc + 1) * 512].rearrange("(c p) n -> p c n", p=128), (8, 512))
                    cur = (s, w3)
                s, w3 = cur
                bk = mmbank()
                MM([(ps[bk][:, :], w3[:, c, sub * 128:(sub + 1) * 128], hT[:, c, tok(tg)], c == 0, c == 7)
                    for c in range(8)], [wkey(s), ("h", tg)], [("ps", bk)])
                ACT(uT[:, idx, :], ps[bk][:, :], AF.Gelu, [], [("ps", bk), ("u", idx)])
                if idx >= 1:
                    spatial_unit(idx - 1)
            spatial_unit(15)

        def w_out(tg):
            for oc in range(4):
                s, w3 = wload(s_wout[j, :, oc * 256:(oc + 1) * 256].rearrange("(k p) n -> p k n", p=128), (16, 256))
                for dcl in range(2):
                    dc = oc * 2 + dcl
                    bk = mmbank()
                    MM([(ps[bk][:, :], w3[:, kc, dcl * 128:(dcl + 1) * 128], uT[:, kc, :], kc == 0, kc == 15)
                        for kc in range(16)], [wkey(s)] + [("u", kc) for kc in range(16)], [("ps", bk)])
                    TT(xT[:, dc, tok(tg)], ps[bk][:, :], xT[:, dc, tok(tg)], ALU.add, [], [("ps", bk), xk(dc, tg)])

        v_path(0)
        norm(1)
        ln_stats(0)
        u_spatial(0)
        for tg in range(1, NTT):
            v_path(tg)
            if tg + 1 < NTT:
                norm(tg + 1)
            ln_stats(tg)
            w_out(tg - 1)
            phase_norm(pi + 1, tg - 1)
            u_spatial(tg)
        w_out(NTT - 1)
        phase_norm(pi + 1, NTT - 1)
        if next_prefetch is not None:
            next_prefetch()
        p.fence()

    def mla_prefetch(j, st8):
        wdkv = p.sb("wdkv%d" % j, [128, 8, 512], BF16, off=WB)
        wuq = p.sb("wuq%d" % j, [128, 2, 2048], BF16, off=WB + 8192)
        p.dma("pool", "wr0", wdkv[:, :, :], w_dkv[j, :, :].rearrange("(c p) n -> p c n", p=128), writes=[wkey(0)])
        p.dma("pool", "wr1", wuq[:, :, :], w_uq[j, :, :].rearrange("(c p) n -> p c n", p=128), writes=[wkey(1)])
        st8["wdkv"], st8["wuq"] = wdkv, wuq

    def mla_layer(j, l, st8, next_prefetch=None, pi=0):
        wdkv, wuq = st8["wdkv"], st8["wuq"]
        wukv = p.sb("wukv%d" % j, [128, 2048], BF16, off=WB + 16384)
        wo_sl = p.sb("wo%d" % j, [128, 2, 1024], BF16, off=WB + 20480)
        OT = p.sb("OT%d" % j, [128, 2, NLOC], BF16, off=WB + 24576)
        KT = p.sb("KT%d" % j, [128, 2, SEQ], BF16, off=AR)
        V = p.sb("V%d" % j, [128, 32, 256], BF16, off=AR + 16384)
        o = PB
        cqn = p.sb("cqn%d" % j, [128, 2, NLOC], BF16, off=o); o += 8192
        QN = p.sb("QN%d" % j, [128, 2, NLOC], BF16, off=o)
        ckvl = p.sb("ckvl%d" % j, [128, NLOC], BF16, off=o)
        krl = p.sb("krl%d" % j, [64, NLOC], BF16, off=o + 4096); o += 8192
        ckva = p.sb("ckva%d" % j, [128, SEQ], BF16, off=o); o += 8192
        kra = p.sb("kra%d" % j, [128, SEQ], BF16, off=o); o += 8192
        QR = p.sb("QR%d" % j, [128, 2, NLOC], BF16, off=o); o += 8192
        PT_OFF = o
        PT = [p.sb("PT%d_%d" % (j, b), [128, 512], BF16, off=o + 1024 * b) for b in range(4)]; o += 4096
        cqf = p.sb("cqf%d" % j, [128, 2, 512], F32, off=o); o += 4096
        ckf = p.sb("ckf%d" % j, [128, 1, 512], F32, off=o); o += 2048
        t1 = p.sb("t1_%d" % j, [64, 512], F32, off=o); o += 2048
        t2 = p.sb("t2_%d" % j, [64, 512], F32, off=o); o += 2048
        acc = [p.sb("acc%d_0" % j, [128, 512], F32, off=o - 4096), p.sb("acc%d_1" % j, [128, 512], F32, off=o - 2048)]
        REC_OFF = o
        rec = [p.sb("rec%d_%d" % (j, b), [128, 512], F32, off=o + 2048 * b) for b in range(2)]; o += 4096
        assert o <= SB_END, o

        p.dma("pool", "mw2", wukv[:, :], w_ukv[j, :, :], writes=["wukv"] + [("xin", b) for b in range(4)])
        p.op("dve", lambda e: e.memset(kra[64:128, :], 0.0), writes=["kraz"])
        p.op("dve", lambda e: e.memset(QR[64:128, :, :], 0.0), writes=["QRz"])

        def norm(tt):
            phase_norm(pi, tt)
        norm(0)
        cqf2 = [cqf, p.sb("cqfb%d" % j, [128, 2, 512], F32, off=PT_OFF)]
        ckf2 = [ckf, p.sb("ckfb%d" % j, [128, 1, 512], F32, off=REC_OFF)]

        def lat_a(tt):
            bsel = tt % 2
            for fch in range(3):
                bk = mmbank()
                MM([(ps[bk][:, :], wdkv[:, c, fch * 128:(fch + 1) * 128], hT[:, c, tok(tt)], c == 0, c == 7)
                    for c in range(8)], [wkey(0), ("h", tt)], [("ps", bk)])
                if fch < 2:
                    EVAC(cqf2[bsel][:, fch, :], ps[bk][:, :], [], [("ps", bk), ("cqf", bsel, fch)])
                else:
                    EVAC(ckf2[bsel][:, 0, :], ps[bk][:, :], [], [("ps", bk), ("ckf", bsel)])
            ba = mmbank()
            MM([(ps[ba][0:64, :], wdkv[:, c, 384:448], hT[:, c, tok(tt)], c == 0, c == 7) for c in range(8)],
               [wkey(0), ("h", tt)], [("ps", ba)])
            bb = mmbank()
            MM([(ps[bb][0:64, :], wdkv[:, c, 448:512], hT[:, c, tok(tt)], c == 0, c == 7) for c in range(8)],
               [wkey(0), ("h", tt)], [("ps", bb)])
            TT(t1[:, :], ps[ba][0:64, :], cos2[:, tok(tt)], ALU.mult, ["cos2"], [("ps", ba), "t1"])
            TT(t2[:, :], ps[bb][0:64, :], sin2s[:, tok(tt)], ALU.mult, ["sin2s"], [("ps", bb), "t2"])
            TT(krl[:, tok(tt)], t1[:, :], t2[:, :], ALU.add, ["t1", "t2"], [("krl", tt)])

        def lat_b(tt):
            bsel = tt % 2
            rmsnorm(cqf2[bsel][:, :, :], 2, 256, 72 + 2 * j, [cqn[:, c, tok(tt)] for c in range(2)], epsn,
                    [("cqf", bsel, 0), ("cqf", bsel, 1)], [("cqn", tt)])
            rmsnorm(ckf2[bsel][:, :, :], 1, 128, 76 + j, [ckvl[:, tok(tt)]], epsn, [("ckf", bsel)], [("ckvl", tt)])

        for tt in range(NTT):
            if tt + 1 < NTT:
                norm(tt + 1)
            lat_a(tt)
            if tt >= 1:
                lat_b(tt - 1)
        lat_b(NTT - 1)
        lk = [("ckvl", tt) for tt in range(NTT)]
        rk = [("krl", tt) for tt in range(NTT)]
        p.dma("sp", "cci%d" % j, cc_in[j].ap()[0:128, :], ckvl[:, :], reads=lk, writes=["ccin_a"])
        p.dma("sp", "cci%d" % j, cc_in[j].ap()[128:192, :], krl[:, :], reads=rk, writes=["ccin_b"])

        def ccfn(e, j=j):
            return e.collective_compute("AllGather", ALU.bypass, replica_groups=PAIRS,
                                        ins=[cc_in[j].ap().opt()], outs=[cc_out[j].ap().opt()])
        p.custom("pool", "cc%d" % j, 1, ccfn, reads=["ccin_a", "ccin_b"], writes=["ccout"])
        for r in range(2):
            p.dma("sp", "ccl%d" % j, ckva[:, :].rearrange("p (i r t) -> p i r t", r=2, t=128)[:, :, r, :],
                  cc_out[j].ap()[r * 192:r * 192 + 128, :].rearrange("p (i t) -> p i t", t=128),
                  reads=["ccout"], writes=[("ckva", r)])
            p.dma("sp", "ccl%d" % j, kra[0:64, :].rearrange("p (i r t) -> p i r t", r=2, t=128)[:, :, r, :],
                  cc_out[j].ap()[r * 192 + 128:r * 192 + 192, :].rearrange("p (i t) -> p i t", t=128),
                  reads=["ccout"], writes=[("kra", r)])
        p.fence()
        for hp in range(4):
            for hl in range(2):
                h = 2 * hp + hl
                for kt in range(8):
                    bk = mmbank((0, 1, 2, 7))
                    MM([(ps[bk][:, :], wukv[:, h * 128:(h + 1) * 128], ckva[:, kt * 512:(kt + 1) * 512], True, True)],
                       ["wukv", ("ckva", 0), ("ckva", 1)], [("ps", bk)])
                    EVAC(KT[:, hl, kt * 512:(kt + 1) * 512], ps[bk][:, :], [], [("ps", bk), ("KT", hl)])
            for kp in range(16):
                bk = mmbank((0, 1, 2, 7))
                MM([(ps[bk][:, q * 256:(q + 1) * 256], ckva[:, (2 * kp + q) * 128:(2 * kp + q + 1) * 128],
                     wukv[:, 1024 + hp * 256:1024 + (hp + 1) * 256], True, True) for q in range(2)],
                   ["wukv", ("ckva", 0), ("ckva", 1)], [("ps", bk)])
                EVAC(V[:, 2 * kp:2 * kp + 2, :], ps[bk][:, :].rearrange("p (q d) -> p q d", q=2), [], [("ps", bk), "V"])
            p.dma("pool", "mw3", wo_sl[:, :, :],
                  w_o[j, hp * 256:(hp + 1) * 256, :].rearrange("(h p) n -> p h n", p=128), writes=["wo"])
            for tt in range(NTT):
                for hl in range(2):
                    h = 2 * hp + hl
                    bk = mmbank((0, 1, 2, 7))
                    MM([(ps[bk][:, :], wuq[:, kc, h * 128:(h + 1) * 128], cqn[:, kc, tok(tt)], kc == 0, kc == 1)
                        for kc in range(2)], [wkey(1), ("cqn", tt)], [("ps", bk)])
                    ACT(QN[:, hl, tok(tt)], ps[bk][:, :], AF.Copy, [], [("ps", bk), ("QN", hl)], scale=SCALE)
                    ba = mmbank((0, 1, 2, 7))
                    MM([(ps[ba][0:64, :], wuq[:, kc, 1024 + h * 64:1024 + (h + 1) * 64], cqn[:, kc, tok(tt)], kc == 0, kc == 1)
                        for kc in range(2)], [wkey(1), ("cqn", tt)], [("ps", ba)])
                    bb = mmbank((0, 1, 2, 7))
                    MM([(ps[bb][0:64, :], wuq[:, kc, 1536 + h * 64:1536 + (h + 1) * 64], cqn[:, kc, tok(tt)], kc == 0, kc == 1)
                        for kc in range(2)], [wkey(1), ("cqn", tt)], [("ps", bb)])
                    STT(t1[:, :], ps[ba][0:64, :], SCALE, cos2[:, tok(tt)], ALU.mult, ALU.mult, ["cos2"], [("ps", ba), "t1"])
                    STT(t2[:, :], ps[bb][0:64, :], SCALE, sin2s[:, tok(tt)], ALU.mult, ALU.mult, ["sin2s"], [("ps", bb), "t2"])
                    TT(QR[0:64, hl, tok(tt)], t1[:, :], t2[:, :], ALU.add, ["t1", "t2"], [("QR", hl)])
            if hp == 3 and next_prefetch is not None:
                next_prefetch()
            units = []
            for qg in range(4):
                i0 = 4 * qg
                nkt = 8 * qg + 8
                for kt in range(nkt):
                    imin = max(i0, kt // 2)
                    q0 = imin * 128
                    n = (i0 + 4) * 128 - q0
                    off = q0 - i0 * 128
                    midx = (kt % 2) if kt >= 2 * i0 else None
                    for hl in range(2):
                        units.append((qg, kt, hl, q0, n, off, midx, kt == 0, kt == nkt - 1))
            LOOK = 2

            def emit_pv(idx):
                (qg, kt, hl, q0, n, off, midx, first, last) = units[idx]
                pb = idx % 4
                ob, sbn = 3 + hl, 5 + hl
                if SUM_MODE == "mm":
                    MM([(ps[ob][:, off:off + n], V[:, kt, hl * 128:(hl + 1) * 128], PT[pb][:, off:off + n], first, last),
                        (ps[sbn][:, off:off + n], ones[:, :], PT[pb][:, off:off + n], first, last)],
                       ["V", ("pt", pb), "ones"], [("ps", ob), ("ps", sbn)])
                else:
                    MM([(ps[ob][:, off:off + n], V[:, kt, hl * 128:(hl + 1) * 128], PT[pb][:, off:off + n], first, last)],
                       ["V", ("pt", pb)], [("ps", ob)])
                    eng = "pool" if (hl == 0 and SUM_MODE == "pooldve") else "dve"
                    akey = "t1" if hl == 0 else "t2"
                    if first:
                        p.op(eng, lambda e, a=acc[hl], pt=PT[pb]: e.tensor_copy(a[:, :], pt[:, :]), [("pt", pb)], [akey])
                    else:
                        p.op(eng, lambda e, a=acc[hl][:, off:off + n], pt=PT[pb][:, off:off + n]: e.tensor_tensor(a, a, pt, ALU.add),
                             [("pt", pb)], [akey])
                if last:
                    if SUM_MODE != "mm":
                        MM([(ps[sbn][:, :], ones_f[:, :], acc[hl][:, :], True, True)], [akey, "ones"], [("ps", sbn)])
                    ACT(rec[hl][:, :], ps[sbn][:, :], AF.Ln, [], [("ps", sbn), ("rec", hl)])
                    ACT(rec[hl][:, :], rec[hl][:, :], AF.Exp, [], [("rec", hl)], scale=-1.0)
                    TT(OT[:, hl, tok(qg)], ps[ob][:, :], rec[hl][:, :], ALU.mult, [("rec", hl)], [("ps", ob), ("OT", qg)])
                    if hl == 1:
                        emit_wo(qg)

            def emit_wo(tt):
                for dc in range(8):
                    MM([(ps[7][:, :], wo_sl[:, h2, dc * 128:(dc + 1) * 128], OT[:, h2, tok(tt)], h2 == 0, h2 == 1)
                        for h2 in range(2)], ["wo", ("OT", tt)], [("ps", 7)])
                    TT(xT[:, dc, tok(tt)], ps[7][:, :], xT[:, dc, tok(tt)], ALU.add, [], [("ps", 7), xk(dc, tt)])

            for idx, (qg, kt, hl, q0, n, off, midx, first, last) in enumerate(units):
                sbk = (0, 1, 2)[idx % 3]
                pb = idx % 4
                ks = slice(kt * 128, (kt + 1) * 128)
                oo = ps[sbk][:, off:off + n]
                MM([(oo, KT[:, hl, ks], QN[:, hl, q0:q0 + n], True, False),
                    (oo, kra[:, ks], QR[:, hl, q0:q0 + n], False, True)],
                   [("KT", hl), ("QN", hl), ("QR", hl), ("kra", 0), ("kra", 1), "kraz", "QRz"], [("ps", sbk)])
                ACT(PT[pb][:, off:off + n], oo, AF.Exp, [], [("ps", sbk), ("pt", pb)])
                if midx is not None:
                    TT(PT[pb][:, off:off + 128], PT[pb][:, off:off + 128], masks[:, midx, 0:128], ALU.mult,
                       ["c"], [("pt", pb)])
                if idx >= LOOK:
                    emit_pv(idx - LOOK)
            for idx in range(len(units) - LOOK, len(units)):
                emit_pv(idx)
            for tt in range(NTT):
                if hp == 3:
                    phase_norm(pi + 1, tt, extra_w=[("KT", 0), ("KT", 1), "V"])
        p.fence()

    sts = [dict() for _ in phases]

    def make_prefetch(i):
        if i >= len(phases):
            return None
        kind, l = phases[i]
        if kind == "ffn":
            return lambda: ffn_prefetch(l)
        if kind == "sgu":
            return lambda: sgu_prefetch(l // 2, sts[i])
        return lambda: mla_prefetch(l // 2, sts[i])

    if phases:
        make_prefetch(0)()
    load_x()
    if not phases or phases[0][0] != "mla":
        p.fence()
    for i, (kind, l) in enumerate(phases):
        nxt = make_prefetch(i + 1)
        if kind == "ffn":
            ffn_layer(l, nxt, i)
        elif kind == "sgu":
            sgu_layer(l // 2, l, sts[i], nxt, i)
        else:
            mla_layer(l // 2, l, sts[i], nxt, i)

    for tt in range(NTT):
        phase_norm(len(phases), tt)
    p.final_wait("sp", okeys)
    p.emit()
    return nc, p


_CACHE = {}


def _prep_inputs(x, positions, norm_mix, norm_ffn, final_norm,
                 mla_w_dkv, mla_q_norm, mla_kv_norm, mla_w_uq, mla_w_ukv, mla_w_o,
                 sgu_w_in, sgu_ln_g, sgu_ln_b, sgu_w_spatial, sgu_b_spatial, sgu_w_out,
                 ffn_w_up, ffn_w_down):
    f32 = np.float32
    A = lambda a: np.ascontiguousarray(np.asarray(a))
    x = A(x); positions = A(positions)
    gains = np.zeros((128, 112), f32)
    gains[:, 0:32] = A(norm_mix).reshape(4, 8, 128).transpose(2, 0, 1).reshape(128, 32)
    gains[:, 32:64] = A(norm_ffn).reshape(4, 8, 128).transpose(2, 0, 1).reshape(128, 32)
    gains[:, 64:72] = A(final_norm).reshape(8, 128).T
    gains[:, 72:76] = A(mla_q_norm).reshape(2, 2, 128).transpose(2, 0, 1).reshape(128, 4)
    gains[:, 76:78] = A(mla_kv_norm).reshape(2, 128).T
    gains[:, 80:112] = A(sgu_ln_g).reshape(2, 16, 128).transpose(2, 0, 1).reshape(128, 32)
    invf = (10000.0 ** (-np.arange(0, 64, 2, dtype=f32) / 64.0)).astype(f32)
    ropec = np.zeros((64, 2), f32)
    ropec[:, 0] = np.concatenate([invf, invf])
    ropec[:, 1] = np.concatenate([-np.ones(32, f32), np.ones(32, f32)])
    ident = np.eye(128, dtype=f32)
    maskT = (np.arange(128)[:, None] <= np.arange(128)[None, :]).astype(ml_dtypes.bfloat16)
    wd = A(mla_w_dkv)
    kr = wd[:, :, 384:448]
    w_dkv = A(np.concatenate([wd, kr[:, :, 32:64], kr[:, :, 0:32]], axis=2))
    wq = A(mla_w_uq).reshape(2, 256, 8, 192)
    qn = wq[:, :, :, 0:128].reshape(2, 256, 1024)
    qr = wq[:, :, :, 128:192]
    qsw = np.concatenate([qr[..., 32:64], qr[..., 0:32]], axis=-1)
    w_uq = A(np.concatenate([qn, qr.reshape(2, 256, 512), qsw.reshape(2, 256, 512)], axis=2))
    wkv = A(mla_w_ukv).reshape(2, 128, 8, 256)
    w_ukv = A(np.concatenate([wkv[:, :, :, 0:128].reshape(2, 128, 1024), wkv[:, :, :, 128:256].reshape(2, 128, 1024)], axis=2))
    s_l2 = A(np.stack([A(sgu_ln_b), np.ones((2, 2048), f32)], axis=1))
    s_b = A(sgu_b_spatial)
    s_wsT = A(A(sgu_w_spatial).transpose(0, 3, 1, 2).reshape(2, 128, 1024))
    shared = dict(maskT=maskT, gains=gains, ropec=ropec, ident=ident, w_dkv=w_dkv, w_uq=w_uq, w_ukv=w_ukv,
                  w_o=A(mla_w_o), s_win=A(sgu_w_in), s_wout=A(sgu_w_out), s_l2=s_l2, s_b=s_b,
                  s_wsT=s_wsT, f_up=A(ffn_w_up), f_dn=A(ffn_w_down))
    tri = (np.arange(128)[:, None] <= np.arange(128)[None, :])
    in_maps = []
    for c in range(8):
        b, r = c // 2, c % 2
        xs = A(x[b].reshape(16, 2, 128, D_MODEL)[:, r].reshape(NLOC, D_MODEL))
        ps_ = positions[b].reshape(16, 2, 128)[:, r].reshape(1, NLOC)
        pos = A(np.broadcast_to(ps_, (64, NLOC))).astype(np.int32)
        m = np.zeros((128, 2, 256), ml_dtypes.bfloat16)
        if r == 0:
            m[:, 0, 0:128] = tri; m[:, 0, 128:256] = tri
        else:
            m[:, 0, :] = 1
            m[:, 1, 0:128] = tri; m[:, 1, 128:256] = tri
        d = dict(shared)
        d.update(xs=xs, pos=pos, masks=m)
        in_maps.append(d)
    return in_maps


def kernel(**inputs):
    if "nc" not in _CACHE:
        _CACHE["nc"] = build_program(4)[0]
    nc = _CACHE["nc"]
    in_maps = _prep_inputs(**inputs)
    res = run_bass_kernel_spmd(nc, in_maps, core_ids=list(range(8)))
    out = np.zeros((4, SEQ, D_MODEL), np.float32)
    ov = out.reshape(4, 16, 2, 128, D_MODEL)
    for c in range(8):
        b, r = c // 2, c % 2
        ov[b, :, r] = np.asarray(res.results[c]["y"]).reshape(16, 128, D_MODEL)
    return out
```

```python
import contextlib
import numpy as np
import ml_dtypes
import concourse.bass as bass
import concourse.mybir as mybir
from concourse.bass_utils import run_bass_kernel_spmd

F32 = mybir.dt.float32
BF16 = mybir.dt.bfloat16
I32 = mybir.dt.int32
AF = mybir.ActivationFunctionType
ALU = mybir.AluOpType
AX = mybir.AxisListType

SB_BASE = 16512
SB_END = 229376

D_MODEL = 1024
SEQ = 4096
NLOC = 2048
NTT = 4
NORM_EPS = 1e-6
LN_EPS = 1e-5
SCALE = 192.0 ** -0.5
PAIRS = [[0, 1], [2, 3], [4, 5], [6, 7]]
SUM_MODE = "dvepsum"


class Prog:
    ENGS = ("pe", "act", "dve", "pool", "sp")

    def __init__(self, nc):
        self.nc = nc
        self.ops = {e: [] for e in self.ENGS}
        self.cnt = {e: 0 for e in self.ENGS}
        self.waited = {e: {} for e in self.ENGS}
        self.last_w = {}
        self.readers = {}
        self.dcnt = {}
        self.sems = {}
        self.sb_off = SB_BASE

    def sb(self, name, shape, dt, off=None):
        size = int(np.prod(shape[1:])) * (4 if dt in (F32, I32) else 2)
        size = (size + 31) // 32 * 32
        if off is None:
            off = self.sb_off
            self.sb_off += size
            assert self.sb_off <= SB_END, (name, self.sb_off)
        else:
            assert off + size <= SB_END, (name, off, size)
        return self.nc.alloc_sbuf_tensor_at(name, list(shape), dt, offset=off)

    def _deps(self, eng, reads, writes):
        deps = []
        for r in reads:
            if r in self.last_w:
                deps.append(self.last_w[r])
        for w in writes:
            if w in self.last_w:
                deps.append(self.last_w[w])
            deps.extend(self.readers.get(w, ()))
        best = {}
        for (sk, val, deng) in deps:
            if deng == "pe" and eng == "pe":
                continue
            if self.waited[eng].get(sk, 0) >= val:
                continue
            best[sk] = max(best.get(sk, 0), val)
        for sk, val in best.items():
            self.waited[eng][sk] = val
        return list(best.items())

    def _commit(self, token, reads, writes):
        for r in reads:
            self.readers.setdefault(r, []).append(token)
        for w in writes:
            self.last_w[w] = token
            self.readers[w] = []

    def op(self, eng, fn, reads=(), writes=()):
        reads = list(reads)
        writes = list(writes)
        waits = self._deps(eng, reads, writes)
        self.cnt[eng] += 1
        self.ops[eng].append((waits, fn, (eng, 1)))
        self._commit((eng, self.cnt[eng], eng), reads, writes)

    def dma(self, q, semname, out, in_, reads=(), writes=()):
        def fn(e, out=out, in_=in_):
            return e.dma_start(out=out, in_=in_)
        self.custom(q, semname, 16, fn, reads, writes)

    def custom(self, q, semname, inc, fn, reads=(), writes=()):
        reads = list(reads)
        writes = list(writes)
        waits = self._deps(q, reads, writes)
        sk = "d_" + semname
        self.dcnt[sk] = self.dcnt.get(sk, 0) + inc
        self.ops[q].append((waits, fn, (sk, inc)))
        self._commit((sk, self.dcnt[sk], "dma"), reads, writes)

    def fence(self):
        for e in self.ENGS:
            waits = []
            for f in ("pe", "act", "dve", "pool"):
                if f == e or self.cnt[f] == 0:
                    continue
                if self.waited[e].get(f, 0) >= self.cnt[f]:
                    continue
                self.waited[e][f] = self.cnt[f]
                waits.append((f, self.cnt[f]))
            for sk, val in self.dcnt.items():
                if self.waited[e].get(sk, 0) >= val:
                    continue
                self.waited[e][sk] = val
                waits.append((sk, val))
            if waits:
                self.ops[e].append((waits, None, None))

    def final_wait(self, q, keys):
        waits = self._deps(q, keys, [])
        self.ops[q].append((waits, None, None))

    def emit(self):
        nc = self.nc
        names = set()
        for e in self.ENGS:
            for waits, fn, inc in self.ops[e]:
                for sk, _ in waits:
                    names.add(sk)
                if inc is not None:
                    names.add(inc[0])
        with contextlib.ExitStack() as st:
            for n in sorted(names):
                self.sems[n] = st.enter_context(nc.semaphore("s_" + n))
            block = st.enter_context(nc.Block())

            def run(e, h):
                for waits, fn, inc in self.ops[e]:
                    for sk, val in waits:
                        h.wait_ge(self.sems[sk], val)
                    if fn is None:
                        continue
                    ins = fn(h)
                    if inc is not None:
                        ins.then_inc(self.sems[inc[0]], inc[1])

            @block.tensor
            def _(h):
                run("pe", h)

            @block.scalar
            def _(h):
                run("act", h)

            @block.vector
            def _(h):
                run("dve", h)

            @block.gpsimd
            def _(h):
                run("pool", h)

            @block.sync
            def _(h):
                run("sp", h)


def build_program(nlayers=4):
    nc = bass.Bass("TRN2", target_bir_lowering=False)

    def din(name, shape, dt):
        return nc.dram_tensor(name, list(shape), dt, kind="ExternalInput").ap()

    xs = din("xs", [NLOC, D_MODEL], F32)
    pos = din("pos", [64, NLOC], I32)
    masks_d = din("masks", [128, 2, 256], BF16)
    maskT_d = din("maskT", [128, 128], BF16)
    gains_d = din("gains", [128, 112], F32)
    rc_d = din("ropec", [64, 2], F32)
    ident_d = din("ident", [128, 128], F32)
    w_dkv = din("w_dkv", [2, 1024, 512], F32)
    w_uq = din("w_uq", [2, 256, 2048], F32)
    w_ukv = din("w_ukv", [2, 128, 2048], F32)
    w_o = din("w_o", [2, 1024, 1024], F32)
    s_win = din("s_win", [2, 1024, 4096], F32)
    s_wout = din("s_wout", [2, 2048, 1024], F32)
    s_l2 = din("s_l2", [2, 2, 2048], F32)
    s_b = din("s_b", [2, 8, 128], F32)
    s_wsT = din("s_wsT", [2, 128, 1024], F32)
    f_up = din("f_up", [4, 1024, 4096], F32)
    f_dn = din("f_dn", [4, 4096, 1024], F32)
    y = nc.dram_tensor("y", [NLOC, D_MODEL], F32, kind="ExternalOutput").ap()
    cc_in = [nc.dram_tensor("cc_in%d" % j, [192, NLOC], BF16) for j in range(2)]
    cc_out = [nc.dram_tensor("cc_out%d" % j, [384, NLOC], BF16) for j in range(2)]

    p = Prog(nc)
    xT = p.sb("xT", [128, 8, NLOC], F32)
    cos2 = p.sb("cos2", [64, NLOC], BF16)
    sin2s = p.sb("sin2s", [64, NLOC], BF16)
    ident = p.sb("ident", [128, 128], F32)
    ones = p.sb("ones", [128, 128], BF16)
    masks = p.sb("masksb", [128, 2, 256], BF16)
    maskT = p.sb("maskTb", [128, 128], BF16)
    gains = p.sb("gainsb", [128, 112], F32)
    ones_f = p.sb("ones_f", [128, 128], F32)
    ropec = p.sb("ropecb", [64, 2], F32)
    epsn = p.sb("epsn", [128, 1], F32)
    epsl = p.sb("epsl", [128, 1], F32)
    AR = p.sb_off
    hT = p.sb("hT", [128, 8, NLOC], BF16, off=AR)
    sq = p.sb("sq", [128, 8, 512], BF16, off=AR + 32768)
    rs = p.sb("rs", [128, 512], F32, off=AR + 40960)
    WB = AR + 43008
    PB = WB + 32768
    assert SB_END - PB >= 59000, SB_END - PB
    ps = [nc.alloc_psum_tensor("ps%d" % b, [128, 512], F32) for b in range(8)]

    state = {"mm": 0, "st": 0}

    def mmbank(pool=(0, 1, 2, 3, 4, 5)):
        b = pool[state["mm"] % len(pool)]
        state["mm"] += 1
        return b

    def stbank():
        b = (6, 7)[state["st"] % 2]
        state["st"] += 1
        return b

    def tok(tt):
        return slice(tt * 512, (tt + 1) * 512)

    def MM(mms, reads, writes):
        def fn(e, mms=mms):
            ins = None
            for (o, l, r, s0, s1) in mms:
                ins = e.matmul(o, l, r, start=s0, stop=s1, skip_group_check=True)
            return ins
        p.op("pe", fn, reads, writes)

    def ACT(out, in_, func, reads, writes, **kw):
        p.op("act", lambda e: e.activation(out, in_, func, **kw), reads, writes)

    def TT(out, a, b, op, reads, writes):
        p.op("dve", lambda e: e.tensor_tensor(out, a, b, op), reads, writes)

    def TS(out, a, s1, s2, op0, op1, reads, writes):
        if op1 is None:
            p.op("dve", lambda e: e.tensor_scalar(out, a, s1, None, op0), reads, writes)
        else:
            p.op("dve", lambda e: e.tensor_scalar(out, a, s1, s2, op0, op1), reads, writes)

    def STT(out, a, s, b, op0, op1, reads, writes):
        p.op("dve", lambda e: e.scalar_tensor_tensor(out=out, in0=a, scalar=s, in1=b, op0=op0, op1=op1), reads, writes)

    def DCOPY(out, in_, reads, writes):
        p.op("dve", lambda e: e.tensor_copy(out, in_), reads, writes)

    def RECIP(out, in_, reads, writes):
        p.op("dve", lambda e: e.reciprocal(out, in_), reads, writes)

    evs = {"n": 0}

    def EVAC(out, in_, reads, writes, scale=None):
        evs["n"] += 1
        if evs["n"] % 2 == 0 and scale is None:
            DCOPY(out, in_, reads, writes)
        else:
            if scale is None:
                ACT(out, in_, AF.Copy, reads, writes)
            else:
                ACT(out, in_, AF.Copy, reads, writes, scale=scale)

    def xk(c, tt):
        return ("x", c, tt)

    def xkeys(tt):
        return [xk(c, tt) for c in range(8)]

    def rmsnorm(src3, nch, D, gcol, dsts, eps_t, reads, writes, np_=128):
        P = slice(0, np_)
        reads = list(reads) + ["cos2", "sin2s"]
        ACT(sq[P, 0:nch, :], src3, AF.Square, reads, ["sq", ("tmpx", 0), ("tmpx", 1)])
        sb_ = stbank()
        MM([(ps[sb_][P, :], ones[P, 0:np_], sq[P, c, :], c == 0, c == nch - 1) for c in range(nch)],
           ["sq", "ones"], [("ps", sb_)])
        ACT(rs[P, :], ps[sb_][P, :], AF.Ln, ["eps"], [("ps", sb_), "rs"], bias=eps_t[P, 0:1], scale=1.0 / D)
        ACT(rs[P, :], rs[P, :], AF.Exp, [], ["rs"], scale=-0.5)
        for c in range(nch):
            STT(dsts[c], src3[:, c, :], gains[P, gcol + c:gcol + c + 1], rs[P, :], ALU.mult, ALU.mult,
                reads + ["rs", "c"], [writes[c]] if isinstance(writes, list) and len(writes) == nch else writes)

    for (dst, src) in ((ident[:, :], ident_d), (masks[:, :, :], masks_d), (maskT[:, :], maskT_d),
                       (gains[:, :], gains_d), (ropec[:, :], rc_d)):
        p.dma("sp", "c", dst, src, writes=["c"])
    p.op("dve", lambda e: e.memset(ones[:, :], 1.0), writes=["ones"])
    p.op("dve", lambda e: e.memset(ones_f[:, :], 1.0), writes=["ones"])
    p.op("dve", lambda e: e.memset(epsn[:, :], NORM_EPS), writes=["eps"])
    p.op("dve", lambda e: e.memset(epsl[:, :], LN_EPS), writes=["eps"])

    pos_i = p.sb("pos_i", [64, NLOC], I32, off=AR)
    pos_f = p.sb("pos_f", [64, NLOC], F32, off=AR + 8192)
    ang = p.sb("ang", [64, NLOC], F32, off=AR + 16384)
    ang2 = p.sb("ang2", [64, NLOC], F32, off=AR + 24576)
    TWO_PI = 2.0 * float(np.pi)
    p.dma("sp", "pos", pos_i[:, :], pos, writes=["pos_i"])
    DCOPY(pos_f[:, :], pos_i[:, :], ["pos_i"], ["pos_f"])
    TS(ang[:, :], pos_f[:, :], ropec[:, 0:1], None, ALU.mult, None, ["pos_f", "c"], ["ang"])

    def reduce_and_sin(dst, shift, key):
        TS(ang2[:, :], ang[:, :], shift, None, ALU.add, None, ["ang"], ["ang2"])
        TS(pos_f[:, :], ang2[:, :], 1.0 / TWO_PI, None, ALU.mult, None, ["ang2"], ["pos_f"])
        DCOPY(pos_i[:, :], pos_f[:, :], ["pos_f"], ["pos_i"])
        DCOPY(pos_f[:, :], pos_i[:, :], ["pos_i"], ["pos_f"])
        STT(ang2[:, :], pos_f[:, :], -TWO_PI, ang2[:, :], ALU.mult, ALU.add, ["pos_f", "ang2"], ["ang2"])
        TS(pos_f[:, :], ang2[:, :], float(np.pi), TWO_PI, ALU.is_gt, ALU.mult, ["ang2"], ["pos_f"])
        TT(ang2[:, :], ang2[:, :], pos_f[:, :], ALU.subtract, ["pos_f", "ang2"], ["ang2"])
        ACT(dst, ang2[:, :], AF.Sin, ["ang2"], [key])

    reduce_and_sin(cos2[:, :], 0.5 * float(np.pi), "cos2")
    sin_f = p.sb("sin_f", [64, NLOC], F32, off=AR + 32768 - 8192 + 8192)
    reduce_and_sin(sin_f[:, :], 0.0, "sin_f")
    TS(sin2s[:, :], sin_f[:, :], ropec[:, 1:2], None, ALU.mult, None, ["sin_f", "c"], ["sin2s"])

    xin = [p.sb("xin%d" % b, [128, 1024], F32, off=WB + 16384 + 4096 * b) for b in range(4)]

    def load_x():
        for tile in range(16):
            b = tile % 4
            tt = tile // 4
            p.dma("sp", "xin%d" % b, xin[b][:, :], xs[tile * 128:(tile + 1) * 128, :], writes=[("xin", b)])
            for half in range(2):
                bk = mmbank()
                p.op("pe", (lambda e, bk=bk, b=b, half=half: [e.transpose(ps[bk][:, k * 128:(k + 1) * 128],
                                                                          xin[b][:, (half * 4 + k) * 128:(half * 4 + k + 1) * 128],
                                                                          ident[:, :]) for k in range(4)][-1]),
                     [("xin", b), "c"], [("ps", bk)])
                EVAC(xT[:, half * 4:half * 4 + 4, tile * 128:(tile + 1) * 128],
                     ps[bk][:, :].rearrange("p (k t) -> p k t", k=4),
                     [], [("ps", bk)] + [xk(half * 4 + k, tt) for k in range(4)])

    phases = []
    for l_ in range(nlayers):
        phases.append(("mla" if l_ % 2 == 0 else "sgu", l_))
        phases.append(("ffn", l_))
    normed = set()
    okeys = []
    yT = p.sb("yT", [128, 8, 512], F32, off=PB + 16384)
    yo = [p.sb("yo%d" % b, [128, 1024], F32, off=WB + 4096 * b) for b in range(4)]

    def final_part(tt, extra_w):
        rmsnorm(xT[:, :, tok(tt)], 8, 1024, 64, [yT[:, c, :] for c in range(8)], epsn, xkeys(tt), ["yT"])
        for tl in range(4):
            tile = tt * 4 + tl
            b = tile % 4
            for half in range(2):
                bk = mmbank()
                p.op("pe", (lambda e, bk=bk, half=half, tl=tl: [e.transpose(ps[bk][:, k * 128:(k + 1) * 128],
                                                                            yT[:, half * 4 + k, tl * 128:(tl + 1) * 128],
                                                                            ident[:, :]) for k in range(4)][-1]),
                     ["yT", "c"], [("ps", bk)])
                EVAC(yo[b][:, half * 512:(half + 1) * 512], ps[bk][:, :], [], [("ps", bk), ("yo", b, half)] + list(extra_w))
            p.dma("sp", "yo%d" % b, y[tile * 128:(tile + 1) * 128, :], yo[b][:, :],
                  reads=[("yo", b, 0), ("yo", b, 1)], writes=[("yout", tile)])
            okeys.append(("yout", tile))

    def phase_norm(i, tt, extra_w=()):
        if (i, tt) in normed:
            return
        normed.add((i, tt))
        if i >= len(phases):
            final_part(tt, extra_w)
            return
        kind, l = phases[i]
        gcol = 32 + l * 8 if kind == "ffn" else l * 8
        rmsnorm(xT[:, :, tok(tt)], 8, 1024, gcol, [hT[:, c, tok(tt)] for c in range(8)], epsn,
                xkeys(tt), [("h", tt)] + list(extra_w))

    def wkey(r):
        return ("Wreg", r)

    def ffn_tensors(l):
        wu = [p.sb("wu%d_%d" % (l, s), [128, 8, 512], BF16, off=WB + 16384 * s) for s in range(2)]
        wd = [p.sb("wd%d_%d" % (l, s), [128, 4, 1024], BF16, off=WB + 8192 + 16384 * s) for s in range(2)]
        return wu, wd

    def ffn_load(l, hc, wu, wd, extra=()):
        s = hc % 2
        p.dma("pool", "wr%d" % (2 * s), wu[s][:, :, :],
              f_up[l, :, hc * 512:(hc + 1) * 512].rearrange("(c p) n -> p c n", p=128),
              writes=[wkey(2 * s)] + [k for k in extra if k[1] == 2 * s])
        p.dma("pool", "wr%d" % (2 * s + 1), wd[s][:, :, :],
              f_dn[l, hc * 512:(hc + 1) * 512, :].rearrange("(s p) n -> p s n", p=128),
              writes=[wkey(2 * s + 1)] + [k for k in extra if k[1] == 2 * s + 1])

    def ffn_prefetch(l):
        wu, wd = ffn_tensors(l)
        ffn_load(l, 0, wu, wd)

    def ffn_layer(l, next_prefetch=None, pi=0):
        hid = [p.sb("hid%d_%d" % (l, b), [128, 4, 512], BF16, off=PB + 4096 * b) for b in range(2)]
        rr = [p.sb("rr%d_%d" % (l, b), [128, 512], F32, off=PB + 8192 + 2048 * b) for b in range(2)]
        wu, wd = ffn_tensors(l)

        def norm(tt):
            phase_norm(pi, tt)
        norm(0)
        rcount = 0
        for hc in range(8):
            s = hc % 2
            if hc > 0:
                ffn_load(l, hc, wu, wd)
            if hc == 7 and next_prefetch is not None:
                next_prefetch()
            for tt in range(NTT):
                if hc == 0 and tt + 1 < NTT:
                    norm(tt + 1)
                hb = (hc * 4 + tt) % 2
                for sub in range(4):
                    bk = mmbank()
                    MM([(ps[bk][:, :], wu[s][:, c, sub * 128:(sub + 1) * 128], hT[:, c, tok(tt)], c == 0, c == 7)
                        for c in range(8)], [wkey(2 * s), ("h", tt)], [("ps", bk)])
                    rb = rcount % 2
                    rcount += 1
                    ACT(rr[rb][:, :], ps[bk][:, :], AF.Relu, [], [("ps", bk), ("rr", rb)])
                    TT(hid[hb][:, sub, :], rr[rb][:, :], rr[rb][:, :], ALU.mult, [("rr", rb)], [("hid", hb, sub)])
                for dc in range(8):
                    bk = mmbank()
                    MM([(ps[bk][:, :], wd[s][:, sub, dc * 128:(dc + 1) * 128], hid[hb][:, sub, :], sub == 0, sub == 3)
                        for sub in range(4)], [wkey(2 * s + 1)] + [("hid", hb, sub) for sub in range(4)], [("ps", bk)])
                    TT(xT[:, dc, tok(tt)], ps[bk][:, :], xT[:, dc, tok(tt)], ALU.add, [], [("ps", bk), xk(dc, tt)])
                if hc == 7:
                    phase_norm(pi + 1, tt, extra_w=[wkey(0), wkey(1)] if pi + 1 >= len(phases) else [])
        p.fence()

    def sgu_wload(j, wc, src_ap, a):
        s = wc["n"] % 4
        wc["n"] += 1
        wsl_s = p.sb("sw%d_%d_%d" % (j, s, wc["n"]), [128, 4096], BF16, off=WB + 8192 * s)
        dst = wsl_s[:, :].rearrange("p (a b) -> p a b", a=a)
        p.dma("pool", "wr%d" % s, dst, src_ap, writes=[wkey(s)])
        return s, dst

    def sgu_vsrc(j, vc):
        return s_win[j, :, 2048 + vc * 512:2048 + (vc + 1) * 512].rearrange("(c p) n -> p c n", p=128)

    def sgu_prefetch(j, st8):
        B2 = p.sb("B2_%d" % j, [128, 16, 128], F32, off=PB + 49152)
        WcT = p.sb("WcT%d" % j, [128, 8, 128], BF16, off=PB + 57344)
        L2 = p.sb("L2_%d" % j, [2, 2048], F32, off=PB + 16384)
        R2 = p.sb("R2_%d" % j, [2, 8, 128], F32, off=PB + 16384 + 8192)
        wtmp = p.sb("wtmp%d" % j, [128, 1024], F32, off=PB + 16384 + 12288)
        p.dma("sp", "sgc1", L2[:, :], s_l2[j, :, :], writes=["L2"])
        p.dma("sp", "sgc2", wtmp[:, :], s_wsT[j, :, :], writes=["wtmp"])
        p.dma("sp", "sgc3", R2[1:2, :, :], s_b[j:j + 1, :, :], writes=["R2b"])
        for g in range(8):
            TT(WcT[:, g, :], wtmp[:, g * 128:(g + 1) * 128], maskT[:, :], ALU.mult, ["wtmp", "c"], [("WcT", g)])
        for half in range(2):
            bk = mmbank()
            MM([(ps[bk][0:1, :], ones[:, 0:1], WcT[:, half * 4:half * 4 + 4, :].rearrange("p a b -> p (a b)"), True, True)],
               [("WcT", g) for g in range(8)] + ["ones"], [("ps", bk)])
            DCOPY(R2[0:1, half * 4:half * 4 + 4, :].rearrange("p a b -> p (a b)"), ps[bk][0:1, :], [], [("ps", bk), ("R2a", half)])
        for bq in range(4):
            bk = mmbank()
            MM([(ps[bk][:, k * 128:(k + 1) * 128], L2[0:2, (4 * bq + k) * 128:(4 * bq + k + 1) * 128],
                 R2[0:2, (4 * bq + k) // 2, :], True, True) for k in range(4)],
               ["L2", "R2b", ("R2a", 0), ("R2a", 1)], [("ps", bk)])
            DCOPY(B2[:, 4 * bq:4 * bq + 4, :], ps[bk][:, :].rearrange("p (k t) -> p k t", k=4), [], [("ps", bk), "B2"])
        st8["B2"], st8["WcT"] = B2, WcT
        st8["wc"] = {"n": 0}
        st8["pre"] = [sgu_wload(j, st8["wc"], sgu_vsrc(j, vc), 8) for vc in range(2)]

    def sgu_layer(j, l, st8, next_prefetch=None, pi=0):
        B2, WcT = st8["B2"], st8["WcT"]
        uT = p.sb("uT%d" % j, [128, 16, 512], BF16, off=PB)
        vraw = p.sb("vraw%d" % j, [128, 4, 2048], F32, off=PB + 16384)
        vlnv = p.sb("vlnv%d" % j, [128, 4, 4096], BF16, off=PB + 16384)
        st = p.sb("st%d" % j, [128, 64], F32, off=PB + 59392)
        tmpx = [p.sb("tmpx%d_%d" % (j, b), [128, 512], F32, off=AR + 32768 + 4096 + 2048 * b) for b in range(2)]
        GC = 80 + 16 * j

        def norm(tt):
            phase_norm(pi, tt)
        norm(0)
        wc = st8["wc"]
        pre = list(st8["pre"])

        def wload(src_ap, shape3):
            return sgu_wload(j, wc, src_ap, shape3[0])

        allv = [("vraw", tch, vc) for tch in range(4) for vc in range(4)]

        def v_path(tg):
            T0 = tg * 512
            for vc in range(4):
                if pre:
                    s, w3 = pre.pop(0)
                else:
                    s, w3 = wload(sgu_vsrc(j, vc), (8, 512))
                for tch in range(4):
                    bk = mmbank()
                    MM([(ps[bk][:, :], hT[:, c, T0 + tch * 128:T0 + (tch + 1) * 128], w3[:, c, :], c == 0, c == 7)
                        for c in range(8)], [wkey(s), ("h", tg)], [("ps", bk)])
                    ACT(vraw[:, tch, vc * 512:(vc + 1) * 512], ps[bk][:, :], AF.Gelu, [],
                        [("ps", bk), ("vraw", tch, vc), ("stp", tch, vc)],
                        accum_out=st[:, tch * 4 + vc:tch * 4 + vc + 1])

        def ln_stats(tg):
            allp = [("stp", tch, vc) for tch in range(4) for vc in range(4)]
            p.op("dve", lambda e: e.reduce_sum(st[:, 16:20], st[:, 0:16].rearrange("p (a b) -> p a b", a=4), axis=AX.X),
                 allp, ["ssum"])
            for tch in range(4):
                ACT(sq[:, 0:4, :].rearrange("p a b -> p (a b)"), vraw[:, tch, :], AF.Square,
                    [("vraw", tch, vc) for vc in range(4)], ["sq", ("ssq", tch)], accum_out=st[:, 20 + tch:21 + tch])
            TS(st[:, 24:28], st[:, 16:20], 1.0 / 2048, None, ALU.mult, None, ["ssum"], ["mean"])
            TT(st[:, 28:32], st[:, 24:28], st[:, 24:28], ALU.mult, ["mean"], ["msq"])
            STT(st[:, 32:36], st[:, 20:24], 1.0 / 2048, st[:, 28:32], ALU.mult, ALU.subtract,
                ["msq"] + [("ssq", t) for t in range(4)], ["var"])
            ACT(st[:, 36:40], st[:, 32:36], AF.Ln, ["var", "eps"], ["sd"], bias=epsl[:, 0:1], scale=1.0)
            ACT(st[:, 40:44], st[:, 36:40], AF.Exp, ["sd"], ["rstd"], scale=-0.5)
            STT(st[:, 44:48], st[:, 24:28], -1.0, st[:, 40:44], ALU.mult, ALU.mult, ["mean", "rstd"], ["nmr"])
            for tch in range(4):
                vk = [("vraw", tch, vc) for vc in range(4)]
                ACT(vlnv[:, tch, 0:2048], vraw[:, tch, :], AF.Identity, ["rstd", "nmr"], vk,
                    bias=st[:, 44 + tch:45 + tch], scale=st[:, 40 + tch:41 + tch])

        tcount = {"n": 0}

        def spatial_unit(dch):
            g = dch // 2
            bk = mmbank()
            MM([(ps[bk][:, tch * 128:(tch + 1) * 128], vlnv[:, tch, dch * 128:(dch + 1) * 128], WcT[:, g, :], True, True)
                for tch in range(4)], allv + [("WcT", g)], [("ps", bk)])
            tb = tcount["n"] % 2
            tcount["n"] += 1
            t3 = tmpx[tb][:, :].rearrange("p (k t) -> p k t", k=4)
            STT(t3, ps[bk][:, :].rearrange("p (k t) -> p k t", k=4), gains[:, GC + dch:GC + dch + 1],
                B2[:, dch, :].unsqueeze(1).broadcast_to([128, 4, 128]), ALU.mult, ALU.add,
                ["B2", "c"], [("ps", bk), ("tmpx", tb)])
            TT(uT[:, dch, :], tmpx[tb][:, :], uT[:, dch, :], ALU.mult, [("tmpx", tb)], [("u", dch)])

        def u_spatial(tg):
            for idx in range(16):
                uc, sub = idx // 4, idx % 4
                if sub == 0:
                    s, w3 = wload(s_win[j, :, uc * 512:(uc + 1) * 512].rearrange("(c p) n -> p c n", p=128), (8, 512))
                    cur = (s, w3)
                s, w3 = cur
                bk = mmbank()
                MM([(ps[bk][:, :], w3[:, c, sub * 128:(sub + 1) * 128], hT[:, c, tok(tg)], c == 0, c == 7)
                    for c in range(8)], [wkey(s), ("h", tg)], [("ps", bk)])
                ACT(uT[:, idx, :], ps[bk][:, :], AF.Gelu, [], [("ps", bk), ("u", idx)])
                if idx >= 1:
                    spatial_unit(idx - 1)
            spatial_unit(15)

        def w_out(tg):
            for oc in range(4):
                s, w3 = wload(s_wout[j, :, oc * 256:(oc + 1) * 256].rearrange("(k p) n -> p k n", p=128), (16, 256))
                for dcl in range(2):
                    dc = oc * 2 + dcl
                    bk = mmbank()
                    MM([(ps[bk][:, :], w3[:, kc, dcl * 128:(dcl + 1) * 128], uT[:, kc, :], kc == 0, kc == 15)
                        for kc in range(16)], [wkey(s)] + [("u", kc) for kc in range(16)], [("ps", bk)])
                    TT(xT[:, dc, tok(tg)], ps[bk][:, :], xT[:, dc, tok(tg)], ALU.add, [], [("ps", bk), xk(dc, tg)])

        v_path(0)
        norm(1)
        ln_stats(0)
        u_spatial(0)
        for tg in range(1, NTT):
            v_path(tg)
            if tg + 1 < NTT:
                norm(tg + 1)
            ln_stats(tg)
            w_out(tg - 1)
            phase_norm(pi + 1, tg - 1)
            u_spatial(tg)
        w_out(NTT - 1)
        phase_norm(pi + 1, NTT - 1)
        if next_prefetch is not None:
            next_prefetch()
        p.fence()

    def mla_prefetch(j, st8):
        wdkv = p.sb("wdkv%d" % j, [128, 8, 512], BF16, off=WB)
        wuq = p.sb("wuq%d" % j, [128, 2, 2048], BF16, off=WB + 8192)
        p.dma("pool", "wr0", wdkv[:, :, :], w_dkv[j, :, :].rearrange("(c p) n -> p c n", p=128), writes=[wkey(0)])
        p.dma("pool", "wr1", wuq[:, :, :], w_uq[j, :, :].rearrange("(c p) n -> p c n", p=128), writes=[wkey(1)])
        st8["wdkv"], st8["wuq"] = wdkv, wuq

    def mla_layer(j, l, st8, next_prefetch=None, pi=0):
        wdkv, wuq = st8["wdkv"], st8["wuq"]
        wukv = p.sb("wukv%d" % j, [128, 2048], BF16, off=WB + 16384)
        wo_sl = p.sb("wo%d" % j, [128, 2, 1024], BF16, off=WB + 20480)
        OT = p.sb("OT%d" % j, [128, 2, NLOC], BF16, off=WB + 24576)
        KT = p.sb("KT%d" % j, [128, 2, SEQ], BF16, off=AR)
        V = p.sb("V%d" % j, [128, 32, 256], BF16, off=AR + 16384)
        o = PB
        cqn = p.sb("cqn%d" % j, [128, 2, NLOC], BF16, off=o); o += 8192
        QN = p.sb("QN%d" % j, [128, 2, NLOC], BF16, off=o)
        ckvl = p.sb("ckvl%d" % j, [128, NLOC], BF16, off=o)
        krl = p.sb("krl%d" % j, [64, NLOC], BF16, off=o + 4096); o += 8192
        ckva = p.sb("ckva%d" % j, [128, SEQ], BF16, off=o); o += 8192
        kra = p.sb("kra%d" % j, [128, SEQ], BF16, off=o); o += 8192
        QR = p.sb("QR%d" % j, [128, 2, NLOC], BF16, off=o); o += 8192
        PT_OFF = o
        PT = [p.sb("PT%d_%d" % (j, b), [128, 512], BF16, off=o + 1024 * b) for b in range(4)]; o += 4096
        cqf = p.sb("cqf%d" % j, [128, 2, 512], F32, off=o); o += 4096
        ckf = p.sb("ckf%d" % j, [128, 1, 512], F32, off=o); o += 2048
        t1 = p.sb("t1_%d" % j, [64, 512], F32, off=o); o += 2048
        t2 = p.sb("t2_%d" % j, [64, 512], F32, off=o); o += 2048
        acc = [p.sb("acc%d_0" % j, [128, 512], F32, off=o - 4096), p.sb("acc%d_1" % j, [128, 512], F32, off=o - 2048)]
        REC_OFF = o
        rec = [p.sb("rec%d_%d" % (j, b), [128, 512], F32, off=o + 2048 * b) for b in range(2)]; o += 4096
        assert o <= SB_END, o

        p.dma("pool", "mw2", wukv[:, :], w_ukv[j, :, :], writes=["wukv"] + [("xin", b) for b in range(4)])
        p.op("dve", lambda e: e.memset(kra[64:128, :], 0.0), writes=["kraz"])
        p.op("dve", lambda e: e.memset(QR[64:128, :, :], 0.0), writes=["QRz"])

        def norm(tt):
            phase_norm(pi, tt)
        norm(0)
        cqf2 = [cqf, p.sb("cqfb%d" % j, [128, 2, 512], F32, off=PT_OFF)]
        ckf2 = [ckf, p.sb("ckfb%d" % j, [128, 1, 512], F32, off=REC_OFF)]

        def lat_a(tt):
            bsel = tt % 2
            for fch in range(3):
                bk = mmbank()
                MM([(ps[bk][:, :], wdkv[:, c, fch * 128:(fch + 1) * 128], hT[:, c, tok(tt)], c == 0, c == 7)
                    for c in range(8)], [wkey(0), ("h", tt)], [("ps", bk)])
                if fch < 2:
                    EVAC(cqf2[bsel][:, fch, :], ps[bk][:, :], [], [("ps", bk), ("cqf", bsel, fch)])
                else:
                    EVAC(ckf2[bsel][:, 0, :], ps[bk][:, :], [], [("ps", bk), ("ckf", bsel)])
            ba = mmbank()
            MM([(ps[ba][0:64, :], wdkv[:, c, 384:448], hT[:, c, tok(tt)], c == 0, c == 7) for c in range(8)],
               [wkey(0), ("h", tt)], [("ps", ba)])
            bb = mmbank()
            MM([(ps[bb][0:64, :], wdkv[:, c, 448:512], hT[:, c, tok(tt)], c == 0, c == 7) for c in range(8)],
               [wkey(0), ("h", tt)], [("ps", bb)])
            TT(t1[:, :], ps[ba][0:64, :], cos2[:, tok(tt)], ALU.mult, ["cos2"], [("ps", ba), "t1"])
            TT(t2[:, :], ps[bb][0:64, :], sin2s[:, tok(tt)], ALU.mult, ["sin2s"], [("ps", bb), "t2"])
            TT(krl[:, tok(tt)], t1[:, :], t2[:, :], ALU.add, ["t1", "t2"], [("krl", tt)])

        def lat_b(tt):
            bsel = tt % 2
            rmsnorm(cqf2[bsel][:, :, :], 2, 256, 72 + 2 * j, [cqn[:, c, tok(tt)] for c in range(2)], epsn,
                    [("cqf", bsel, 0), ("cqf", bsel, 1)], [("cqn", tt)])
            rmsnorm(ckf2[bsel][:, :, :], 1, 128, 76 + j, [ckvl[:, tok(tt)]], epsn, [("ckf", bsel)], [("ckvl", tt)])

        for tt in range(NTT):
            if tt + 1 < NTT:
                norm(tt + 1)
            lat_a(tt)
            if tt >= 1:
                lat_b(tt - 1)
        lat_b(NTT - 1)
        lk = [("ckvl", tt) for tt in range(NTT)]
        rk = [("krl", tt) for tt in range(NTT)]
        p.dma("sp", "cci%d" % j, cc_in[j].ap()[0:128, :], ckvl[:, :], reads=lk, writes=["ccin_a"])
        p.dma("sp", "cci%d" % j, cc_in[j].ap()[128:192, :], krl[:, :], reads=rk, writes=["ccin_b"])

        def ccfn(e, j=j):
            return e.collective_compute("AllGather", ALU.bypass, replica_groups=PAIRS,
                                        ins=[cc_in[j].ap().opt()], outs=[cc_out[j].ap().opt()])
        p.custom("pool", "cc%d" % j, 1, ccfn, reads=["ccin_a", "ccin_b"], writes=["ccout"])
        for r in range(2):
            p.dma("sp", "ccl%d" % j, ckva[:, :].rearrange("p (i r t) -> p i r t", r=2, t=128)[:, :, r, :],
                  cc_out[j].ap()[r * 192:r * 192 + 128, :].rearrange("p (i t) -> p i t", t=128),
                  reads=["ccout"], writes=[("ckva", r)])
            p.dma("sp", "ccl%d" % j, kra[0:64, :].rearrange("p (i r t) -> p i r t", r=2, t=128)[:, :, r, :],
                  cc_out[j].ap()[r * 192 + 128:r * 192 + 192, :].rearrange("p (i t) -> p i t", t=128),
                  reads=["ccout"], writes=[("kra", r)])
        p.fence()
        for hp in range(4):
            for hl in range(2):
                h = 2 * hp + hl
                for kt in range(8):
                    bk = mmbank((0, 1, 2, 7))
                    MM([(ps[bk][:, :], wukv[:, h * 128:(h + 1) * 128], ckva[:, kt * 512:(kt + 1) * 512], True, True)],
                       ["wukv", ("ckva", 0), ("ckva", 1)], [("ps", bk)])
                    EVAC(KT[:, hl, kt * 512:(kt + 1) * 512], ps[bk][:, :], [], [("ps", bk), ("KT", hl)])
            for kp in range(16):
                bk = mmbank((0, 1, 2, 7))
                MM([(ps[bk][:, q * 256:(q + 1) * 256], ckva[:, (2 * kp + q) * 128:(2 * kp + q + 1) * 128],
                     wukv[:, 1024 + hp * 256:1024 + (hp + 1) * 256], True, True) for q in range(2)],
                   ["wukv", ("ckva", 0), ("ckva", 1)], [("ps", bk)])
                EVAC(V[:, 2 * kp:2 * kp + 2, :], ps[bk][:, :].rearrange("p (q d) -> p q d", q=2), [], [("ps", bk), "V"])
            p.dma("pool", "mw3", wo_sl[:, :, :],
                  w_o[j, hp * 256:(hp + 1) * 256, :].rearrange("(h p) n -> p h n", p=128), writes=["wo"])
            for tt in range(NTT):
                for hl in range(2):
                    h = 2 * hp + hl
                    bk = mmbank((0, 1, 2, 7))
                    MM([(ps[bk][:, :], wuq[:, kc, h * 128:(h + 1) * 128], cqn[:, kc, tok(tt)], kc == 0, kc == 1)
                        for kc in range(2)], [wkey(1), ("cqn", tt)], [("ps", bk)])
                    ACT(QN[:, hl, tok(tt)], ps[bk][:, :], AF.Copy, [], [("ps", bk), ("QN", hl)], scale=SCALE)
                    ba = mmbank((0, 1, 2, 7))
                    MM([(ps[ba][0:64, :], wuq[:, kc, 1024 + h * 64:1024 + (h + 1) * 64], cqn[:, kc, tok(tt)], kc == 0, kc == 1)
                        for kc in range(2)], [wkey(1), ("cqn", tt)], [("ps", ba)])
                    bb = mmbank((0, 1, 2, 7))
                    MM([(ps[bb][0:64, :], wuq[:, kc, 1536 + h * 64:1536 + (h + 1) * 64], cqn[:, kc, tok(tt)], kc == 0, kc == 1)
                        for kc in range(2)], [wkey(1), ("cqn", tt)], [("ps", bb)])
                    STT(t1[:, :], ps[ba][0:64, :], SCALE, cos2[:, tok(tt)], ALU.mult, ALU.mult, ["cos2"], [("ps", ba), "t1"])
                    STT(t2[:, :], ps[bb][0:64, :], SCALE, sin2s[:, tok(tt)], ALU.mult, ALU.mult, ["sin2s"], [("ps", bb), "t2"])
                    TT(QR[0:64, hl, tok(tt)], t1[:, :], t2[:, :], ALU.add, ["t1", "t2"], [("QR", hl)])
            if hp == 3 and next_prefetch is not None:
                next_prefetch()
            units = []
            for qg in range(4):
                i0 = 4 * qg
                nkt = 8 * qg + 8
                for kt in range(nkt):
                    imin = max(i0, kt // 2)
                    q0 = imin * 128
                    n = (i0 + 4) * 128 - q0
                    off = q0 - i0 * 128
                    midx = (kt % 2) if kt >= 2 * i0 else None
                    for hl in range(2):
                        units.append((qg, kt, hl, q0, n, off, midx, kt == 0, kt == nkt - 1))
            LOOK = 2

            def emit_pv(idx):
                (qg, kt, hl, q0, n, off, midx, first, last) = units[idx]
                pb = idx % 4
                ob, sbn = 3 + hl, 5 + hl
                if SUM_MODE == "mm":
                    MM([(ps[ob][:, off:off + n], V[:, kt, hl * 128:(hl + 1) * 128], PT[pb][:, off:off + n], first, last),
                        (ps[sbn][:, off:off + n], ones[:, :], PT[pb][:, off:off + n], first, last)],
                       ["V", ("pt", pb), "ones"], [("ps", ob), ("ps", sbn)])
                else:
                    MM([(ps[ob][:, off:off + n], V[:, kt, hl * 128:(hl + 1) * 128], PT[pb][:, off:off + n], first, last)],
                       ["V", ("pt", pb)], [("ps", ob)])
                    eng = "pool" if (hl == 0 and SUM_MODE == "pooldve") else "dve"
                    akey = "t1" if hl == 0 else "t2"
                    if SUM_MODE == "dvepsum":
                        if first:
                            p.op("dve", lambda e, a=ps[sbn], pt=PT[pb]: e.tensor_copy(a[:, :], pt[:, :]),
                                 [("pt", pb)], [("ps", sbn)])
                        else:
                            p.op("dve", lambda e, a=ps[sbn][:, off:off + n], pt=PT[pb][:, off:off + n]: e.tensor_tensor(a, a, pt, ALU.add),
                                 [("pt", pb)], [("ps", sbn)])
                    elif first:
                        p.op(eng, lambda e, a=acc[hl], pt=PT[pb]: e.tensor_copy(a[:, :], pt[:, :]), [("pt", pb)], [akey])
                    else:
                        p.op(eng, lambda e, a=acc[hl][:, off:off + n], pt=PT[pb][:, off:off + n]: e.tensor_tensor(a, a, pt, ALU.add),
                             [("pt", pb)], [akey])
                if last:
                    fb = sbn
                    if SUM_MODE == "dvepsum":
                        ACT(acc[hl][:, :], ps[sbn][:, :], AF.Copy, [], [("ps", sbn), akey])
                        MM([(ps[7][:, :], ones_f[:, :], acc[hl][:, :], True, True)], [akey, "ones"], [("ps", 7)])
                        fb = 7
                    elif SUM_MODE != "mm":
                        MM([(ps[sbn][:, :], ones_f[:, :], acc[hl][:, :], True, True)], [akey, "ones"], [("ps", sbn)])
                    ACT(rec[hl][:, :], ps[fb][:, :], AF.Ln, [], [("ps", fb), ("rec", hl)])
                    ACT(rec[hl][:, :], rec[hl][:, :], AF.Exp, [], [("rec", hl)], scale=-1.0)
                    TT(OT[:, hl, tok(qg)], ps[ob][:, :], rec[hl][:, :], ALU.mult, [("rec", hl)], [("ps", ob), ("OT", qg)])

            for idx, (qg, kt, hl, q0, n, off, midx, first, last) in enumerate(units):
                sbk = (0, 1, 2)[idx % 3]
                pb = idx % 4
                ks = slice(kt * 128, (kt + 1) * 128)
                oo = ps[sbk][:, off:off + n]
                MM([(oo, KT[:, hl, ks], QN[:, hl, q0:q0 + n], True, False),
                    (oo, kra[:, ks], QR[:, hl, q0:q0 + n], False, True)],
                   [("KT", hl), ("QN", hl), ("QR", hl), ("kra", 0), ("kra", 1), "kraz", "QRz"], [("ps", sbk)])
                ACT(PT[pb][:, off:off + n], oo, AF.Exp, [], [("ps", sbk), ("pt", pb)])
                if midx is not None:
                    TT(PT[pb][:, off:off + 128], PT[pb][:, off:off + 128], masks[:, midx, 0:128], ALU.mult,
                       ["c"], [("pt", pb)])
                if idx >= LOOK:
                    emit_pv(idx - LOOK)
            for idx in range(len(units) - LOOK, len(units)):
                emit_pv(idx)
            for tt in range(NTT):
                for dc in range(8):
                    bk = mmbank((0, 1, 2, 7))
                    MM([(ps[bk][:, :], wo_sl[:, hl, dc * 128:(dc + 1) * 128], OT[:, hl, tok(tt)], hl == 0, hl == 1)
                        for hl in range(2)], ["wo", ("OT", tt)], [("ps", bk)])
                    TT(xT[:, dc, tok(tt)], ps[bk][:, :], xT[:, dc, tok(tt)], ALU.add, [], [("ps", bk), xk(dc, tt)])
                if hp == 3:
                    phase_norm(pi + 1, tt, extra_w=[("KT", 0), ("KT", 1), "V"])
        p.fence()

    sts = [dict() for _ in phases]

    def make_prefetch(i):
        if i >= len(phases):
            return None
        kind, l = phases[i]
        if kind == "ffn":
            return lambda: ffn_prefetch(l)
        if kind == "sgu":
            return lambda: sgu_prefetch(l // 2, sts[i])
        return lambda: mla_prefetch(l // 2, sts[i])

    if phases:
        make_prefetch(0)()
    load_x()
    if not phases or phases[0][0] != "mla":
        p.fence()
    for i, (kind, l) in enumerate(phases):
        nxt = make_prefetch(i + 1)
        if kind == "ffn":
            ffn_layer(l, nxt, i)
        elif kind == "sgu":
            sgu_layer(l // 2, l, sts[i], nxt, i)
        else:
            mla_layer(l // 2, l, sts[i], nxt, i)

    for tt in range(NTT):
        phase_norm(len(phases), tt)
    p.final_wait("sp", okeys)
    p.emit()
    return nc, p


_CACHE = {}


def _prep_inputs(x, positions, norm_mix, norm_ffn, final_norm,
                 mla_w_dkv, mla_q_norm, mla_kv_norm, mla_w_uq, mla_w_ukv, mla_w_o,
                 sgu_w_in, sgu_ln_g, sgu_ln_b, sgu_w_spatial, sgu_b_spatial, sgu_w_out,
                 ffn_w_up, ffn_w_down):
    f32 = np.float32
    A = lambda a: np.ascontiguousarray(np.asarray(a))
    x = A(x); positions = A(positions)
    gains = np.zeros((128, 112), f32)
    gains[:, 0:32] = A(norm_mix).reshape(4, 8, 128).transpose(2, 0, 1).reshape(128, 32)
    gains[:, 32:64] = A(norm_ffn).reshape(4, 8, 128).transpose(2, 0, 1).reshape(128, 32)
    gains[:, 64:72] = A(final_norm).reshape(8, 128).T
    gains[:, 72:76] = A(mla_q_norm).reshape(2, 2, 128).transpose(2, 0, 1).reshape(128, 4)
    gains[:, 76:78] = A(mla_kv_norm).reshape(2, 128).T
    gains[:, 80:112] = A(sgu_ln_g).reshape(2, 16, 128).transpose(2, 0, 1).reshape(128, 32)
    invf = (10000.0 ** (-np.arange(0, 64, 2, dtype=f32) / 64.0)).astype(f32)
    ropec = np.zeros((64, 2), f32)
    ropec[:, 0] = np.concatenate([invf, invf])
    ropec[:, 1] = np.concatenate([-np.ones(32, f32), np.ones(32, f32)])
    ident = np.eye(128, dtype=f32)
    maskT = (np.arange(128)[:, None] <= np.arange(128)[None, :]).astype(ml_dtypes.bfloat16)
    wd = A(mla_w_dkv)
    kr = wd[:, :, 384:448]
    w_dkv = A(np.concatenate([wd, kr[:, :, 32:64], kr[:, :, 0:32]], axis=2))
    wq = A(mla_w_uq).reshape(2, 256, 8, 192)
    qn = wq[:, :, :, 0:128].reshape(2, 256, 1024)
    qr = wq[:, :, :, 128:192]
    qsw = np.concatenate([qr[..., 32:64], qr[..., 0:32]], axis=-1)
    w_uq = A(np.concatenate([qn, qr.reshape(2, 256, 512), qsw.reshape(2, 256, 512)], axis=2))
    wkv = A(mla_w_ukv).reshape(2, 128, 8, 256)
    w_ukv = A(np.concatenate([wkv[:, :, :, 0:128].reshape(2, 128, 1024), wkv[:, :, :, 128:256].reshape(2, 128, 1024)], axis=2))
    s_l2 = A(np.stack([A(sgu_ln_b), np.ones((2, 2048), f32)], axis=1))
    s_b = A(sgu_b_spatial)
    s_wsT = A(A(sgu_w_spatial).transpose(0, 3, 1, 2).reshape(2, 128, 1024))
    shared = dict(maskT=maskT, gains=gains, ropec=ropec, ident=ident, w_dkv=w_dkv, w_uq=w_uq, w_ukv=w_ukv,
                  w_o=A(mla_w_o), s_win=A(sgu_w_in), s_wout=A(sgu_w_out), s_l2=s_l2, s_b=s_b,
                  s_wsT=s_wsT, f_up=A(ffn_w_up), f_dn=A(ffn_w_down))
    tri = (np.arange(128)[:, None] <= np.arange(128)[None, :])
    in_maps = []
    for c in range(8):
        b, r = c // 2, c % 2
        xs = A(x[b].reshape(16, 2, 128, D_MODEL)[:, r].reshape(NLOC, D_MODEL))
        ps_ = positions[b].reshape(16, 2, 128)[:, r].reshape(1, NLOC)
        pos = A(np.broadcast_to(ps_, (64, NLOC))).astype(np.int32)
        m = np.zeros((128, 2, 256), ml_dtypes.bfloat16)
        if r == 0:
            m[:, 0, 0:128] = tri; m[:, 0, 128:256] = tri
        else:
            m[:, 0, :] = 1
            m[:, 1, 0:128] = tri; m[:, 1, 128:256] = tri
        d = dict(shared)
        d.update(xs=xs, pos=pos, masks=m)
        in_maps.append(d)
    return in_maps


def kernel(**inputs):
    if "nc" not in _CACHE:
        _CACHE["nc"] = build_program(4)[0]
    nc = _CACHE["nc"]
    in_maps = _prep_inputs(**inputs)
    res = run_bass_kernel_spmd(nc, in_maps, core_ids=list(range(8)))
    out = np.zeros((4, SEQ, D_MODEL), np.float32)
    ov = out.reshape(4, 16, 2, 128, D_MODEL)
    for c in range(8):
        b, r = c // 2, c % 2
        ov[b, :, r] = np.asarray(res.results[c]["y"]).reshape(16, 128, D_MODEL)
    return out
```

```python
import contextlib
import numpy as np
import ml_dtypes
import concourse.bass as bass
import concourse.mybir as mybir
from concourse.bass_utils import run_bass_kernel_spmd

F32 = mybir.dt.float32
BF16 = mybir.dt.bfloat16
I32 = mybir.dt.int32
AF = mybir.ActivationFunctionType
ALU = mybir.AluOpType
AX = mybir.AxisListType

SB_BASE = 16512
SB_END = 229376

D_MODEL = 1024
SEQ = 4096
NLOC = 2048
NTT = 4
NORM_EPS = 1e-6
LN_EPS = 1e-5
SCALE = 192.0 ** -0.5
PAIRS = [[0, 1], [2, 3], [4, 5], [6, 7]]
SUM_MODE = "dve"


class Prog:
    ENGS = ("pe", "act", "dve", "pool", "sp")

    def __init__(self, nc):
        self.nc = nc
        self.ops = {e: [] for e in self.ENGS}
        self.cnt = {e: 0 for e in self.ENGS}
        self.waited = {e: {} for e in self.ENGS}
        self.last_w = {}
        self.readers = {}
        self.dcnt = {}
        self.sems = {}
        self.sb_off = SB_BASE

    def sb(self, name, shape, dt, off=None):
        size = int(np.prod(shape[1:])) * (4 if dt in (F32, I32) else 2)
        size = (size + 31) // 32 * 32
        if off is None:
            off = self.sb_off
            self.sb_off += size
            assert self.sb_off <= SB_END, (name, self.sb_off)
        else:
            assert off + size <= SB_END, (name, off, size)
        return self.nc.alloc_sbuf_tensor_at(name, list(shape), dt, offset=off)

    def _deps(self, eng, reads, writes):
        deps = []
        for r in reads:
            if r in self.last_w:
                deps.append(self.last_w[r])
        for w in writes:
            if w in self.last_w:
                deps.append(self.last_w[w])
            deps.extend(self.readers.get(w, ()))
        best = {}
        for (sk, val, deng) in deps:
            if deng == "pe" and eng == "pe":
                continue
            if self.waited[eng].get(sk, 0) >= val:
                continue
            best[sk] = max(best.get(sk, 0), val)
        for sk, val in best.items():
            self.waited[eng][sk] = val
        return list(best.items())

    def _commit(self, token, reads, writes):
        for r in reads:
            self.readers.setdefault(r, []).append(token)
        for w in writes:
            self.last_w[w] = token
            self.readers[w] = []

    def op(self, eng, fn, reads=(), writes=()):
        reads = list(reads)
        writes = list(writes)
        waits = self._deps(eng, reads, writes)
        self.cnt[eng] += 1
        self.ops[eng].append((waits, fn, (eng, 1)))
        self._commit((eng, self.cnt[eng], eng), reads, writes)

    def dma(self, q, semname, out, in_, reads=(), writes=()):
        def fn(e, out=out, in_=in_):
            return e.dma_start(out=out, in_=in_)
        self.custom(q, semname, 16, fn, reads, writes)

    def custom(self, q, semname, inc, fn, reads=(), writes=()):
        reads = list(reads)
        writes = list(writes)
        waits = self._deps(q, reads, writes)
        sk = "d_" + semname
        self.dcnt[sk] = self.dcnt.get(sk, 0) + inc
        self.ops[q].append((waits, fn, (sk, inc)))
        self._commit((sk, self.dcnt[sk], "dma"), reads, writes)

    def fence(self):
        for e in self.ENGS:
            waits = []
            for f in ("pe", "act", "dve", "pool"):
                if f == e or self.cnt[f] == 0:
                    continue
                if self.waited[e].get(f, 0) >= self.cnt[f]:
                    continue
                self.waited[e][f] = self.cnt[f]
                waits.append((f, self.cnt[f]))
            for sk, val in self.dcnt.items():
                if self.waited[e].get(sk, 0) >= val:
                    continue
                self.waited[e][sk] = val
                waits.append((sk, val))
            if waits:
                self.ops[e].append((waits, None, None))

    def final_wait(self, q, keys):
        waits = self._deps(q, keys, [])
        self.ops[q].append((waits, None, None))

    def emit(self):
        nc = self.nc
        names = set()
        for e in self.ENGS:
            for waits, fn, inc in self.ops[e]:
                for sk, _ in waits:
                    names.add(sk)
                if inc is not None:
                    names.add(inc[0])
        with contextlib.ExitStack() as st:
            for n in sorted(names):
                self.sems[n] = st.enter_context(nc.semaphore("s_" + n))
            block = st.enter_context(nc.Block())

            def run(e, h):
                for waits, fn, inc in self.ops[e]:
                    for sk, val in waits:
                        h.wait_ge(self.sems[sk], val)
                    if fn is None:
                        continue
                    ins = fn(h)
                    if inc is not None:
                        ins.then_inc(self.sems[inc[0]], inc[1])

            @block.tensor
            def _(h):
                run("pe", h)

            @block.scalar
            def _(h):
                run("act", h)

            @block.vector
            def _(h):
                run("dve", h)

            @block.gpsimd
            def _(h):
                run("pool", h)

            @block.sync
            def _(h):
                run("sp", h)


def build_program(nlayers=4):
    nc = bass.Bass("TRN2", target_bir_lowering=False)

    def din(name, shape, dt):
        return nc.dram_tensor(name, list(shape), dt, kind="ExternalInput").ap()

    xs = din("xs", [NLOC, D_MODEL], F32)
    pos = din("pos", [64, NLOC], I32)
    masks_d = din("masks", [128, 2, 256], BF16)
    maskT_d = din("maskT", [128, 128], BF16)
    gains_d = din("gains", [128, 112], F32)
    rc_d = din("ropec", [64, 2], F32)
    ident_d = din("ident", [128, 128], F32)
    w_dkv = din("w_dkv", [2, 1024, 512], F32)
    w_uq = din("w_uq", [2, 256, 2048], F32)
    w_ukv = din("w_ukv", [2, 128, 2048], F32)
    w_o = din("w_o", [2, 1024, 1024], F32)
    s_win = din("s_win", [2, 1024, 4096], F32)
    s_wout = din("s_wout", [2, 2048, 1024], F32)
    s_l2 = din("s_l2", [2, 2, 2048], F32)
    s_b = din("s_b", [2, 8, 128], F32)
    s_wsT = din("s_wsT", [2, 128, 1024], F32)
    f_up = din("f_up", [4, 1024, 4096], F32)
    f_dn = din("f_dn", [4, 4096, 1024], F32)
    y = nc.dram_tensor("y", [NLOC, D_MODEL], F32, kind="ExternalOutput").ap()
    cc_in = [nc.dram_tensor("cc_in%d" % j, [192, NLOC], BF16) for j in range(2)]
    cc_out = [nc.dram_tensor("cc_out%d" % j, [384, NLOC], BF16) for j in range(2)]

    p = Prog(nc)
    xT = p.sb("xT", [128, 8, NLOC], F32)
    cos2 = p.sb("cos2", [64, NLOC], BF16)
    sin2s = p.sb("sin2s", [64, NLOC], BF16)
    ident = p.sb("ident", [128, 128], F32)
    ones = p.sb("ones", [128, 128], BF16)
    masks = p.sb("masksb", [128, 2, 256], BF16)
    maskT = p.sb("maskTb", [128, 128], BF16)
    gains = p.sb("gainsb", [128, 112], F32)
    ones_f = p.sb("ones_f", [128, 128], F32)
    ropec = p.sb("ropecb", [64, 2], F32)
    epsn = p.sb("epsn", [128, 1], F32)
    epsl = p.sb("epsl", [128, 1], F32)
    AR = p.sb_off
    hT = p.sb("hT", [128, 8, NLOC], BF16, off=AR)
    sq = p.sb("sq", [128, 8, 512], BF16, off=AR + 32768)
    rs = p.sb("rs", [128, 512], F32, off=AR + 40960)
    WB = AR + 43008
    PB = WB + 32768
    assert SB_END - PB >= 59000, SB_END - PB
    ps = [nc.alloc_psum_tensor("ps%d" % b, [128, 512], F32) for b in range(8)]

    state = {"mm": 0, "st": 0}

    def mmbank(pool=(0, 1, 2, 3, 4, 5)):
        b = pool[state["mm"] % len(pool)]
        state["mm"] += 1
        return b

    def stbank():
        b = (6, 7)[state["st"] % 2]
        state["st"] += 1
        return b

    def tok(tt):
        return slice(tt * 512, (tt + 1) * 512)

    def MM(mms, reads, writes):
        def fn(e, mms=mms):
            ins = None
            for (o, l, r, s0, s1) in mms:
                ins = e.matmul(o, l, r, start=s0, stop=s1, skip_group_check=True)
            return ins
        p.op("pe", fn, reads, writes)

    def ACT(out, in_, func, reads, writes, **kw):
        p.op("act", lambda e: e.activation(out, in_, func, **kw), reads, writes)

    def TT(out, a, b, op, reads, writes):
        p.op("dve", lambda e: e.tensor_tensor(out, a, b, op), reads, writes)

    def TS(out, a, s1, s2, op0, op1, reads, writes):
        if op1 is None:
            p.op("dve", lambda e: e.tensor_scalar(out, a, s1, None, op0), reads, writes)
        else:
            p.op("dve", lambda e: e.tensor_scalar(out, a, s1, s2, op0, op1), reads, writes)

    def STT(out, a, s, b, op0, op1, reads, writes):
        p.op("dve", lambda e: e.scalar_tensor_tensor(out=out, in0=a, scalar=s, in1=b, op0=op0, op1=op1), reads, writes)

    def DCOPY(out, in_, reads, writes):
        p.op("dve", lambda e: e.tensor_copy(out, in_), reads, writes)

    def RECIP(out, in_, reads, writes):
        p.op("dve", lambda e: e.reciprocal(out, in_), reads, writes)

    evs = {"n": 0}

    def EVAC(out, in_, reads, writes, scale=None):
        evs["n"] += 1
        if evs["n"] % 2 == 0 and scale is None:
            DCOPY(out, in_, reads, writes)
        else:
            if scale is None:
                ACT(out, in_, AF.Copy, reads, writes)
            else:
                ACT(out, in_, AF.Copy, reads, writes, scale=scale)

    def xk(c, tt):
        return ("x", c, tt)

    def xkeys(tt):
        return [xk(c, tt) for c in range(8)]

    def rmsnorm(src3, nch, D, gcol, dsts, eps_t, reads, writes, np_=128):
        P = slice(0, np_)
        reads = list(reads) + ["cos2", "sin2s"]
        ACT(sq[P, 0:nch, :], src3, AF.Square, reads, ["sq", ("tmpx", 0), ("tmpx", 1)])
        sb_ = stbank()
        MM([(ps[sb_][P, :], ones[P, 0:np_], sq[P, c, :], c == 0, c == nch - 1) for c in range(nch)],
           ["sq", "ones"], [("ps", sb_)])
        ACT(rs[P, :], ps[sb_][P, :], AF.Ln, ["eps"], [("ps", sb_), "rs"], bias=eps_t[P, 0:1], scale=1.0 / D)
        ACT(rs[P, :], rs[P, :], AF.Exp, [], ["rs"], scale=-0.5)
        for c in range(nch):
            STT(dsts[c], src3[:, c, :], gains[P, gcol + c:gcol + c + 1], rs[P, :], ALU.mult, ALU.mult,
                reads + ["rs", "c"], [writes[c]] if isinstance(writes, list) and len(writes) == nch else writes)

    for (dst, src) in ((ident[:, :], ident_d), (masks[:, :, :], masks_d), (maskT[:, :], maskT_d),
                       (gains[:, :], gains_d), (ropec[:, :], rc_d)):
        p.dma("sp", "c", dst, src, writes=["c"])
    p.op("dve", lambda e: e.memset(ones[:, :], 1.0), writes=["ones"])
    p.op("dve", lambda e: e.memset(ones_f[:, :], 1.0), writes=["ones"])
    p.op("dve", lambda e: e.memset(epsn[:, :], NORM_EPS), writes=["eps"])
    p.op("dve", lambda e: e.memset(epsl[:, :], LN_EPS), writes=["eps"])

    pos_i = p.sb("pos_i", [64, NLOC], I32, off=AR)
    pos_f = p.sb("pos_f", [64, NLOC], F32, off=AR + 8192)
    ang = p.sb("ang", [64, NLOC], F32, off=AR + 16384)
    ang2 = p.sb("ang2", [64, NLOC], F32, off=AR + 24576)
    TWO_PI = 2.0 * float(np.pi)
    p.dma("sp", "pos", pos_i[:, :], pos, writes=["pos_i"])
    DCOPY(pos_f[:, :], pos_i[:, :], ["pos_i"], ["pos_f"])
    TS(ang[:, :], pos_f[:, :], ropec[:, 0:1], None, ALU.mult, None, ["pos_f", "c"], ["ang"])

    def reduce_and_sin(dst, shift, key):
        TS(ang2[:, :], ang[:, :], shift, None, ALU.add, None, ["ang"], ["ang2"])
        TS(pos_f[:, :], ang2[:, :], 1.0 / TWO_PI, None, ALU.mult, None, ["ang2"], ["pos_f"])
        DCOPY(pos_i[:, :], pos_f[:, :], ["pos_f"], ["pos_i"])
        DCOPY(pos_f[:, :], pos_i[:, :], ["pos_i"], ["pos_f"])
        STT(ang2[:, :], pos_f[:, :], -TWO_PI, ang2[:, :], ALU.mult, ALU.add, ["pos_f", "ang2"], ["ang2"])
        TS(pos_f[:, :], ang2[:, :], float(np.pi), TWO_PI, ALU.is_gt, ALU.mult, ["ang2"], ["pos_f"])
        TT(ang2[:, :], ang2[:, :], pos_f[:, :], ALU.subtract, ["pos_f", "ang2"], ["ang2"])
        ACT(dst, ang2[:, :], AF.Sin, ["ang2"], [key])

    reduce_and_sin(cos2[:, :], 0.5 * float(np.pi), "cos2")
    sin_f = p.sb("sin_f", [64, NLOC], F32, off=AR + 32768 - 8192 + 8192)
    reduce_and_sin(sin_f[:, :], 0.0, "sin_f")
    TS(sin2s[:, :], sin_f[:, :], ropec[:, 1:2], None, ALU.mult, None, ["sin_f", "c"], ["sin2s"])

    xin = [p.sb("xin%d" % b, [128, 1024], F32, off=WB + 16384 + 4096 * b) for b in range(4)]

    def load_x():
        for tile in range(16):
            b = tile % 4
            tt = tile // 4
            p.dma("sp", "xin%d" % b, xin[b][:, :], xs[tile * 128:(tile + 1) * 128, :], writes=[("xin", b)])
            for half in range(2):
                bk = mmbank()
                p.op("pe", (lambda e, bk=bk, b=b, half=half: [e.transpose(ps[bk][:, k * 128:(k + 1) * 128],
                                                                          xin[b][:, (half * 4 + k) * 128:(half * 4 + k + 1) * 128],
                                                                          ident[:, :]) for k in range(4)][-1]),
                     [("xin", b), "c"], [("ps", bk)])
                EVAC(xT[:, half * 4:half * 4 + 4, tile * 128:(tile + 1) * 128],
                     ps[bk][:, :].rearrange("p (k t) -> p k t", k=4),
                     [], [("ps", bk)] + [xk(half * 4 + k, tt) for k in range(4)])

    phases = []
    for l_ in range(nlayers):
        phases.append(("mla" if l_ % 2 == 0 else "sgu", l_))
        phases.append(("ffn", l_))
    normed = set()
    okeys = []
    yT = p.sb("yT", [128, 8, 512], F32, off=PB + 16384)
    yo = [p.sb("yo%d" % b, [128, 1024], F32, off=WB + 4096 * b) for b in range(4)]

    def final_part(tt, extra_w):
        rmsnorm(xT[:, :, tok(tt)], 8, 1024, 64, [yT[:, c, :] for c in range(8)], epsn, xkeys(tt), ["yT"])
        for tl in range(4):
            tile = tt * 4 + tl
            b = tile % 4
            for half in range(2):
                bk = mmbank()
                p.op("pe", (lambda e, bk=bk, half=half, tl=tl: [e.transpose(ps[bk][:, k * 128:(k + 1) * 128],
                                                                            yT[:, half * 4 + k, tl * 128:(tl + 1) * 128],
                                                                            ident[:, :]) for k in range(4)][-1]),
                     ["yT", "c"], [("ps", bk)])
                EVAC(yo[b][:, half * 512:(half + 1) * 512], ps[bk][:, :], [], [("ps", bk), ("yo", b, half)] + list(extra_w))
            p.dma("sp", "yo%d" % b, y[tile * 128:(tile + 1) * 128, :], yo[b][:, :],
                  reads=[("yo", b, 0), ("yo", b, 1)], writes=[("yout", tile)])
            okeys.append(("yout", tile))

    def phase_norm(i, tt, extra_w=()):
        if (i, tt) in normed:
            return
        normed.add((i, tt))
        if i >= len(phases):
            final_part(tt, extra_w)
            return
        kind, l = phases[i]
        gcol = 32 + l * 8 if kind == "ffn" else l * 8
        rmsnorm(xT[:, :, tok(tt)], 8, 1024, gcol, [hT[:, c, tok(tt)] for c in range(8)], epsn,
                xkeys(tt), [("h", tt)] + list(extra_w))

    def wkey(r):
        return ("Wreg", r)

    def ffn_tensors(l):
        wu = [p.sb("wu%d_%d" % (l, s), [128, 8, 512], BF16, off=WB + 16384 * s) for s in range(2)]
        wd = [p.sb("wd%d_%d" % (l, s), [128, 4, 1024], BF16, off=WB + 8192 + 16384 * s) for s in range(2)]
        return wu, wd

    def ffn_load(l, hc, wu, wd, extra=()):
        s = hc % 2
        p.dma("pool", "wr%d" % (2 * s), wu[s][:, :, :],
              f_up[l, :, hc * 512:(hc + 1) * 512].rearrange("(c p) n -> p c n", p=128),
              writes=[wkey(2 * s)] + [k for k in extra if k[1] == 2 * s])
        p.dma("pool", "wr%d" % (2 * s + 1), wd[s][:, :, :],
              f_dn[l, hc * 512:(hc + 1) * 512, :].rearrange("(s p) n -> p s n", p=128),
              writes=[wkey(2 * s + 1)] + [k for k in extra if k[1] == 2 * s + 1])

    def ffn_prefetch(l):
        wu, wd = ffn_tensors(l)
        ffn_load(l, 0, wu, wd)

    def ffn_layer(l, next_prefetch=None, pi=0):
        hid = [p.sb("hid%d_%d" % (l, b), [128, 4, 512], BF16, off=PB + 4096 * b) for b in range(2)]
        rr = [p.sb("rr%d_%d" % (l, b), [128, 512], F32, off=PB + 8192 + 2048 * b) for b in range(2)]
        wu, wd = ffn_tensors(l)

        def norm(tt):
            phase_norm(pi, tt)
        norm(0)
        rcount = 0
        for hc in range(8):
            s = hc % 2
            if hc > 0:
                ffn_load(l, hc, wu, wd)
            if hc == 7 and next_prefetch is not None:
                next_prefetch()
            for tt in range(NTT):
                if hc == 0 and tt + 1 < NTT:
                    norm(tt + 1)
                hb = (hc * 4 + tt) % 2
                for sub in range(4):
                    bk = mmbank()
                    MM([(ps[bk][:, :], wu[s][:, c, sub * 128:(sub + 1) * 128], hT[:, c, tok(tt)], c == 0, c == 7)
                        for c in range(8)], [wkey(2 * s), ("h", tt)], [("ps", bk)])
                    rb = rcount % 2
                    rcount += 1
                    ACT(rr[rb][:, :], ps[bk][:, :], AF.Relu, [], [("ps", bk), ("rr", rb)])
                    TT(hid[hb][:, sub, :], rr[rb][:, :], rr[rb][:, :], ALU.mult, [("rr", rb)], [("hid", hb, sub)])
                for dc in range(8):
                    bk = mmbank()
                    MM([(ps[bk][:, :], wd[s][:, sub, dc * 128:(dc + 1) * 128], hid[hb][:, sub, :], sub == 0, sub == 3)
                        for sub in range(4)], [wkey(2 * s + 1)] + [("hid", hb, sub) for sub in range(4)], [("ps", bk)])
                    TT(xT[:, dc, tok(tt)], ps[bk][:, :], xT[:, dc, tok(tt)], ALU.add, [], [("ps", bk), xk(dc, tt)])
                if hc == 7:
                    phase_norm(pi + 1, tt, extra_w=[wkey(0), wkey(1)] if pi + 1 >= len(phases) else [])
        p.fence()

    def sgu_wload(j, wc, src_ap, a):
        s = wc["n"] % 4
        wc["n"] += 1
        wsl_s = p.sb("sw%d_%d_%d" % (j, s, wc["n"]), [128, 4096], BF16, off=WB + 8192 * s)
        dst = wsl_s[:, :].rearrange("p (a b) -> p a b", a=a)
        p.dma("pool", "wr%d" % s, dst, src_ap, writes=[wkey(s)])
        return s, dst

    def sgu_vsrc(j, vc):
        return s_win[j, :, 2048 + vc * 512:2048 + (vc + 1) * 512].rearrange("(c p) n -> p c n", p=128)

    def sgu_prefetch(j, st8):
        B2 = p.sb("B2_%d" % j, [128, 16, 128], F32, off=PB + 49152)
        WcT = p.sb("WcT%d" % j, [128, 8, 128], BF16, off=PB + 57344)
        L2 = p.sb("L2_%d" % j, [2, 2048], F32, off=PB + 16384)
        R2 = p.sb("R2_%d" % j, [2, 8, 128], F32, off=PB + 16384 + 8192)
        wtmp = p.sb("wtmp%d" % j, [128, 1024], F32, off=PB + 16384 + 12288)
        p.dma("sp", "sgc1", L2[:, :], s_l2[j, :, :], writes=["L2"])
        p.dma("sp", "sgc2", wtmp[:, :], s_wsT[j, :, :], writes=["wtmp"])
        p.dma("sp", "sgc3", R2[1:2, :, :], s_b[j:j + 1, :, :], writes=["R2b"])
        for g in range(8):
            TT(WcT[:, g, :], wtmp[:, g * 128:(g + 1) * 128], maskT[:, :], ALU.mult, ["wtmp", "c"], [("WcT", g)])
        for half in range(2):
            bk = mmbank()
            MM([(ps[bk][0:1, :], ones[:, 0:1], WcT[:, half * 4:half * 4 + 4, :].rearrange("p a b -> p (a b)"), True, True)],
               [("WcT", g) for g in range(8)] + ["ones"], [("ps", bk)])
            DCOPY(R2[0:1, half * 4:half * 4 + 4, :].rearrange("p a b -> p (a b)"), ps[bk][0:1, :], [], [("ps", bk), ("R2a", half)])
        for bq in range(4):
            bk = mmbank()
            MM([(ps[bk][:, k * 128:(k + 1) * 128], L2[0:2, (4 * bq + k) * 128:(4 * bq + k + 1) * 128],
                 R2[0:2, (4 * bq + k) // 2, :], True, True) for k in range(4)],
               ["L2", "R2b", ("R2a", 0), ("R2a", 1)], [("ps", bk)])
            DCOPY(B2[:, 4 * bq:4 * bq + 4, :], ps[bk][:, :].rearrange("p (k t) -> p k t", k=4), [], [("ps", bk), "B2"])
        st8["B2"], st8["WcT"] = B2, WcT
        st8["wc"] = {"n": 0}
        st8["pre"] = [sgu_wload(j, st8["wc"], sgu_vsrc(j, vc), 8) for vc in range(2)]

    def sgu_layer(j, l, st8, next_prefetch=None, pi=0):
        B2, WcT = st8["B2"], st8["WcT"]
        uT = p.sb("uT%d" % j, [128, 16, 512], BF16, off=PB)
        vraw = p.sb("vraw%d" % j, [128, 4, 2048], F32, off=PB + 16384)
        vlnv = p.sb("vlnv%d" % j, [128, 4, 4096], BF16, off=PB + 16384)
        st = p.sb("st%d" % j, [128, 64], F32, off=PB + 59392)
        tmpx = [p.sb("tmpx%d_%d" % (j, b), [128, 512], F32, off=AR + 32768 + 4096 + 2048 * b) for b in range(2)]
        GC = 80 + 16 * j

        def norm(tt):
            phase_norm(pi, tt)
        norm(0)
        wc = st8["wc"]
        pre = list(st8["pre"])

        def wload(src_ap, shape3):
            return sgu_wload(j, wc, src_ap, shape3[0])

        allv = [("vraw", tch, vc) for tch in range(4) for vc in range(4)]

        def v_path(tg):
            T0 = tg * 512
            for vc in range(4):
                if pre:
                    s, w3 = pre.pop(0)
                else:
                    s, w3 = wload(sgu_vsrc(j, vc), (8, 512))
                for tch in range(4):
                    bk = mmbank()
                    MM([(ps[bk][:, :], hT[:, c, T0 + tch * 128:T0 + (tch + 1) * 128], w3[:, c, :], c == 0, c == 7)
                        for c in range(8)], [wkey(s), ("h", tg)], [("ps", bk)])
                    ACT(vraw[:, tch, vc * 512:(vc + 1) * 512], ps[bk][:, :], AF.Gelu, [],
                        [("ps", bk), ("vraw", tch, vc), ("stp", tch, vc)],
                        accum_out=st[:, tch * 4 + vc:tch * 4 + vc + 1])

        def ln_stats(tg):
            allp = [("stp", tch, vc) for tch in range(4) for vc in range(4)]
            p.op("dve", lambda e: e.reduce_sum(st[:, 16:20], st[:, 0:16].rearrange("p (a b) -> p a b", a=4), axis=AX.X),
                 allp, ["ssum"])
            for tch in range(4):
                ACT(sq[:, 0:4, :].rearrange("p a b -> p (a b)"), vraw[:, tch, :], AF.Square,
                    [("vraw", tch, vc) for vc in range(4)], ["sq", ("ssq", tch)], accum_out=st[:, 20 + tch:21 + tch])
            TS(st[:, 24:28], st[:, 16:20], 1.0 / 2048, None, ALU.mult, None, ["ssum"], ["mean"])
            TT(st[:, 28:32], st[:, 24:28], st[:, 24:28], ALU.mult, ["mean"], ["msq"])
            STT(st[:, 32:36], st[:, 20:24], 1.0 / 2048, st[:, 28:32], ALU.mult, ALU.subtract,
                ["msq"] + [("ssq", t) for t in range(4)], ["var"])
            ACT(st[:, 36:40], st[:, 32:36], AF.Ln, ["var", "eps"], ["sd"], bias=epsl[:, 0:1], scale=1.0)
            ACT(st[:, 40:44], st[:, 36:40], AF.Exp, ["sd"], ["rstd"], scale=-0.5)
            STT(st[:, 44:48], st[:, 24:28], -1.0, st[:, 40:44], ALU.mult, ALU.mult, ["mean", "rstd"], ["nmr"])
            for tch in range(4):
                vk = [("vraw", tch, vc) for vc in range(4)]
                ACT(vlnv[:, tch, 0:2048], vraw[:, tch, :], AF.Identity, ["rstd", "nmr"], vk,
                    bias=st[:, 44 + tch:45 + tch], scale=st[:, 40 + tch:41 + tch])

        tcount = {"n": 0}

        def spatial_unit(dch):
            g = dch // 2
            bk = mmbank()
            MM([(ps[bk][:, tch * 128:(tch + 1) * 128], vlnv[:, tch, dch * 128:(dch + 1) * 128], WcT[:, g, :], True, True)
                for tch in range(4)], allv + [("WcT", g)], [("ps", bk)])
            tb = tcount["n"] % 2
            tcount["n"] += 1
            t3 = tmpx[tb][:, :].rearrange("p (k t) -> p k t", k=4)
            STT(t3, ps[bk][:, :].rearrange("p (k t) -> p k t", k=4), gains[:, GC + dch:GC + dch + 1],
                B2[:, dch, :].unsqueeze(1).broadcast_to([128, 4, 128]), ALU.mult, ALU.add,
                ["B2", "c"], [("ps", bk), ("tmpx", tb)])
            TT(uT[:, dch, :], tmpx[tb][:, :], uT[:, dch, :], ALU.mult, [("tmpx", tb)], [("u", dch)])

        def u_spatial(tg):
            for idx in range(16):
                uc, sub = idx // 4, idx % 4
                if sub == 0:
                    s, w3 = wload(s_win[j, :, uc * 512:(uc + 1) * 512].rearrange("(c p) n -> p c n", p=128), (8, 512))
                    cur = (s, w3)
                s, w3 = cur
                bk = mmbank()
                MM([(ps[bk][:, :], w3[:, c, sub * 128:(sub + 1) * 128], hT[:, c, tok(tg)], c == 0, c == 7)
                    for c in range(8)], [wkey(s), ("h", tg)], [("ps", bk)])
                ACT(uT[:, idx, :], ps[bk][:, :], AF.Gelu, [], [("ps", bk), ("u", idx)])
                if idx >= 1:
                    spatial_unit(idx - 1)
            spatial_unit(15)

        def w_out(tg):
            for oc in range(4):
                s, w3 = wload(s_wout[j, :, oc * 256:(oc + 1) * 256].rearrange("(k p) n -> p k n", p=128), (16, 256))
                for dcl in range(2):
                    dc = oc * 2 + dcl
                    bk = mmbank()
                    MM([(ps[bk][:, :], w3[:, kc, dcl * 128:(dcl + 1) * 128], uT[:, kc, :], kc == 0, kc == 15)
                        for kc in range(16)], [wkey(s)] + [("u", kc) for kc in range(16)], [("ps", bk)])
                    TT(xT[:, dc, tok(tg)], ps[bk][:, :], xT[:, dc, tok(tg)], ALU.add, [], [("ps", bk), xk(dc, tg)])

        v_path(0)
        norm(1)
        ln_stats(0)
        u_spatial(0)
        for tg in range(1, NTT):
            v_path(tg)
            if tg + 1 < NTT:
                norm(tg + 1)
            ln_stats(tg)
            w_out(tg - 1)
            phase_norm(pi + 1, tg - 1)
            u_spatial(tg)
        w_out(NTT - 1)
        phase_norm(pi + 1, NTT - 1)
        if next_prefetch is not None:
            next_prefetch()
        p.fence()

    def mla_prefetch(j, st8):
        wdkv = p.sb("wdkv%d" % j, [128, 8, 512], BF16, off=WB)
        wuq = p.sb("wuq%d" % j, [128, 2, 2048], BF16, off=WB + 8192)
        p.dma("pool", "wr0", wdkv[:, :, :], w_dkv[j, :, :].rearrange("(c p) n -> p c n", p=128), writes=[wkey(0)])
        p.dma("pool", "wr1", wuq[:, :, :], w_uq[j, :, :].rearrange("(c p) n -> p c n", p=128), writes=[wkey(1)])
        st8["wdkv"], st8["wuq"] = wdkv, wuq

    def mla_layer(j, l, st8, next_prefetch=None, pi=0):
        wdkv, wuq = st8["wdkv"], st8["wuq"]
        wukv = p.sb("wukv%d" % j, [128, 2048], BF16, off=WB + 16384)
        wo_sl = p.sb("wo%d" % j, [128, 2, 1024], BF16, off=WB + 20480)
        OT = p.sb("OT%d" % j, [128, 2, NLOC], BF16, off=WB + 24576)
        KT = p.sb("KT%d" % j, [128, 2, SEQ], BF16, off=AR)
        V = p.sb("V%d" % j, [128, 32, 256], BF16, off=AR + 16384)
        o = PB
        cqn = p.sb("cqn%d" % j, [128, 2, NLOC], BF16, off=o); o += 8192
        QN = p.sb("QN%d" % j, [128, 2, NLOC], BF16, off=o)
        ckvl = p.sb("ckvl%d" % j, [128, NLOC], BF16, off=o)
        krl = p.sb("krl%d" % j, [64, NLOC], BF16, off=o + 4096); o += 8192
        ckva = p.sb("ckva%d" % j, [128, SEQ], BF16, off=o); o += 8192
        kra = p.sb("kra%d" % j, [128, SEQ], BF16, off=o); o += 8192
        QR = p.sb("QR%d" % j, [128, 2, NLOC], BF16, off=o); o += 8192
        PT_OFF = o
        PT = [p.sb("PT%d_%d" % (j, b), [128, 512], BF16, off=o + 1024 * b) for b in range(4)]; o += 4096
        CQF_OFF = o
        cqf = p.sb("cqf%d" % j, [128, 2, 512], F32, off=o); o += 4096
        ckf = p.sb("ckf%d" % j, [128, 1, 512], F32, off=o); o += 2048
        PT = PT + [p.sb("PTx%d_%d" % (j, b), [128, 512], BF16, off=CQF_OFF + 1024 * b) for b in range(6)]
        t1 = p.sb("t1_%d" % j, [64, 512], F32, off=o); o += 2048
        t2 = p.sb("t2_%d" % j, [64, 512], F32, off=o); o += 2048
        acc = [p.sb("acc%d_0" % j, [128, 512], F32, off=o - 4096), p.sb("acc%d_1" % j, [128, 512], F32, off=o - 2048)]
        REC_OFF = o
        rec = [p.sb("rec%d_%d" % (j, b), [128, 512], F32, off=o + 2048 * b) for b in range(2)]; o += 4096
        assert o <= SB_END, o

        p.dma("pool", "mw2", wukv[:, :], w_ukv[j, :, :], writes=["wukv"] + [("xin", b) for b in range(4)])
        p.op("dve", lambda e: e.memset(kra[64:128, :], 0.0), writes=["kraz"])
        p.op("dve", lambda e: e.memset(QR[64:128, :, :], 0.0), writes=["QRz"])

        def norm(tt):
            phase_norm(pi, tt)
        norm(0)
        cqf2 = [cqf, p.sb("cqfb%d" % j, [128, 2, 512], F32, off=PT_OFF)]
        ckf2 = [ckf, p.sb("ckfb%d" % j, [128, 1, 512], F32, off=REC_OFF)]

        def lat_a(tt):
            bsel = tt % 2
            for fch in range(3):
                bk = mmbank()
                MM([(ps[bk][:, :], wdkv[:, c, fch * 128:(fch + 1) * 128], hT[:, c, tok(tt)], c == 0, c == 7)
                    for c in range(8)], [wkey(0), ("h", tt)], [("ps", bk)])
                if fch < 2:
                    EVAC(cqf2[bsel][:, fch, :], ps[bk][:, :], [], [("ps", bk), ("cqf", bsel, fch)])
                else:
                    EVAC(ckf2[bsel][:, 0, :], ps[bk][:, :], [], [("ps", bk), ("ckf", bsel)])
            ba = mmbank()
            MM([(ps[ba][0:64, :], wdkv[:, c, 384:448], hT[:, c, tok(tt)], c == 0, c == 7) for c in range(8)],
               [wkey(0), ("h", tt)], [("ps", ba)])
            bb = mmbank()
            MM([(ps[bb][0:64, :], wdkv[:, c, 448:512], hT[:, c, tok(tt)], c == 0, c == 7) for c in range(8)],
               [wkey(0), ("h", tt)], [("ps", bb)])
            TT(t1[:, :], ps[ba][0:64, :], cos2[:, tok(tt)], ALU.mult, ["cos2"], [("ps", ba), "t1"])
            TT(t2[:, :], ps[bb][0:64, :], sin2s[:, tok(tt)], ALU.mult, ["sin2s"], [("ps", bb), "t2"])
            TT(krl[:, tok(tt)], t1[:, :], t2[:, :], ALU.add, ["t1", "t2"], [("krl", tt)])

        def lat_b(tt):
            bsel = tt % 2
            rmsnorm(cqf2[bsel][:, :, :], 2, 256, 72 + 2 * j, [cqn[:, c, tok(tt)] for c in range(2)], epsn,
                    [("cqf", bsel, 0), ("cqf", bsel, 1)], [("cqn", tt)])
            rmsnorm(ckf2[bsel][:, :, :], 1, 128, 76 + j, [ckvl[:, tok(tt)]], epsn, [("ckf", bsel)], [("ckvl", tt)])

        for tt in range(NTT):
            if tt + 1 < NTT:
                norm(tt + 1)
            lat_a(tt)
            if tt >= 1:
                lat_b(tt - 1)
        lat_b(NTT - 1)
        lk = [("ckvl", tt) for tt in range(NTT)]
        rk = [("krl", tt) for tt in range(NTT)]
        p.dma("sp", "cci%d" % j, cc_in[j].ap()[0:128, :], ckvl[:, :], reads=lk, writes=["ccin_a"])
        p.dma("sp", "cci%d" % j, cc_in[j].ap()[128:192, :], krl[:, :], reads=rk, writes=["ccin_b"])

        def ccfn(e, j=j):
            return e.collective_compute("AllGather", ALU.bypass, replica_groups=PAIRS,
                                        ins=[cc_in[j].ap().opt()], outs=[cc_out[j].ap().opt()])
        p.custom("pool", "cc%d" % j, 1, ccfn, reads=["ccin_a", "ccin_b"], writes=["ccout"])
        for r in range(2):
            p.dma("sp", "ccl%d" % j, ckva[:, :].rearrange("p (i r t) -> p i r t", r=2, t=128)[:, :, r, :],
                  cc_out[j].ap()[r * 192:r * 192 + 128, :].rearrange("p (i t) -> p i t", t=128),
                  reads=["ccout"], writes=[("ckva", r)])
            p.dma("sp", "ccl%d" % j, kra[0:64, :].rearrange("p (i r t) -> p i r t", r=2, t=128)[:, :, r, :],
                  cc_out[j].ap()[r * 192 + 128:r * 192 + 192, :].rearrange("p (i t) -> p i t", t=128),
                  reads=["ccout"], writes=[("kra", r)])
        p.fence()
        for hp in range(4):
            for hl in range(2):
                h = 2 * hp + hl
                for kt in range(8):
                    bk = mmbank((0, 1, 2, 7))
                    MM([(ps[bk][:, :], wukv[:, h * 128:(h + 1) * 128], ckva[:, kt * 512:(kt + 1) * 512], True, True)],
                       ["wukv", ("ckva", 0), ("ckva", 1)], [("ps", bk)])
                    EVAC(KT[:, hl, kt * 512:(kt + 1) * 512], ps[bk][:, :], [], [("ps", bk), ("KT", hl)])
            for kp in range(16):
                bk = mmbank((0, 1, 2, 7))
                MM([(ps[bk][:, q * 256:(q + 1) * 256], ckva[:, (2 * kp + q) * 128:(2 * kp + q + 1) * 128],
                     wukv[:, 1024 + hp * 256:1024 + (hp + 1) * 256], True, True) for q in range(2)],
                   ["wukv", ("ckva", 0), ("ckva", 1)], [("ps", bk)])
                EVAC(V[:, 2 * kp:2 * kp + 2, :], ps[bk][:, :].rearrange("p (q d) -> p q d", q=2), [], [("ps", bk), "V"])
            p.dma("pool", "mw3", wo_sl[:, :, :],
                  w_o[j, hp * 256:(hp + 1) * 256, :].rearrange("(h p) n -> p h n", p=128), writes=["wo"])
            for tt in range(NTT):
                for hl in range(2):
                    h = 2 * hp + hl
                    bk = mmbank((0, 1, 2, 7))
                    MM([(ps[bk][:, :], wuq[:, kc, h * 128:(h + 1) * 128], cqn[:, kc, tok(tt)], kc == 0, kc == 1)
                        for kc in range(2)], [wkey(1), ("cqn", tt)], [("ps", bk)])
                    ACT(QN[:, hl, tok(tt)], ps[bk][:, :], AF.Copy, [], [("ps", bk), ("QN", hl)], scale=SCALE)
                    ba = mmbank((0, 1, 2, 7))
                    MM([(ps[ba][0:64, :], wuq[:, kc, 1024 + h * 64:1024 + (h + 1) * 64], cqn[:, kc, tok(tt)], kc == 0, kc == 1)
                        for kc in range(2)], [wkey(1), ("cqn", tt)], [("ps", ba)])
                    bb = mmbank((0, 1, 2, 7))
                    MM([(ps[bb][0:64, :], wuq[:, kc, 1536 + h * 64:1536 + (h + 1) * 64], cqn[:, kc, tok(tt)], kc == 0, kc == 1)
                        for kc in range(2)], [wkey(1), ("cqn", tt)], [("ps", bb)])
                    STT(t1[:, :], ps[ba][0:64, :], SCALE, cos2[:, tok(tt)], ALU.mult, ALU.mult, ["cos2"], [("ps", ba), "t1"])
                    STT(t2[:, :], ps[bb][0:64, :], SCALE, sin2s[:, tok(tt)], ALU.mult, ALU.mult, ["sin2s"], [("ps", bb), "t2"])
                    TT(QR[0:64, hl, tok(tt)], t1[:, :], t2[:, :], ALU.add, ["t1", "t2"], [("QR", hl)])
            if hp == 3 and next_prefetch is not None:
                next_prefetch()
            units = []
            for qg in range(4):
                i0 = 4 * qg
                nkt = 8 * qg + 8
                for kt in range(nkt):
                    imin = max(i0, kt // 2)
                    q0 = imin * 128
                    n = (i0 + 4) * 128 - q0
                    off = q0 - i0 * 128
                    midx = (kt % 2) if kt >= 2 * i0 else None
                    units.append((qg, kt, q0, n, off, midx, kt == 0, kt == nkt - 1))
            SPAIR = ((0, 1), (2, 7))
            NPP = len(PT) // 2

            def emit_pv(idx):
                (qg, kt, q0, n, off, midx, first, last) = units[idx]
                pp = idx % NPP
                MM([(ps[3 + hl][:, off:off + n], V[:, kt, hl * 128:(hl + 1) * 128], PT[2 * pp + hl][:, off:off + n], first, last)
                    for hl in range(2)], ["V", ("pt", 2 * pp), ("pt", 2 * pp + 1)], [("ps", 3), ("ps", 4)])
                for hl in range(2):
                    akey = "t1" if hl == 0 else "t2"
                    pb = 2 * pp + hl
                    if first:
                        p.op("dve", lambda e, a=acc[hl], pt=PT[pb]: e.tensor_copy(a[:, :], pt[:, :]), [("pt", pb)], [akey])
                    else:
                        p.op("dve", lambda e, a=acc[hl][:, off:off + n], pt=PT[pb][:, off:off + n]: e.tensor_tensor(a, a, pt, ALU.add),
                             [("pt", pb)], [akey])
                if last:
                    for hl in range(2):
                        akey = "t1" if hl == 0 else "t2"
                        ob, sbn = 3 + hl, 5 + hl
                        MM([(ps[sbn][:, :], ones_f[:, :], acc[hl][:, :], True, True)], [akey, "ones"], [("ps", sbn)])
                        ACT(rec[hl][:, :], ps[sbn][:, :], AF.Ln, [], [("ps", sbn), ("rec", hl)])
                        ACT(rec[hl][:, :], rec[hl][:, :], AF.Exp, [], [("rec", hl)], scale=-1.0)
                        TT(OT[:, hl, tok(qg)], ps[ob][:, :], rec[hl][:, :], ALU.mult, [("rec", hl)], [("ps", ob), ("OT", qg)])

            for idx, (qg, kt, q0, n, off, midx, first, last) in enumerate(units):
                sb2 = SPAIR[idx % 2]
                pp = idx % NPP
                ks = slice(kt * 128, (kt + 1) * 128)
                mm = []
                for hl in range(2):
                    oo = ps[sb2[hl]][:, off:off + n]
                    mm.append((oo, KT[:, hl, ks], QN[:, hl, q0:q0 + n], True, False))
                    mm.append((oo, kra[:, ks], QR[:, hl, q0:q0 + n], False, True))
                MM(mm, [("KT", 0), ("KT", 1), ("QN", 0), ("QN", 1), ("QR", 0), ("QR", 1), ("kra", 0), ("kra", 1), "kraz", "QRz"],
                   [("ps", sb2[0]), ("ps", sb2[1])])
                for hl in range(2):
                    pb = 2 * pp + hl
                    ACT(PT[pb][:, off:off + n], ps[sb2[hl]][:, off:off + n], AF.Exp, [], [("ps", sb2[hl]), ("pt", pb)])
                    if midx is not None:
                        TT(PT[pb][:, off:off + 128], PT[pb][:, off:off + 128], masks[:, midx, 0:128], ALU.mult,
                           ["c"], [("pt", pb)])
                if idx >= 1:
                    emit_pv(idx - 1)
            emit_pv(len(units) - 1)
            for tt in range(NTT):
                for dc in range(8):
                    bk = mmbank((0, 1, 2, 7))
                    MM([(ps[bk][:, :], wo_sl[:, hl, dc * 128:(dc + 1) * 128], OT[:, hl, tok(tt)], hl == 0, hl == 1)
                        for hl in range(2)], ["wo", ("OT", tt)], [("ps", bk)])
                    TT(xT[:, dc, tok(tt)], ps[bk][:, :], xT[:, dc, tok(tt)], ALU.add, [], [("ps", bk), xk(dc, tt)])
                if hp == 3:
                    phase_norm(pi + 1, tt, extra_w=[("KT", 0), ("KT", 1), "V"])
        p.fence()

    sts = [dict() for _ in phases]

    def make_prefetch(i):
        if i >= len(phases):
            return None
        kind, l = phases[i]
        if kind == "ffn":
            return lambda: ffn_prefetch(l)
        if kind == "sgu":
            return lambda: sgu_prefetch(l // 2, sts[i])
        return lambda: mla_prefetch(l // 2, sts[i])

    if phases:
        make_prefetch(0)()
    load_x()
    if not phases or phases[0][0] != "mla":
        p.fence()
    for i, (kind, l) in enumerate(phases):
        nxt = make_prefetch(i + 1)
        if kind == "ffn":
            ffn_layer(l, nxt, i)
        elif kind == "sgu":
            sgu_layer(l // 2, l, sts[i], nxt, i)
        else:
            mla_layer(l // 2, l, sts[i], nxt, i)

    for tt in range(NTT):
        phase_norm(len(phases), tt)
    p.final_wait("sp", okeys)
    p.emit()
    return nc, p


_CACHE = {}


def _prep_inputs(x, positions, norm_mix, norm_ffn, final_norm,
                 mla_w_dkv, mla_q_norm, mla_kv_norm, mla_w_uq, mla_w_ukv, mla_w_o,
                 sgu_w_in, sgu_ln_g, sgu_ln_b, sgu_w_spatial, sgu_b_spatial, sgu_w_out,
                 ffn_w_up, ffn_w_down):
    f32 = np.float32
    A = lambda a: np.ascontiguousarray(np.asarray(a))
    x = A(x); positions = A(positions)
    gains = np.zeros((128, 112), f32)
    gains[:, 0:32] = A(norm_mix).reshape(4, 8, 128).transpose(2, 0, 1).reshape(128, 32)
    gains[:, 32:64] = A(norm_ffn).reshape(4, 8, 128).transpose(2, 0, 1).reshape(128, 32)
    gains[:, 64:72] = A(final_norm).reshape(8, 128).T
    gains[:, 72:76] = A(mla_q_norm).reshape(2, 2, 128).transpose(2, 0, 1).reshape(128, 4)
    gains[:, 76:78] = A(mla_kv_norm).reshape(2, 128).T
    gains[:, 80:112] = A(sgu_ln_g).reshape(2, 16, 128).transpose(2, 0, 1).reshape(128, 32)
    invf = (10000.0 ** (-np.arange(0, 64, 2, dtype=f32) / 64.0)).astype(f32)
    ropec = np.zeros((64, 2), f32)
    ropec[:, 0] = np.concatenate([invf, invf])
    ropec[:, 1] = np.concatenate([-np.ones(32, f32), np.ones(32, f32)])
    ident = np.eye(128, dtype=f32)
    maskT = (np.arange(128)[:, None] <= np.arange(128)[None, :]).astype(ml_dtypes.bfloat16)
    wd = A(mla_w_dkv)
    kr = wd[:, :, 384:448]
    w_dkv = A(np.concatenate([wd, kr[:, :, 32:64], kr[:, :, 0:32]], axis=2))
    wq = A(mla_w_uq).reshape(2, 256, 8, 192)
    qn = wq[:, :, :, 0:128].reshape(2, 256, 1024)
    qr = wq[:, :, :, 128:192]
    qsw = np.concatenate([qr[..., 32:64], qr[..., 0:32]], axis=-1)
    w_uq = A(np.concatenate([qn, qr.reshape(2, 256, 512), qsw.reshape(2, 256, 512)], axis=2))
    wkv = A(mla_w_ukv).reshape(2, 128, 8, 256)
    w_ukv = A(np.concatenate([wkv[:, :, :, 0:128].reshape(2, 128, 1024), wkv[:, :, :, 128:256].reshape(2, 128, 1024)], axis=2))
    s_l2 = A(np.stack([A(sgu_ln_b), np.ones((2, 2048), f32)], axis=1))
    s_b = A(sgu_b_spatial)
    s_wsT = A(A(sgu_w_spatial).transpose(0, 3, 1, 2).reshape(2, 128, 1024))
    shared = dict(maskT=maskT, gains=gains, ropec=ropec, ident=ident, w_dkv=w_dkv, w_uq=w_uq, w_ukv=w_ukv,
                  w_o=A(mla_w_o), s_win=A(sgu_w_in), s_wout=A(sgu_w_out), s_l2=s_l2, s_b=s_b,
                  s_wsT=s_wsT, f_up=A(ffn_w_up), f_dn=A(ffn_w_down))
    tri = (np.arange(128)[:, None] <= np.arange(128)[None, :])
    in_maps = []
    for c in range(8):
        b, r = c // 2, c % 2
        xs = A(x[b].reshape(16, 2, 128, D_MODEL)[:, r].reshape(NLOC, D_MODEL))
        ps_ = positions[b].reshape(16, 2, 128)[:, r].reshape(1, NLOC)
        pos = A(np.broadcast_to(ps_, (64, NLOC))).astype(np.int32)
        m = np.zeros((128, 2, 256), ml_dtypes.bfloat16)
        if r == 0:
            m[:, 0, 0:128] = tri; m[:, 0, 128:256] = tri
        else:
            m[:, 0, :] = 1
            m[:, 1, 0:128] = tri; m[:, 1, 128:256] = tri
        d = dict(shared)
        d.update(xs=xs, pos=pos, masks=m)
        in_maps.append(d)
    return in_maps


def kernel(**inputs):
    if "nc" not in _CACHE:
        _CACHE["nc"] = build_program(4)[0]
    nc = _CACHE["nc"]
    in_maps = _prep_inputs(**inputs)
    res = run_bass_kernel_spmd(nc, in_maps, core_ids=list(range(8)))
    out = np.zeros((4, SEQ, D_MODEL), np.float32)
    ov = out.reshape(4, 16, 2, 128, D_MODEL)
    for c in range(8):
        b, r = c // 2, c % 2
        ov[b, :, r] = np.asarray(res.results[c]["y"]).reshape(16, 128, D_MODEL)
    return out
```

```python
import contextlib
import numpy as np
import ml_dtypes
import concourse.bass as bass
import concourse.mybir as mybir
from concourse.bass_utils import run_bass_kernel_spmd

F32 = mybir.dt.float32
BF16 = mybir.dt.bfloat16
I32 = mybir.dt.int32
AF = mybir.ActivationFunctionType
ALU = mybir.AluOpType
AX = mybir.AxisListType

SB_BASE = 16512
SB_END = 229376

D_MODEL = 1024
SEQ = 4096
NLOC = 2048
NTT = 4
NORM_EPS = 1e-6
LN_EPS = 1e-5
SCALE = 192.0 ** -0.5
PAIRS = [[0, 1], [2, 3], [4, 5], [6, 7]]
SUM_MODE = "dve"


class Prog:
    ENGS = ("pe", "act", "dve", "pool", "sp")

    def __init__(self, nc):
        self.nc = nc
        self.ops = {e: [] for e in self.ENGS}
        self.cnt = {e: 0 for e in self.ENGS}
        self.waited = {e: {} for e in self.ENGS}
        self.last_w = {}
        self.readers = {}
        self.dcnt = {}
        self.sems = {}
        self.sb_off = SB_BASE

    def sb(self, name, shape, dt, off=None):
        size = int(np.prod(shape[1:])) * (4 if dt in (F32, I32) else 2)
        size = (size + 31) // 32 * 32
        if off is None:
            off = self.sb_off
            self.sb_off += size
            assert self.sb_off <= SB_END, (name, self.sb_off)
        else:
            assert off + size <= SB_END, (name, off, size)
        return self.nc.alloc_sbuf_tensor_at(name, list(shape), dt, offset=off)

    def _deps(self, eng, reads, writes):
        deps = []
        for r in reads:
            if r in self.last_w:
                deps.append(self.last_w[r])
        for w in writes:
            if w in self.last_w:
                deps.append(self.last_w[w])
            deps.extend(self.readers.get(w, ()))
        best = {}
        for (sk, val, deng) in deps:
            if deng == "pe" and eng == "pe":
                continue
            if self.waited[eng].get(sk, 0) >= val:
                continue
            best[sk] = max(best.get(sk, 0), val)
        for sk, val in best.items():
            self.waited[eng][sk] = val
        return list(best.items())

    def _commit(self, token, reads, writes):
        for r in reads:
            self.readers.setdefault(r, []).append(token)
        for w in writes:
            self.last_w[w] = token
            self.readers[w] = []

    def op(self, eng, fn, reads=(), writes=()):
        reads = list(reads)
        writes = list(writes)
        waits = self._deps(eng, reads, writes)
        self.cnt[eng] += 1
        self.ops[eng].append((waits, fn, (eng, 1)))
        self._commit((eng, self.cnt[eng], eng), reads, writes)

    def dma(self, q, semname, out, in_, reads=(), writes=()):
        def fn(e, out=out, in_=in_):
            return e.dma_start(out=out, in_=in_)
        self.custom(q, semname, 16, fn, reads, writes)

    def custom(self, q, semname, inc, fn, reads=(), writes=()):
        reads = list(reads)
        writes = list(writes)
        waits = self._deps(q, reads, writes)
        sk = "d_" + semname
        self.dcnt[sk] = self.dcnt.get(sk, 0) + inc
        self.ops[q].append((waits, fn, (sk, inc)))
        self._commit((sk, self.dcnt[sk], "dma"), reads, writes)

    def fence(self):
        for e in self.ENGS:
            waits = []
            for f in ("pe", "act", "dve", "pool"):
                if f == e or self.cnt[f] == 0:
                    continue
                if self.waited[e].get(f, 0) >= self.cnt[f]:
                    continue
                self.waited[e][f] = self.cnt[f]
                waits.append((f, self.cnt[f]))
            for sk, val in self.dcnt.items():
                if self.waited[e].get(sk, 0) >= val:
                    continue
                self.waited[e][sk] = val
                waits.append((sk, val))
            if waits:
                self.ops[e].append((waits, None, None))

    def final_wait(self, q, keys):
        waits = self._deps(q, keys, [])
        self.ops[q].append((waits, None, None))

    def emit(self):
        nc = self.nc
        names = set()
        for e in self.ENGS:
            for waits, fn, inc in self.ops[e]:
                for sk, _ in waits:
                    names.add(sk)
                if inc is not None:
                    names.add(inc[0])
        with contextlib.ExitStack() as st:
            for n in sorted(names):
                self.sems[n] = st.enter_context(nc.semaphore("s_" + n))
            block = st.enter_context(nc.Block())

            def run(e, h):
                for waits, fn, inc in self.ops[e]:
                    for sk, val in waits:
                        h.wait_ge(self.sems[sk], val)
                    if fn is None:
                        continue
                    ins = fn(h)
                    if inc is not None:
                        ins.then_inc(self.sems[inc[0]], inc[1])

            @block.tensor
            def _(h):
                run("pe", h)

            @block.scalar
            def _(h):
                run("act", h)

            @block.vector
            def _(h):
                run("dve", h)

            @block.gpsimd
            def _(h):
                run("pool", h)

            @block.sync
            def _(h):
                run("sp", h)


def build_program(nlayers=4):
    nc = bass.Bass("TRN2", target_bir_lowering=False)

    def din(name, shape, dt):
        return nc.dram_tensor(name, list(shape), dt, kind="ExternalInput").ap()

    xs = din("xs", [NLOC, D_MODEL], F32)
    pos = din("pos", [64, NLOC], I32)
    masks_d = din("masks", [128, 2, 256], BF16)
    maskT_d = din("maskT", [128, 128], BF16)
    gains_d = din("gains", [128, 112], F32)
    rc_d = din("ropec", [64, 2], F32)
    ident_d = din("ident", [128, 128], F32)
    w_dkv = din("w_dkv", [2, 1024, 512], F32)
    w_uq = din("w_uq", [2, 256, 2048], F32)
    w_ukv = din("w_ukv", [2, 128, 2048], F32)
    w_o = din("w_o", [2, 1024, 1024], F32)
    s_win = din("s_win", [2, 1024, 4096], F32)
    s_wout = din("s_wout", [2, 2048, 1024], F32)
    s_l2 = din("s_l2", [2, 2, 2048], F32)
    s_b = din("s_b", [2, 8, 128], F32)
    s_wsT = din("s_wsT", [2, 128, 1024], F32)
    f_up = din("f_up", [4, 1024, 4096], F32)
    f_dn = din("f_dn", [4, 4096, 1024], F32)
    y = nc.dram_tensor("y", [NLOC, D_MODEL], F32, kind="ExternalOutput").ap()
    cc_in = [nc.dram_tensor("cc_in%d" % j, [192, NLOC], BF16) for j in range(2)]
    cc_out = [nc.dram_tensor("cc_out%d" % j, [384, NLOC], BF16) for j in range(2)]

    p = Prog(nc)
    xT = p.sb("xT", [128, 8, NLOC], F32)
    cos2 = p.sb("cos2", [64, NLOC], BF16)
    sin2s = p.sb("sin2s", [64, NLOC], BF16)
    ident = p.sb("ident", [128, 128], F32)
    ones = p.sb("ones", [128, 128], BF16)
    masks = p.sb("masksb", [128, 2, 256], BF16)
    maskT = p.sb("maskTb", [128, 128], BF16)
    gains = p.sb("gainsb", [128, 112], F32)
    ones_f = p.sb("ones_f", [128, 128], F32)
    ropec = p.sb("ropecb", [64, 2], F32)
    epsn = p.sb("epsn", [128, 1], F32)
    epsl = p.sb("epsl", [128, 1], F32)
    AR = p.sb_off
    hT = p.sb("hT", [128, 8, NLOC], BF16, off=AR)
    sq = p.sb("sq", [128, 8, 512], BF16, off=AR + 32768)
    rs = p.sb("rs", [128, 512], F32, off=AR + 40960)
    WB = AR + 43008
    PB = WB + 32768
    assert SB_END - PB >= 59000, SB_END - PB
    ps = [nc.alloc_psum_tensor("ps%d" % b, [128, 512], F32) for b in range(8)]

    state = {"mm": 0, "st": 0}

    def mmbank(pool=(0, 1, 2, 3, 4, 5)):
        b = pool[state["mm"] % len(pool)]
        state["mm"] += 1
        return b

    def stbank():
        b = (6, 7)[state["st"] % 2]
        state["st"] += 1
        return b

    def tok(tt):
        return slice(tt * 512, (tt + 1) * 512)

    def MM(mms, reads, writes):
        def fn(e, mms=mms):
            ins = None
            for (o, l, r, s0, s1) in mms:
                ins = e.matmul(o, l, r, start=s0, stop=s1, skip_group_check=True)
            return ins
        p.op("pe", fn, reads, writes)

    def ACT(out, in_, func, reads, writes, **kw):
        p.op("act", lambda e: e.activation(out, in_, func, **kw), reads, writes)

    def TT(out, a, b, op, reads, writes):
        p.op("dve", lambda e: e.tensor_tensor(out, a, b, op), reads, writes)

    def TS(out, a, s1, s2, op0, op1, reads, writes):
        if op1 is None:
            p.op("dve", lambda e: e.tensor_scalar(out, a, s1, None, op0), reads, writes)
        else:
            p.op("dve", lambda e: e.tensor_scalar(out, a, s1, s2, op0, op1), reads, writes)

    def STT(out, a, s, b, op0, op1, reads, writes):
        p.op("dve", lambda e: e.scalar_tensor_tensor(out=out, in0=a, scalar=s, in1=b, op0=op0, op1=op1), reads, writes)

    def DCOPY(out, in_, reads, writes):
        p.op("dve", lambda e: e.tensor_copy(out, in_), reads, writes)

    def RECIP(out, in_, reads, writes):
        p.op("dve", lambda e: e.reciprocal(out, in_), reads, writes)

    evs = {"n": 0}

    def EVAC(out, in_, reads, writes, scale=None):
        evs["n"] += 1
        if evs["n"] % 2 == 0 and scale is None:
            DCOPY(out, in_, reads, writes)
        else:
            if scale is None:
                ACT(out, in_, AF.Copy, reads, writes)
            else:
                ACT(out, in_, AF.Copy, reads, writes, scale=scale)

    def xk(c, tt):
        return ("x", c, tt)

    def xkeys(tt):
        return [xk(c, tt) for c in range(8)]

    def rmsnorm(src3, nch, D, gcol, dsts, eps_t, reads, writes, np_=128):
        P = slice(0, np_)
        reads = list(reads) + ["cos2", "sin2s"]
        ACT(sq[P, 0:nch, :], src3, AF.Square, reads, ["sq", ("tmpx", 0), ("tmpx", 1)])
        sb_ = stbank()
        MM([(ps[sb_][P, :], ones[P, 0:np_], sq[P, c, :], c == 0, c == nch - 1) for c in range(nch)],
           ["sq", "ones"], [("ps", sb_)])
        ACT(rs[P, :], ps[sb_][P, :], AF.Ln, ["eps"], [("ps", sb_), "rs"], bias=eps_t[P, 0:1], scale=1.0 / D)
        ACT(rs[P, :], rs[P, :], AF.Exp, [], ["rs"], scale=-0.5)
        for c in range(nch):
            STT(dsts[c], src3[:, c, :], gains[P, gcol + c:gcol + c + 1], rs[P, :], ALU.mult, ALU.mult,
                reads + ["rs", "c"], [writes[c]] if isinstance(writes, list) and len(writes) == nch else writes)

    for (dst, src) in ((ident[:, :], ident_d), (masks[:, :, :], masks_d), (maskT[:, :], maskT_d),
                       (gains[:, :], gains_d), (ropec[:, :], rc_d)):
        p.dma("sp", "c", dst, src, writes=["c"])
    p.op("dve", lambda e: e.memset(ones[:, :], 1.0), writes=["ones"])
    p.op("dve", lambda e: e.memset(ones_f[:, :], 1.0), writes=["ones"])
    p.op("dve", lambda e: e.memset(epsn[:, :], NORM_EPS), writes=["eps"])
    p.op("dve", lambda e: e.memset(epsl[:, :], LN_EPS), writes=["eps"])

    pos_i = p.sb("pos_i", [64, NLOC], I32, off=AR)
    pos_f = p.sb("pos_f", [64, NLOC], F32, off=AR + 8192)
    ang = p.sb("ang", [64, NLOC], F32, off=AR + 16384)
    ang2 = p.sb("ang2", [64, NLOC], F32, off=AR + 24576)
    TWO_PI = 2.0 * float(np.pi)
    p.dma("sp", "pos", pos_i[:, :], pos, writes=["pos_i"])
    DCOPY(pos_f[:, :], pos_i[:, :], ["pos_i"], ["pos_f"])
    TS(ang[:, :], pos_f[:, :], ropec[:, 0:1], None, ALU.mult, None, ["pos_f", "c"], ["ang"])

    def reduce_and_sin(dst, shift, key):
        TS(ang2[:, :], ang[:, :], shift, None, ALU.add, None, ["ang"], ["ang2"])
        TS(pos_f[:, :], ang2[:, :], 1.0 / TWO_PI, None, ALU.mult, None, ["ang2"], ["pos_f"])
        DCOPY(pos_i[:, :], pos_f[:, :], ["pos_f"], ["pos_i"])
        DCOPY(pos_f[:, :], pos_i[:, :], ["pos_i"], ["pos_f"])
        STT(ang2[:, :], pos_f[:, :], -TWO_PI, ang2[:, :], ALU.mult, ALU.add, ["pos_f", "ang2"], ["ang2"])
        TS(pos_f[:, :], ang2[:, :], float(np.pi), TWO_PI, ALU.is_gt, ALU.mult, ["ang2"], ["pos_f"])
        TT(ang2[:, :], ang2[:, :], pos_f[:, :], ALU.subtract, ["pos_f", "ang2"], ["ang2"])
        ACT(dst, ang2[:, :], AF.Sin, ["ang2"], [key])

    reduce_and_sin(cos2[:, :], 0.5 * float(np.pi), "cos2")
    sin_f = p.sb("sin_f", [64, NLOC], F32, off=AR + 32768 - 8192 + 8192)
    reduce_and_sin(sin_f[:, :], 0.0, "sin_f")
    TS(sin2s[:, :], sin_f[:, :], ropec[:, 1:2], None, ALU.mult, None, ["sin_f", "c"], ["sin2s"])

    xin = [p.sb("xin%d" % b, [128, 1024], F32, off=WB + 16384 + 4096 * b) for b in range(4)]

    def load_x():
        for tile in range(16):
            b = tile % 4
            tt = tile // 4
            p.dma("sp", "xin%d" % b, xin[b][:, :], xs[tile * 128:(tile + 1) * 128, :], writes=[("xin", b)])
            for half in range(2):
                bk = mmbank()
                p.op("pe", (lambda e, bk=bk, b=b, half=half: [e.transpose(ps[bk][:, k * 128:(k + 1) * 128],
                                                                          xin[b][:, (half * 4 + k) * 128:(half * 4 + k + 1) * 128],
                                                                          ident[:, :]) for k in range(4)][-1]),
                     [("xin", b), "c"], [("ps", bk)])
                EVAC(xT[:, half * 4:half * 4 + 4, tile * 128:(tile + 1) * 128],
                     ps[bk][:, :].rearrange("p (k t) -> p k t", k=4),
                     [], [("ps", bk)] + [xk(half * 4 + k, tt) for k in range(4)])

    phases = []
    for l_ in range(nlayers):
        phases.append(("mla" if l_ % 2 == 0 else "sgu", l_))
        phases.append(("ffn", l_))
    normed = set()
    okeys = []
    yT = p.sb("yT", [128, 8, 512], F32, off=PB + 16384)
    yo = [p.sb("yo%d" % b, [128, 1024], F32, off=WB + 4096 * b) for b in range(4)]

    def final_part(tt, extra_w):
        rmsnorm(xT[:, :, tok(tt)], 8, 1024, 64, [yT[:, c, :] for c in range(8)], epsn, xkeys(tt), ["yT"])
        for tl in range(4):
            tile = tt * 4 + tl
            b = tile % 4
            for half in range(2):
                bk = mmbank()
                p.op("pe", (lambda e, bk=bk, half=half, tl=tl: [e.transpose(ps[bk][:, k * 128:(k + 1) * 128],
                                                                            yT[:, half * 4 + k, tl * 128:(tl + 1) * 128],
                                                                            ident[:, :]) for k in range(4)][-1]),
                     ["yT", "c"], [("ps", bk)])
                EVAC(yo[b][:, half * 512:(half + 1) * 512], ps[bk][:, :], [], [("ps", bk), ("yo", b, half)] + list(extra_w))
            p.dma("sp", "yo%d" % b, y[tile * 128:(tile + 1) * 128, :], yo[b][:, :],
                  reads=[("yo", b, 0), ("yo", b, 1)], writes=[("yout", tile)])
            okeys.append(("yout", tile))

    def phase_norm(i, tt, extra_w=()):
        if (i, tt) in normed:
            return
        normed.add((i, tt))
        if i >= len(phases):
            final_part(tt, extra_w)
            return
        kind, l = phases[i]
        gcol = 32 + l * 8 if kind == "ffn" else l * 8
        rmsnorm(xT[:, :, tok(tt)], 8, 1024, gcol, [hT[:, c, tok(tt)] for c in range(8)], epsn,
                xkeys(tt), [("h", tt)] + list(extra_w))

    def wkey(r):
        return ("Wreg", r)

    def ffn_tensors(l):
        wu = [p.sb("wu%d_%d" % (l, s), [128, 8, 512], BF16, off=WB + 16384 * s) for s in range(2)]
        wd = [p.sb("wd%d_%d" % (l, s), [128, 4, 1024], BF16, off=WB + 8192 + 16384 * s) for s in range(2)]
        return wu, wd

    def ffn_load(l, hc, wu, wd, extra=()):
        s = hc % 2
        p.dma("pool", "wr%d" % (2 * s), wu[s][:, :, :],
              f_up[l, :, hc * 512:(hc + 1) * 512].rearrange("(c p) n -> p c n", p=128),
              writes=[wkey(2 * s)] + [k for k in extra if k[1] == 2 * s])
        p.dma("pool", "wr%d" % (2 * s + 1), wd[s][:, :, :],
              f_dn[l, hc * 512:(hc + 1) * 512, :].rearrange("(s p) n -> p s n", p=128),
              writes=[wkey(2 * s + 1)] + [k for k in extra if k[1] == 2 * s + 1])

    def ffn_prefetch(l):
        wu, wd = ffn_tensors(l)
        ffn_load(l, 0, wu, wd)

    def ffn_layer(l, next_prefetch=None, pi=0):
        hid = [p.sb("hid%d_%d" % (l, b), [128, 4, 512], BF16, off=PB + 4096 * b) for b in range(2)]
        rr = [p.sb("rr%d_%d" % (l, b), [128, 512], F32, off=PB + 8192 + 2048 * b) for b in range(2)]
        wu, wd = ffn_tensors(l)

        for tt in range(NTT):
            phase_norm(pi, tt)
        rc = {"n": 0}

        def up(hc, tt, hb):
            s = hc % 2
            for sub in range(4):
                bk = mmbank()
                MM([(ps[bk][:, :], wu[s][:, c, sub * 128:(sub + 1) * 128], hT[:, c, tok(tt)], c == 0, c == 7)
                    for c in range(8)], [wkey(2 * s), ("h", tt)], [("ps", bk)])
                rb = rc["n"] % 2
                rc["n"] += 1
                ACT(rr[rb][:, :], ps[bk][:, :], AF.Relu, [], [("ps", bk), ("rr", rb)])
                TT(hid[hb][:, sub, :], rr[rb][:, :], rr[rb][:, :], ALU.mult, [("rr", rb)], [("hid", hb, sub)])

        def down(hc, tt, hb):
            s = hc % 2
            for dc in range(8):
                bk = mmbank()
                MM([(ps[bk][:, :], wd[s][:, sub, dc * 128:(dc + 1) * 128], hid[hb][:, sub, :], sub == 0, sub == 3)
                    for sub in range(4)], [wkey(2 * s + 1)] + [("hid", hb, sub) for sub in range(4)], [("ps", bk)])
                TT(xT[:, dc, tok(tt)], ps[bk][:, :], xT[:, dc, tok(tt)], ALU.add, [], [("ps", bk), xk(dc, tt)])
            if hc == 7:
                phase_norm(pi + 1, tt, extra_w=[wkey(0), wkey(1)] if pi + 1 >= len(phases) else [])

        blocks = [(hc, tt) for hc in range(8) for tt in range(NTT)]
        for i, (hc, tt) in enumerate(blocks):
            if tt == 0 and hc > 0:
                ffn_load(l, hc, wu, wd)
            up(hc, tt, i % 2)
            if i >= 1:
                ph, pt_ = blocks[i - 1]
                down(ph, pt_, (i - 1) % 2)
                if (ph, pt_) == (6, NTT - 1) and next_prefetch is not None:
                    next_prefetch()
        down(blocks[-1][0], blocks[-1][1], (len(blocks) - 1) % 2)
        p.fence()

    def sgu_wload(j, wc, src_ap, a):
        s = wc["n"] % 4
        wc["n"] += 1
        wsl_s = p.sb("sw%d_%d_%d" % (j, s, wc["n"]), [128, 4096], BF16, off=WB + 8192 * s)
        dst = wsl_s[:, :].rearrange("p (a b) -> p a b", a=a)
        p.dma("pool", "wr%d" % s, dst, src_ap, writes=[wkey(s)])
        return s, dst

    def sgu_vsrc(j, vc):
        return s_win[j, :, 2048 + vc * 512:2048 + (vc + 1) * 512].rearrange("(c p) n -> p c n", p=128)

    def sgu_prefetch(j, st8):
        B2 = p.sb("B2_%d" % j, [128, 16, 128], F32, off=PB + 49152)
        WcT = p.sb("WcT%d" % j, [128, 8, 128], BF16, off=PB + 57344)
        L2 = p.sb("L2_%d" % j, [2, 2048], F32, off=PB + 16384)
        R2 = p.sb("R2_%d" % j, [2, 8, 128], F32, off=PB + 16384 + 8192)
        wtmp = p.sb("wtmp%d" % j, [128, 1024], F32, off=PB + 16384 + 12288)
        p.dma("sp", "sgc1", L2[:, :], s_l2[j, :, :], writes=["L2"])
        p.dma("sp", "sgc2", wtmp[:, :], s_wsT[j, :, :], writes=["wtmp"])
        p.dma("sp", "sgc3", R2[1:2, :, :], s_b[j:j + 1, :, :], writes=["R2b"])
        for g in range(8):
            TT(WcT[:, g, :], wtmp[:, g * 128:(g + 1) * 128], maskT[:, :], ALU.mult, ["wtmp", "c"], [("WcT", g)])
        for half in range(2):
            bk = mmbank()
            MM([(ps[bk][0:1, :], ones[:, 0:1], WcT[:, half * 4:half * 4 + 4, :].rearrange("p a b -> p (a b)"), True, True)],
               [("WcT", g) for g in range(8)] + ["ones"], [("ps", bk)])
            DCOPY(R2[0:1, half * 4:half * 4 + 4, :].rearrange("p a b -> p (a b)"), ps[bk][0:1, :], [], [("ps", bk), ("R2a", half)])
        for bq in range(4):
            bk = mmbank()
            MM([(ps[bk][:, k * 128:(k + 1) * 128], L2[0:2, (4 * bq + k) * 128:(4 * bq + k + 1) * 128],
                 R2[0:2, (4 * bq + k) // 2, :], True, True) for k in range(4)],
               ["L2", "R2b", ("R2a", 0), ("R2a", 1)], [("ps", bk)])
            DCOPY(B2[:, 4 * bq:4 * bq + 4, :], ps[bk][:, :].rearrange("p (k t) -> p k t", k=4), [], [("ps", bk), "B2"])
        st8["B2"], st8["WcT"] = B2, WcT
        st8["wc"] = {"n": 0}
        st8["pre"] = [sgu_wload(j, st8["wc"], sgu_vsrc(j, vc), 8) for vc in range(2)]

    def sgu_layer(j, l, st8, next_prefetch=None, pi=0):
        B2, WcT = st8["B2"], st8["WcT"]
        uT = p.sb("uT%d" % j, [128, 16, 512], BF16, off=PB)
        vraw = p.sb("vraw%d" % j, [128, 4, 2048], F32, off=PB + 16384)
        vlnv = p.sb("vlnv%d" % j, [128, 4, 4096], BF16, off=PB + 16384)
        st = p.sb("st%d" % j, [128, 64], F32, off=PB + 59392)
        tmpx = [p.sb("tmpx%d_%d" % (j, b), [128, 512], F32, off=AR + 32768 + 4096 + 2048 * b) for b in range(2)]
        GC = 80 + 16 * j

        def norm(tt):
            phase_norm(pi, tt)
        norm(0)
        wc = st8["wc"]
        pre = list(st8["pre"])

        def wload(src_ap, shape3):
            return sgu_wload(j, wc, src_ap, shape3[0])

        allv = [("vraw", tch, vc) for tch in range(4) for vc in range(4)]

        def v_path(tg):
            T0 = tg * 512
            for vc in range(4):
                if pre:
                    s, w3 = pre.pop(0)
                else:
                    s, w3 = wload(sgu_vsrc(j, vc), (8, 512))
                for tch in range(4):
                    bk = mmbank()
                    MM([(ps[bk][:, :], hT[:, c, T0 + tch * 128:T0 + (tch + 1) * 128], w3[:, c, :], c == 0, c == 7)
                        for c in range(8)], [wkey(s), ("h", tg)], [("ps", bk)])
                    ACT(vraw[:, tch, vc * 512:(vc + 1) * 512], ps[bk][:, :], AF.Gelu, [],
                        [("ps", bk), ("vraw", tch, vc), ("stp", tch, vc)],
                        accum_out=st[:, tch * 4 + vc:tch * 4 + vc + 1])

        def ln_stats(tg):
            allp = [("stp", tch, vc) for tch in range(4) for vc in range(4)]
            p.op("dve", lambda e: e.reduce_sum(st[:, 16:20], st[:, 0:16].rearrange("p (a b) -> p a b", a=4), axis=AX.X),
                 allp, ["ssum"])
            for tch in range(4):
                ACT(sq[:, 0:4, :].rearrange("p a b -> p (a b)"), vraw[:, tch, :], AF.Square,
                    [("vraw", tch, vc) for vc in range(4)], ["sq", ("ssq", tch)], accum_out=st[:, 20 + tch:21 + tch])
            TS(st[:, 24:28], st[:, 16:20], 1.0 / 2048, None, ALU.mult, None, ["ssum"], ["mean"])
            TT(st[:, 28:32], st[:, 24:28], st[:, 24:28], ALU.mult, ["mean"], ["msq"])
            STT(st[:, 32:36], st[:, 20:24], 1.0 / 2048, st[:, 28:32], ALU.mult, ALU.subtract,
                ["msq"] + [("ssq", t) for t in range(4)], ["var"])
            ACT(st[:, 36:40], st[:, 32:36], AF.Ln, ["var", "eps"], ["sd"], bias=epsl[:, 0:1], scale=1.0)
            ACT(st[:, 40:44], st[:, 36:40], AF.Exp, ["sd"], ["rstd"], scale=-0.5)
            STT(st[:, 44:48], st[:, 24:28], -1.0, st[:, 40:44], ALU.mult, ALU.mult, ["mean", "rstd"], ["nmr"])
            for tch in range(4):
                vk = [("vraw", tch, vc) for vc in range(4)]
                ACT(vlnv[:, tch, 0:2048], vraw[:, tch, :], AF.Identity, ["rstd", "nmr"], vk,
                    bias=st[:, 44 + tch:45 + tch], scale=st[:, 40 + tch:41 + tch])

        tcount = {"n": 0}

        def spatial_unit(dch):
            g = dch // 2
            bk = mmbank()
            MM([(ps[bk][:, tch * 128:(tch + 1) * 128], vlnv[:, tch, dch * 128:(dch + 1) * 128], WcT[:, g, :], True, True)
                for tch in range(4)], allv + [("WcT", g)], [("ps", bk)])
            tb = tcount["n"] % 2
            tcount["n"] += 1
            t3 = tmpx[tb][:, :].rearrange("p (k t) -> p k t", k=4)
            STT(t3, ps[bk][:, :].rearrange("p (k t) -> p k t", k=4), gains[:, GC + dch:GC + dch + 1],
                B2[:, dch, :].unsqueeze(1).broadcast_to([128, 4, 128]), ALU.mult, ALU.add,
                ["B2", "c"], [("ps", bk), ("tmpx", tb)])
            TT(uT[:, dch, :], tmpx[tb][:, :], uT[:, dch, :], ALU.mult, [("tmpx", tb)], [("u", dch)])

        def u_spatial(tg):
            for idx in range(16):
                uc, sub = idx // 4, idx % 4
                if sub == 0:
                    s, w3 = wload(s_win[j, :, uc * 512:(uc + 1) * 512].rearrange("(c p) n -> p c n", p=128), (8, 512))
                    cur = (s, w3)
                s, w3 = cur
                bk = mmbank()
                MM([(ps[bk][:, :], w3[:, c, sub * 128:(sub + 1) * 128], hT[:, c, tok(tg)], c == 0, c == 7)
                    for c in range(8)], [wkey(s), ("h", tg)], [("ps", bk)])
                ACT(uT[:, idx, :], ps[bk][:, :], AF.Gelu, [], [("ps", bk), ("u", idx)])
                if idx >= 1:
                    spatial_unit(idx - 1)
            spatial_unit(15)

        def w_out(tg):
            for oc in range(4):
                s, w3 = wload(s_wout[j, :, oc * 256:(oc + 1) * 256].rearrange("(k p) n -> p k n", p=128), (16, 256))
                for dcl in range(2):
                    dc = oc * 2 + dcl
                    bk = mmbank()
                    MM([(ps[bk][:, :], w3[:, kc, dcl * 128:(dcl + 1) * 128], uT[:, kc, :], kc == 0, kc == 15)
                        for kc in range(16)], [wkey(s)] + [("u", kc) for kc in range(16)], [("ps", bk)])
                    TT(xT[:, dc, tok(tg)], ps[bk][:, :], xT[:, dc, tok(tg)], ALU.add, [], [("ps", bk), xk(dc, tg)])

        v_path(0)
        norm(1)
        ln_stats(0)
        u_spatial(0)
        for tg in range(1, NTT):
            v_path(tg)
            if tg + 1 < NTT:
                norm(tg + 1)
            ln_stats(tg)
            w_out(tg - 1)
            phase_norm(pi + 1, tg - 1)
            u_spatial(tg)
        w_out(NTT - 1)
        phase_norm(pi + 1, NTT - 1)
        if next_prefetch is not None:
            next_prefetch()
        p.fence()

    def mla_prefetch(j, st8):
        wdkv = p.sb("wdkv%d" % j, [128, 8, 512], BF16, off=WB)
        wuq = p.sb("wuq%d" % j, [128, 2, 2048], BF16, off=WB + 8192)
        p.dma("pool", "wr0", wdkv[:, :, :], w_dkv[j, :, :].rearrange("(c p) n -> p c n", p=128), writes=[wkey(0)])
        p.dma("pool", "wr1", wuq[:, :, :], w_uq[j, :, :].rearrange("(c p) n -> p c n", p=128), writes=[wkey(1)])
        st8["wdkv"], st8["wuq"] = wdkv, wuq

    def mla_layer(j, l, st8, next_prefetch=None, pi=0):
        wdkv, wuq = st8["wdkv"], st8["wuq"]
        wukv = p.sb("wukv%d" % j, [128, 2048], BF16, off=WB + 16384)
        wo_sl = p.sb("wo%d" % j, [128, 2, 1024], BF16, off=WB + 20480)
        OT = p.sb("OT%d" % j, [128, 2, NLOC], BF16, off=WB + 24576)
        KT = p.sb("KT%d" % j, [128, 2, SEQ], BF16, off=AR)
        V = p.sb("V%d" % j, [128, 32, 256], BF16, off=AR + 16384)
        o = PB
        cqn = p.sb("cqn%d" % j, [128, 2, NLOC], BF16, off=o); o += 8192
        QN = p.sb("QN%d" % j, [128, 2, NLOC], BF16, off=o)
        ckvl = p.sb("ckvl%d" % j, [128, NLOC], BF16, off=o)
        krl = p.sb("krl%d" % j, [64, NLOC], BF16, off=o + 4096); o += 8192
        ckva = p.sb("ckva%d" % j, [128, SEQ], BF16, off=o); o += 8192
        kra = p.sb("kra%d" % j, [128, SEQ], BF16, off=o); o += 8192
        QR = p.sb("QR%d" % j, [128, 2, NLOC], BF16, off=o); o += 8192
        PT_OFF = o
        PT = [p.sb("PT%d_%d" % (j, b), [128, 512], BF16, off=o + 1024 * b) for b in range(4)]; o += 4096
        cqf = p.sb("cqf%d" % j, [128, 2, 512], F32, off=o); o += 4096
        ckf = p.sb("ckf%d" % j, [128, 1, 512], F32, off=o); o += 2048
        t1 = p.sb("t1_%d" % j, [64, 512], F32, off=o); o += 2048
        t2 = p.sb("t2_%d" % j, [64, 512], F32, off=o); o += 2048
        acc = [p.sb("acc%d_0" % j, [128, 512], F32, off=o - 4096), p.sb("acc%d_1" % j, [128, 512], F32, off=o - 2048)]
        REC_OFF = o
        rec = [p.sb("rec%d_%d" % (j, b), [128, 512], F32, off=o + 2048 * b) for b in range(2)]; o += 4096
        assert o <= SB_END, o

        p.dma("pool", "mw2", wukv[:, :], w_ukv[j, :, :], writes=["wukv"] + [("xin", b) for b in range(4)])
        p.op("dve", lambda e: e.memset(kra[64:128, :], 0.0), writes=["kraz"])
        p.op("dve", lambda e: e.memset(QR[64:128, :, :], 0.0), writes=["QRz"])

        def norm(tt):
            phase_norm(pi, tt)
        norm(0)
        cqf2 = [cqf, p.sb("cqfb%d" % j, [128, 2, 512], F32, off=PT_OFF)]
        ckf2 = [ckf, p.sb("ckfb%d" % j, [128, 1, 512], F32, off=REC_OFF)]

        def lat_a(tt):
            bsel = tt % 2
            for fch in range(3):
                bk = mmbank()
                MM([(ps[bk][:, :], wdkv[:, c, fch * 128:(fch + 1) * 128], hT[:, c, tok(tt)], c == 0, c == 7)
                    for c in range(8)], [wkey(0), ("h", tt)], [("ps", bk)])
                if fch < 2:
                    EVAC(cqf2[bsel][:, fch, :], ps[bk][:, :], [], [("ps", bk), ("cqf", bsel, fch)])
                else:
                    EVAC(ckf2[bsel][:, 0, :], ps[bk][:, :], [], [("ps", bk), ("ckf", bsel)])
            ba = mmbank()
            MM([(ps[ba][0:64, :], wdkv[:, c, 384:448], hT[:, c, tok(tt)], c == 0, c == 7) for c in range(8)],
               [wkey(0), ("h", tt)], [("ps", ba)])
            bb = mmbank()
            MM([(ps[bb][0:64, :], wdkv[:, c, 448:512], hT[:, c, tok(tt)], c == 0, c == 7) for c in range(8)],
               [wkey(0), ("h", tt)], [("ps", bb)])
            TT(t1[:, :], ps[ba][0:64, :], cos2[:, tok(tt)], ALU.mult, ["cos2"], [("ps", ba), "t1"])
            TT(t2[:, :], ps[bb][0:64, :], sin2s[:, tok(tt)], ALU.mult, ["sin2s"], [("ps", bb), "t2"])
            TT(krl[:, tok(tt)], t1[:, :], t2[:, :], ALU.add, ["t1", "t2"], [("krl", tt)])

        def lat_b(tt):
            bsel = tt % 2
            rmsnorm(cqf2[bsel][:, :, :], 2, 256, 72 + 2 * j, [cqn[:, c, tok(tt)] for c in range(2)], epsn,
                    [("cqf", bsel, 0), ("cqf", bsel, 1)], [("cqn", tt)])
            rmsnorm(ckf2[bsel][:, :, :], 1, 128, 76 + j, [ckvl[:, tok(tt)]], epsn, [("ckf", bsel)], [("ckvl", tt)])

        for tt in range(NTT):
            if tt + 1 < NTT:
                norm(tt + 1)
            lat_a(tt)
            if tt >= 1:
                lat_b(tt - 1)
        lat_b(NTT - 1)
        lk = [("ckvl", tt) for tt in range(NTT)]
        rk = [("krl", tt) for tt in range(NTT)]
        p.dma("sp", "cci%d" % j, cc_in[j].ap()[0:128, :], ckvl[:, :], reads=lk, writes=["ccin_a"])
        p.dma("sp", "cci%d" % j, cc_in[j].ap()[128:192, :], krl[:, :], reads=rk, writes=["ccin_b"])

        def ccfn(e, j=j):
            return e.collective_compute("AllGather", ALU.bypass, replica_groups=PAIRS,
                                        ins=[cc_in[j].ap().opt()], outs=[cc_out[j].ap().opt()])
        p.custom("pool", "cc%d" % j, 1, ccfn, reads=["ccin_a", "ccin_b"], writes=["ccout"])
        for r in range(2):
            p.dma("sp", "ccl%d" % j, ckva[:, :].rearrange("p (i r t) -> p i r t", r=2, t=128)[:, :, r, :],
                  cc_out[j].ap()[r * 192:r * 192 + 128, :].rearrange("p (i t) -> p i t", t=128),
                  reads=["ccout"], writes=[("ckva", r)])
            p.dma("sp", "ccl%d" % j, kra[0:64, :].rearrange("p (i r t) -> p i r t", r=2, t=128)[:, :, r, :],
                  cc_out[j].ap()[r * 192 + 128:r * 192 + 192, :].rearrange("p (i t) -> p i t", t=128),
                  reads=["ccout"], writes=[("kra", r)])
        p.fence()
        for hp in range(4):
            for hl in range(2):
                h = 2 * hp + hl
                for kt in range(8):
                    bk = mmbank((0, 1, 2, 7))
                    MM([(ps[bk][:, :], wukv[:, h * 128:(h + 1) * 128], ckva[:, kt * 512:(kt + 1) * 512], True, True)],
                       ["wukv", ("ckva", 0), ("ckva", 1)], [("ps", bk)])
                    EVAC(KT[:, hl, kt * 512:(kt + 1) * 512], ps[bk][:, :], [], [("ps", bk), ("KT", hl)])
            for kp in range(16):
                bk = mmbank((0, 1, 2, 7))
                MM([(ps[bk][:, q * 256:(q + 1) * 256], ckva[:, (2 * kp + q) * 128:(2 * kp + q + 1) * 128],
                     wukv[:, 1024 + hp * 256:1024 + (hp + 1) * 256], True, True) for q in range(2)],
                   ["wukv", ("ckva", 0), ("ckva", 1)], [("ps", bk)])
                EVAC(V[:, 2 * kp:2 * kp + 2, :], ps[bk][:, :].rearrange("p (q d) -> p q d", q=2), [], [("ps", bk), "V"])
            p.dma("pool", "mw3", wo_sl[:, :, :],
                  w_o[j, hp * 256:(hp + 1) * 256, :].rearrange("(h p) n -> p h n", p=128), writes=["wo"])
            for tt in range(NTT):
                for hl in range(2):
                    h = 2 * hp + hl
                    bk = mmbank((0, 1, 2, 7))
                    MM([(ps[bk][:, :], wuq[:, kc, h * 128:(h + 1) * 128], cqn[:, kc, tok(tt)], kc == 0, kc == 1)
                        for kc in range(2)], [wkey(1), ("cqn", tt)], [("ps", bk)])
                    ACT(QN[:, hl, tok(tt)], ps[bk][:, :], AF.Copy, [], [("ps", bk), ("QN", hl)], scale=SCALE)
                    ba = mmbank((0, 1, 2, 7))
                    MM([(ps[ba][0:64, :], wuq[:, kc, 1024 + h * 64:1024 + (h + 1) * 64], cqn[:, kc, tok(tt)], kc == 0, kc == 1)
                        for kc in range(2)], [wkey(1), ("cqn", tt)], [("ps", ba)])
                    bb = mmbank((0, 1, 2, 7))
                    MM([(ps[bb][0:64, :], wuq[:, kc, 1536 + h * 64:1536 + (h + 1) * 64], cqn[:, kc, tok(tt)], kc == 0, kc == 1)
                        for kc in range(2)], [wkey(1), ("cqn", tt)], [("ps", bb)])
                    STT(t1[:, :], ps[ba][0:64, :], SCALE, cos2[:, tok(tt)], ALU.mult, ALU.mult, ["cos2"], [("ps", ba), "t1"])
                    STT(t2[:, :], ps[bb][0:64, :], SCALE, sin2s[:, tok(tt)], ALU.mult, ALU.mult, ["sin2s"], [("ps", bb), "t2"])
                    TT(QR[0:64, hl, tok(tt)], t1[:, :], t2[:, :], ALU.add, ["t1", "t2"], [("QR", hl)])
            if hp == 3 and next_prefetch is not None:
                next_prefetch()
            units = []
            for qg in range(4):
                i0 = 4 * qg
                nkt = 8 * qg + 8
                for kt in range(nkt):
                    imin = max(i0, kt // 2)
                    q0 = imin * 128
                    n = (i0 + 4) * 128 - q0
                    off = q0 - i0 * 128
                    midx = (kt % 2) if kt >= 2 * i0 else None
                    for hl in range(2):
                        units.append((qg, kt, hl, q0, n, off, midx, kt == 0, kt == nkt - 1))
            LOOK = 2

            def emit_pv(idx):
                (qg, kt, hl, q0, n, off, midx, first, last) = units[idx]
                pb = idx % 4
                ob, sbn = 3 + hl, 5 + hl
                if SUM_MODE == "mm":
                    MM([(ps[ob][:, off:off + n], V[:, kt, hl * 128:(hl + 1) * 128], PT[pb][:, off:off + n], first, last),
                        (ps[sbn][:, off:off + n], ones[:, :], PT[pb][:, off:off + n], first, last)],
                       ["V", ("pt", pb), "ones"], [("ps", ob), ("ps", sbn)])
                else:
                    MM([(ps[ob][:, off:off + n], V[:, kt, hl * 128:(hl + 1) * 128], PT[pb][:, off:off + n], first, last)],
                       ["V", ("pt", pb)], [("ps", ob)])
                    eng = "pool" if (hl == 0 and SUM_MODE == "pooldve") else "dve"
                    akey = "t1" if hl == 0 else "t2"
                    if first:
                        p.op(eng, lambda e, a=acc[hl], pt=PT[pb]: e.tensor_copy(a[:, :], pt[:, :]), [("pt", pb)], [akey])
                    else:
                        p.op(eng, lambda e, a=acc[hl][:, off:off + n], pt=PT[pb][:, off:off + n]: e.tensor_tensor(a, a, pt, ALU.add),
                             [("pt", pb)], [akey])
                if last:
                    if SUM_MODE != "mm":
                        MM([(ps[sbn][:, :], ones_f[:, :], acc[hl][:, :], True, True)], [akey, "ones"], [("ps", sbn)])
                    ACT(rec[hl][:, :], ps[sbn][:, :], AF.Ln, [], [("ps", sbn), ("rec", hl)])
                    ACT(rec[hl][:, :], rec[hl][:, :], AF.Exp, [], [("rec", hl)], scale=-1.0)
                    TT(OT[:, hl, tok(qg)], ps[ob][:, :], rec[hl][:, :], ALU.mult, [("rec", hl)], [("ps", ob), ("OT", qg)])

            for idx, (qg, kt, hl, q0, n, off, midx, first, last) in enumerate(units):
                sbk = (0, 1, 2)[idx % 3]
                pb = idx % 4
                ks = slice(kt * 128, (kt + 1) * 128)
                oo = ps[sbk][:, off:off + n]
                MM([(oo, KT[:, hl, ks], QN[:, hl, q0:q0 + n], True, False),
                    (oo, kra[:, ks], QR[:, hl, q0:q0 + n], False, True)],
                   [("KT", hl), ("QN", hl), ("QR", hl), ("kra", 0), ("kra", 1), "kraz", "QRz"], [("ps", sbk)])
                ACT(PT[pb][:, off:off + n], oo, AF.Exp, [], [("ps", sbk), ("pt", pb)])
                if midx is not None:
                    TT(PT[pb][:, off:off + 128], PT[pb][:, off:off + 128], masks[:, midx, 0:128], ALU.mult,
                       ["c"], [("pt", pb)])
                if idx >= LOOK:
                    emit_pv(idx - LOOK)
            for idx in range(len(units) - LOOK, len(units)):
                emit_pv(idx)
            for tt in range(NTT):
                for dc in range(8):
                    bk = mmbank((0, 1, 2, 7))
                    MM([(ps[bk][:, :], wo_sl[:, hl, dc * 128:(dc + 1) * 128], OT[:, hl, tok(tt)], hl == 0, hl == 1)
                        for hl in range(2)], ["wo", ("OT", tt)], [("ps", bk)])
                    TT(xT[:, dc, tok(tt)], ps[bk][:, :], xT[:, dc, tok(tt)], ALU.add, [], [("ps", bk), xk(dc, tt)])
                if hp == 3:
                    phase_norm(pi + 1, tt, extra_w=[("KT", 0), ("KT", 1), "V"])
        p.fence()

    sts = [dict() for _ in phases]

    def make_prefetch(i):
        if i >= len(phases):
            return None
        kind, l = phases[i]
        if kind == "ffn":
            return lambda: ffn_prefetch(l)
        if kind == "sgu":
            return lambda: sgu_prefetch(l // 2, sts[i])
        return lambda: mla_prefetch(l // 2, sts[i])

    if phases:
        make_prefetch(0)()
    load_x()
    if not phases or phases[0][0] != "mla":
        p.fence()
    for i, (kind, l) in enumerate(phases):
        nxt = make_prefetch(i + 1)
        if kind == "ffn":
            ffn_layer(l, nxt, i)
        elif kind == "sgu":
            sgu_layer(l // 2, l, sts[i], nxt, i)
        else:
            mla_layer(l // 2, l, sts[i], nxt, i)

    for tt in range(NTT):
        phase_norm(len(phases), tt)
    p.final_wait("sp", okeys)
    p.emit()
    return nc, p


_CACHE = {}


def _prep_inputs(x, positions, norm_mix, norm_ffn, final_norm,
                 mla_w_dkv, mla_q_norm, mla_kv_norm, mla_w_uq, mla_w_ukv, mla_w_o,
                 sgu_w_in, sgu_ln_g, sgu_ln_b, sgu_w_spatial, sgu_b_spatial, sgu_w_out,
                 ffn_w_up, ffn_w_down):
    f32 = np.float32
    A = lambda a: np.ascontiguousarray(np.asarray(a))
    x = A(x); positions = A(positions)
    gains = np.zeros((128, 112), f32)
    gains[:, 0:32] = A(norm_mix).reshape(4, 8, 128).transpose(2, 0, 1).reshape(128, 32)
    gains[:, 32:64] = A(norm_ffn).reshape(4, 8, 128).transpose(2, 0, 1).reshape(128, 32)
    gains[:, 64:72] = A(final_norm).reshape(8, 128).T
    gains[:, 72:76] = A(mla_q_norm).reshape(2, 2, 128).transpose(2, 0, 1).reshape(128, 4)
    gains[:, 76:78] = A(mla_kv_norm).reshape(2, 128).T
    gains[:, 80:112] = A(sgu_ln_g).reshape(2, 16, 128).transpose(2, 0, 1).reshape(128, 32)
    invf = (10000.0 ** (-np.arange(0, 64, 2, dtype=f32) / 64.0)).astype(f32)
    ropec = np.zeros((64, 2), f32)
    ropec[:, 0] = np.concatenate([invf, invf])
    ropec[:, 1] = np.concatenate([-np.ones(32, f32), np.ones(32, f32)])
    ident = np.eye(128, dtype=f32)
    maskT = (np.arange(128)[:, None] <= np.arange(128)[None, :]).astype(ml_dtypes.bfloat16)
    wd = A(mla_w_dkv)
    kr = wd[:, :, 384:448]
    w_dkv = A(np.concatenate([wd, kr[:, :, 32:64], kr[:, :, 0:32]], axis=2))
    wq = A(mla_w_uq).reshape(2, 256, 8, 192)
    qn = wq[:, :, :, 0:128].reshape(2, 256, 1024)
    qr = wq[:, :, :, 128:192]
    qsw = np.concatenate([qr[..., 32:64], qr[..., 0:32]], axis=-1)
    w_uq = A(np.concatenate([qn, qr.reshape(2, 256, 512), qsw.reshape(2, 256, 512)], axis=2))
    wkv = A(mla_w_ukv).reshape(2, 128, 8, 256)
    w_ukv = A(np.concatenate([wkv[:, :, :, 0:128].reshape(2, 128, 1024), wkv[:, :, :, 128:256].reshape(2, 128, 1024)], axis=2))
    s_l2 = A(np.stack([A(sgu_ln_b), np.ones((2, 2048), f32)], axis=1))
    s_b = A(sgu_b_spatial)
    s_wsT = A(A(sgu_w_spatial).transpose(0, 3, 1, 2).reshape(2, 128, 1024))
    shared = dict(maskT=maskT, gains=gains, ropec=ropec, ident=ident, w_dkv=w_dkv, w_uq=w_uq, w_ukv=w_ukv,
                  w_o=A(mla_w_o), s_win=A(sgu_w_in), s_wout=A(sgu_w_out), s_l2=s_l2, s_b=s_b,
                  s_wsT=s_wsT, f_up=A(ffn_w_up), f_dn=A(ffn_w_down))
    tri = (np.arange(128)[:, None] <= np.arange(128)[None, :])
    in_maps = []
    for c in range(8):
        b, r = c // 2, c % 2
        xs = A(x[b].reshape(16, 2, 128, D_MODEL)[:, r].reshape(NLOC, D_MODEL))
        ps_ = positions[b].reshape(16, 2, 128)[:, r].reshape(1, NLOC)
        pos = A(np.broadcast_to(ps_, (64, NLOC))).astype(np.int32)
        m = np.zeros((128, 2, 256), ml_dtypes.bfloat16)
        if r == 0:
            m[:, 0, 0:128] = tri; m[:, 0, 128:256] = tri
        else:
            m[:, 0, :] = 1
            m[:, 1, 0:128] = tri; m[:, 1, 128:256] = tri
        d = dict(shared)
        d.update(xs=xs, pos=pos, masks=m)
        in_maps.append(d)
    return in_maps


def kernel(**inputs):
    if "nc" not in _CACHE:
        _CACHE["nc"] = build_program(4)[0]
    nc = _CACHE["nc"]
    in_maps = _prep_inputs(**inputs)
    res = run_bass_kernel_spmd(nc, in_maps, core_ids=list(range(8)))
    out = np.zeros((4, SEQ, D_MODEL), np.float32)
    ov = out.reshape(4, 16, 2, 128, D_MODEL)
    for c in range(8):
        b, r = c // 2, c % 2
        ov[b, :, r] = np.asarray(res.results[c]["y"]).reshape(16, 128, D_MODEL)
    return out
```
